# Optimizing a Trainium2 kernel written in Bass

```python
import math, functools
import jax, jax.numpy as jnp
from jax import lax
import numpy as np

D_MODEL = 1024
BATCH = 8
SEQ = 2048
DEPTH = 4
DEC_BATCH = 128
DEC_SEQ = 8
PAST_LEN = 16384
PAGE_SIZE = 128

N_MIXERS = 3
N_GLA = (DEPTH + 2) // 3
N_GDN = (DEPTH + 1) // 3
N_S5 = DEPTH // 3
ALPHA_RES = (2 * DEPTH) ** 0.25
BETA_INIT = (8 * DEPTH) ** -0.25
LN_EPS = 1e-5
NORM_EPS = 1e-6
CHUNK = 64

GLA_HEADS = 4
GLA_KW = D_MODEL // 2
GLA_VW = D_MODEL
GLA_DK = GLA_KW // GLA_HEADS
GLA_DV = GLA_VW // GLA_HEADS
GLA_LOWRANK = 16
GLA_TAU = 16.0
GLA_IN = 2 * GLA_KW + 2 * GLA_VW + GLA_LOWRANK

GDN_HEADS = 8
GDN_DK = 128
GDN_DV = 128
GDN_KW = GDN_HEADS * GDN_DK
GDN_VW = GDN_HEADS * GDN_DV
GDN_CONV = 4
GDN_CONV_CH = 2 * GDN_KW + GDN_VW
GDN_IN = GDN_CONV_CH + GDN_VW + 2 * GDN_HEADS

S5_WIDTH = D_MODEL
S5_GROUP = 16
S5_GROUPS = S5_WIDTH // S5_GROUP
S5_STATE = 64
S5_IN = 2 * S5_WIDTH

F32 = jnp.float32

kernel_name = 'hybrid_gla_gdn_s5_deepnorm_step'


def layer_norm(x, g, b):
    xf = x.astype(F32)
    mu = jnp.mean(xf, -1, keepdims=True)
    var = jnp.mean(jnp.square(xf - mu), -1, keepdims=True)
    return ((xf - mu) * lax.rsqrt(var + LN_EPS) * g.astype(F32) + b.astype(F32)).astype(x.dtype)


def rms_head(o, g):
    return o * lax.rsqrt(jnp.mean(jnp.square(o), -1, keepdims=True) + NORM_EPS) * g.astype(F32)


def l2_normalize(t):
    return t * lax.rsqrt(jnp.sum(jnp.square(t), -1, keepdims=True) + NORM_EPS)


def to_chunks(t, c):
    bt, L = t.shape[:2]
    t = t.reshape((bt, L // c, c) + t.shape[2:])
    return t.transpose((1, 0, 3, 2) + tuple(range(4, t.ndim)))


def from_chunks(t):
    n, bt, h, c, d = t.shape
    return t.transpose(1, 0, 3, 2, 4).reshape(bt, n * c, h, d)


def gla_chunked(q, k, v, log_a, s0):
    L = q.shape[1]
    c = math.gcd(L, CHUNK)
    q, k, v, log_a = (to_chunks(t, c) for t in (q, k, v, log_a))
    b = jnp.cumsum(log_a, axis=-2)
    q_dec = q * jnp.exp(b)
    k_inv = k * jnp.exp(-b)
    k_end = k * jnp.exp(b[..., -1:, :] - b)
    incl = jnp.tril(jnp.ones((c, c), bool))
    att = jnp.where(incl, jnp.einsum('nbhid,nbhjd->nbhij', q_dec, k_inv), 0.0)
    o_intra = jnp.einsum('nbhij,nbhjv->nbhiv', att, v)

    def step(s, inp):
        qd, ke, vv, bl = inp
        o_inter = jnp.einsum('bhid,bhdv->bhiv', qd, s)
        s = s * jnp.exp(bl)[..., None] + jnp.einsum('bhjd,bhjv->bhdv', ke, vv)
        return s, o_inter

    s, o_inter = lax.scan(step, s0, (q_dec, k_end, v, b[..., -1, :]))
    return from_chunks(o_intra + o_inter), s


def gdn_chunked(q, k, v, g, beta, s0):
    L = q.shape[1]
    c = math.gcd(L, CHUNK)
    q, k, v = (to_chunks(t, c) for t in (q, k, v))
    g, beta = to_chunks(g, c), to_chunks(beta, c)
    gc = jnp.cumsum(g, axis=-1)
    incl = jnp.tril(jnp.ones((c, c), bool))
    strict = jnp.tril(jnp.ones((c, c), bool), -1)
    decay = jnp.exp(jnp.where(incl, gc[..., :, None] - gc[..., None, :], -jnp.inf))
    kb = k * beta[..., None]
    a_mat = jnp.where(strict, jnp.einsum('nbhid,nbhjd->nbhij', kb, k) * decay, 0.0)
    lhs = a_mat + jnp.eye(c, dtype=a_mat.dtype)
    solve = functools.partial(lax.linalg.triangular_solve, left_side=True, lower=True, unit_diagonal=True)
    u = solve(lhs, v * beta[..., None])
    w = solve(lhs, kb * jnp.exp(gc)[..., None])
    att = jnp.einsum('nbhid,nbhjd->nbhij', q, k) * decay
    q_dec = q * jnp.exp(gc)[..., None]
    k_end = k * jnp.exp(gc[..., -1:] - gc)[..., None]
    g_end = jnp.exp(gc[..., -1])

    def step(s, inp):
        qd, ke, uu, ww, aa, ge = inp
        v_new = uu - jnp.einsum('bhcd,bhdv->bhcv', ww, s)
        o = jnp.einsum('bhcd,bhdv->bhcv', qd, s) + jnp.einsum('bhij,bhjv->bhiv', aa, v_new)
        s = s * ge[..., None, None] + jnp.einsum('bhcd,bhcv->bhdv', ke, v_new)
        return s, o

    s, o = lax.scan(step, s0, (q_dec, k_end, u, w, att, g_end))
    return from_chunks(o), s


def complex_affine_combine(e1, e2):
    a1r, a1i, b1r, b1i = e1
    a2r, a2i, b2r, b2i = e2
    return (a2r * a1r - a2i * a1i, a2r * a1i + a2i * a1r,
            a2r * b1r - a2i * b1i + b2r, a2r * b1i + a2i * b1r + b2i)


def gla_mixer(x, w_in, w_a2, b_a, norm_g, w_out, s0):
    bt, L, _ = x.shape
    h = jnp.einsum('bld,de->ble', x, w_in).astype(F32)
    q, k, v, r, lr = jnp.split(h, [GLA_KW, 2 * GLA_KW, 2 * GLA_KW + GLA_VW, 2 * GLA_KW + 2 * GLA_VW], axis=-1)
    log_a = jax.nn.log_sigmoid(jnp.einsum('blr,rk->blk', lr, w_a2.astype(F32)) + b_a.astype(F32)) / GLA_TAU
    heads = lambda t, d: t.reshape(bt, L, GLA_HEADS, d)
    o, s = gla_chunked(heads(q, GLA_DK) * GLA_DK ** -0.5, heads(k, GLA_DK), heads(v, GLA_DV),
                       heads(log_a, GLA_DK), s0.astype(F32))
    o = rms_head(o, norm_g) * jax.nn.silu(heads(r, GLA_DV))
    out = jnp.einsum('ble,ed->bld', o.reshape(bt, L, GLA_VW), w_out.astype(F32))
    return out.astype(x.dtype), s


def gdn_mixer(x, w_in, w_conv, a_log, dt_bias, norm_g, w_out, conv0, s0):
    bt, L, _ = x.shape
    h = jnp.einsum('bld,de->ble', x, w_in).astype(F32)
    qkv, z, a, b = jnp.split(h, [GDN_CONV_CH, GDN_CONV_CH + GDN_VW, GDN_CONV_CH + GDN_VW + GDN_HEADS], axis=-1)
    ext = jnp.concatenate([conv0.astype(F32), qkv], axis=1)
    wc = w_conv.astype(F32)
    conv = ext[:, 0:L] * wc[0]
    for t in range(1, GDN_CONV):
        conv = conv + ext[:, t:t + L] * wc[t]
    conv_new = ext[:, L:]
    qkv = jax.nn.silu(conv)
    q, k, v = jnp.split(qkv, [GDN_KW, 2 * GDN_KW], axis=-1)
    q = l2_normalize(q.reshape(bt, L, GDN_HEADS, GDN_DK)) * GDN_DK ** -0.5
    k = l2_normalize(k.reshape(bt, L, GDN_HEADS, GDN_DK))
    v = v.reshape(bt, L, GDN_HEADS, GDN_DV)
    g = -jnp.exp(a_log.astype(F32)) * jax.nn.softplus(a + dt_bias.astype(F32))
    beta = jax.nn.sigmoid(b)
    o, s = gdn_chunked(q, k, v, g, beta, s0.astype(F32))
    o = rms_head(o, norm_g) * jax.nn.silu(z.reshape(bt, L, GDN_HEADS, GDN_DV))
    out = jnp.einsum('ble,ed->bld', o.reshape(bt, L, GDN_VW), w_out.astype(F32))
    return out.astype(x.dtype), conv_new, s


def s5_mixer(x, w_in, lam_re, lam_im, log_dt, b_re, b_im, c_re, c_im, d, w_glu, b_glu, w_out, h0_re, h0_im):
    bt, L, _ = x.shape
    h = jnp.einsum('bld,de->ble', x, w_in).astype(F32)
    u, z = jnp.split(h, [S5_WIDTH], axis=-1)
    ug = u.reshape(bt, L, S5_GROUPS, S5_GROUP)
    lam_re, lam_im = lam_re.astype(F32), lam_im.astype(F32)
    dt = jnp.exp(log_dt.astype(F32))[:, None]
    mag = jnp.exp(lam_re * dt)
    ab_re, ab_im = mag * jnp.cos(lam_im * dt), mag * jnp.sin(lam_im * dt)
    den = jnp.square(lam_re) + jnp.square(lam_im)
    num_re = ab_re - 1.0
    coef_re = (num_re * lam_re + ab_im * lam_im) / den
    coef_im = (ab_im * lam_re - num_re * lam_im) / den
    br, bi = b_re.astype(F32), b_im.astype(F32)
    bb_re = coef_re[..., None] * br - coef_im[..., None] * bi
    bb_im = coef_re[..., None] * bi + coef_im[..., None] * br
    bu_re = jnp.einsum('gpc,blgc->blgp', bb_re, ug)
    bu_im = jnp.einsum('gpc,blgc->blgp', bb_im, ug)
    h0r, h0i = h0_re.astype(F32), h0_im.astype(F32)
    bu_re = bu_re.at[:, 0].add(ab_re * h0r - ab_im * h0i)
    bu_im = bu_im.at[:, 0].add(ab_re * h0i + ab_im * h0r)
    a_re = jnp.broadcast_to(ab_re, bu_re.shape)
    a_im = jnp.broadcast_to(ab_im, bu_im.shape)
    _, _, hs_re, hs_im = lax.associative_scan(complex_affine_combine, (a_re, a_im, bu_re, bu_im), axis=1)
    y = (jnp.einsum('gcp,blgp->blgc', c_re.astype(F32), hs_re)
         - jnp.einsum('gcp,blgp->blgc', c_im.astype(F32), hs_im) + d.astype(F32) * ug)
    y = jax.nn.gelu(y.reshape(bt, L, S5_WIDTH))
    y1, y2 = jnp.split(jnp.einsum('blw,wv->blv', y, w_glu.astype(F32)) + b_glu.astype(F32), 2, axis=-1)
    y = y1 * jax.nn.sigmoid(y2) * jax.nn.silu(z)
    out = jnp.einsum('blw,wd->bld', y, w_out.astype(F32))
    return out.astype(x.dtype), hs_re[:, -1], hs_im[:, -1]


def trunk(x, st_gla, st_gdn, st_conv, st_re, st_im, wts):
    out_gla, out_gdn, out_conv, out_re, out_im = [], [], [], [], []
    for i in range(DEPTH):
        j = i // N_MIXERS
        kind = i % N_MIXERS
        if kind == 0:
            f, s = gla_mixer(x, wts['gla_w_in'][j], wts['gla_w_a2'][j], wts['gla_b_a'][j],
                             wts['gla_norm_g'][j], wts['gla_w_out'][j], st_gla[j])
            out_gla.append(s)
        elif kind == 1:
            f, cv, s = gdn_mixer(x, wts['gdn_w_in'][j], wts['gdn_w_conv'][j], wts['gdn_a_log'][j],
                                 wts['gdn_dt_bias'][j], wts['gdn_norm_g'][j], wts['gdn_w_out'][j],
                                 st_conv[j], st_gdn[j])
            out_gdn.append(s)
            out_conv.append(cv)
        else:
            f, hr, hi = s5_mixer(x, wts['s5_w_in'][j], wts['s5_lam_re'][j], wts['s5_lam_im'][j],
                                 wts['s5_log_dt'][j], wts['s5_b_re'][j], wts['s5_b_im'][j],
                                 wts['s5_c_re'][j], wts['s5_c_im'][j], wts['s5_d'][j],
                                 wts['s5_w_glu'][j], wts['s5_b_glu'][j], wts['s5_w_out'][j],
                                 st_re[j], st_im[j])
            out_re.append(hr)
            out_im.append(hi)
        x = layer_norm(ALPHA_RES * x + f, wts['ln_g'][i], wts['ln_b'][i])
    return (x, jnp.stack(out_gla), jnp.stack(out_gdn), jnp.stack(out_conv),
            jnp.stack(out_re), jnp.stack(out_im))


def setup_inputs(seed: int = 0) -> dict:
    key = jax.random.key(seed)
    ks = iter(jax.random.split(key, 48))
    nrm = lambda shape, scale: jax.random.normal(next(ks), shape, F32) * scale
    uni = lambda shape, lo, hi: jax.random.uniform(next(ks), shape, F32, lo, hi)
    x_prompt = nrm((BATCH, SEQ, D_MODEL), 1.0)
    x_sample = nrm((DEC_BATCH, DEC_SEQ, D_MODEL), 1.0)
    state_gla = nrm((N_GLA, DEC_BATCH, GLA_HEADS, GLA_DK, GLA_DV), 0.1)
    state_gdn = nrm((N_GDN, DEC_BATCH, GDN_HEADS, GDN_DK, GDN_DV), 0.1)
    state_gdn_conv = nrm((N_GDN, DEC_BATCH, GDN_CONV - 1, GDN_CONV_CH), 1.0)
    state_s5_re = nrm((N_S5, DEC_BATCH, S5_GROUPS, S5_STATE), 0.5)
    state_s5_im = nrm((N_S5, DEC_BATCH, S5_GROUPS, S5_STATE), 0.5)
    ln_g = 1.0 + nrm((DEPTH, D_MODEL), 0.01)
    ln_b = nrm((DEPTH, D_MODEL), 0.01)
    gla_w_in = nrm((N_GLA, D_MODEL, GLA_IN), D_MODEL ** -0.5)
    gla_w_a2 = nrm((N_GLA, GLA_LOWRANK, GLA_KW), GLA_LOWRANK ** -0.5)
    gla_b_a = nrm((N_GLA, GLA_KW), 0.1)
    gla_norm_g = 1.0 + nrm((N_GLA, GLA_DV), 0.01)
    gla_w_out = nrm((N_GLA, GLA_VW, D_MODEL), GLA_VW ** -0.5 * BETA_INIT)
    gdn_w_in = nrm((N_GDN, D_MODEL, GDN_IN), D_MODEL ** -0.5)
    gdn_w_conv = nrm((N_GDN, GDN_CONV, GDN_CONV_CH), GDN_CONV ** -0.5)
    gdn_a_log = jnp.log(uni((N_GDN, GDN_HEADS), 1.0, 16.0))
    gdn_dt = jnp.exp(uni((N_GDN, GDN_HEADS), math.log(1e-3), math.log(1e-1)))
    gdn_dt_bias = gdn_dt + jnp.log(-jnp.expm1(-gdn_dt))
    gdn_norm_g = 1.0 + nrm((N_GDN, GDN_DV), 0.01)
    gdn_w_out = nrm((N_GDN, GDN_VW, D_MODEL), GDN_VW ** -0.5 * BETA_INIT)
    s5_w_in = nrm((N_S5, D_MODEL, S5_IN), D_MODEL ** -0.5)
    s5_lam_re = -0.5 + nrm((N_S5, S5_GROUPS, S5_STATE), 0.01)
    s5_lam_im = math.pi * jnp.arange(S5_STATE, dtype=F32) + nrm((N_S5, S5_GROUPS, S5_STATE), 0.01)
    s5_log_dt = uni((N_S5, S5_GROUPS), math.log(1e-3), math.log(1e-1))
    s5_b_re = nrm((N_S5, S5_GROUPS, S5_STATE, S5_GROUP), (2 * S5_GROUP) ** -0.5)
    s5_b_im = nrm((N_S5, S5_GROUPS, S5_STATE, S5_GROUP), (2 * S5_GROUP) ** -0.5)
    s5_c_re = nrm((N_S5, S5_GROUPS, S5_GROUP, S5_STATE), (2 * S5_STATE) ** -0.5)
    s5_c_im = nrm((N_S5, S5_GROUPS, S5_GROUP, S5_STATE), (2 * S5_STATE) ** -0.5)
    s5_d = nrm((N_S5, S5_GROUPS, S5_GROUP), 1.0)
    s5_w_glu = nrm((N_S5, S5_WIDTH, 2 * S5_WIDTH), S5_WIDTH ** -0.5)
    s5_b_glu = nrm((N_S5, 2 * S5_WIDTH), 0.01)
    s5_w_out = nrm((N_S5, S5_WIDTH, D_MODEL), S5_WIDTH ** -0.5 * BETA_INIT)
    return {'x_prompt': x_prompt, 'x_sample': x_sample,
            'state_gla': state_gla, 'state_gdn': state_gdn, 'state_gdn_conv': state_gdn_conv,
            'state_s5_re': state_s5_re, 'state_s5_im': state_s5_im,
            'ln_g': ln_g, 'ln_b': ln_b,
            'gla_w_in': gla_w_in, 'gla_w_a2': gla_w_a2, 'gla_b_a': gla_b_a,
            'gla_norm_g': gla_norm_g, 'gla_w_out': gla_w_out,
            'gdn_w_in': gdn_w_in, 'gdn_w_conv': gdn_w_conv, 'gdn_a_log': gdn_a_log,
            'gdn_dt_bias': gdn_dt_bias, 'gdn_norm_g': gdn_norm_g, 'gdn_w_out': gdn_w_out,
            's5_w_in': s5_w_in, 's5_lam_re': s5_lam_re, 's5_lam_im': s5_lam_im, 's5_log_dt': s5_log_dt,
            's5_b_re': s5_b_re, 's5_b_im': s5_b_im, 's5_c_re': s5_c_re, 's5_c_im': s5_c_im,
            's5_d': s5_d, 's5_w_glu': s5_w_glu, 's5_b_glu': s5_b_glu, 's5_w_out': s5_w_out}


def reference(x_prompt, x_sample, state_gla, state_gdn, state_gdn_conv, state_s5_re, state_s5_im,
              ln_g, ln_b, gla_w_in, gla_w_a2, gla_b_a, gla_norm_g, gla_w_out,
              gdn_w_in, gdn_w_conv, gdn_a_log, gdn_dt_bias, gdn_norm_g, gdn_w_out,
              s5_w_in, s5_lam_re, s5_lam_im, s5_log_dt, s5_b_re, s5_b_im, s5_c_re, s5_c_im,
              s5_d, s5_w_glu, s5_b_glu, s5_w_out):
    wts = dict(ln_g=ln_g, ln_b=ln_b,
               gla_w_in=gla_w_in, gla_w_a2=gla_w_a2, gla_b_a=gla_b_a, gla_norm_g=gla_norm_g,
               gla_w_out=gla_w_out,
               gdn_w_in=gdn_w_in, gdn_w_conv=gdn_w_conv, gdn_a_log=gdn_a_log, gdn_dt_bias=gdn_dt_bias,
               gdn_norm_g=gdn_norm_g, gdn_w_out=gdn_w_out,
               s5_w_in=s5_w_in, s5_lam_re=s5_lam_re, s5_lam_im=s5_lam_im, s5_log_dt=s5_log_dt,
               s5_b_re=s5_b_re, s5_b_im=s5_b_im, s5_c_re=s5_c_re, s5_c_im=s5_c_im, s5_d=s5_d,
               s5_w_glu=s5_w_glu, s5_b_glu=s5_b_glu, s5_w_out=s5_w_out)
    bp = x_prompt.shape[0]
    z_gla = jnp.zeros((N_GLA, bp, GLA_HEADS, GLA_DK, GLA_DV), F32)
    z_gdn = jnp.zeros((N_GDN, bp, GDN_HEADS, GDN_DK, GDN_DV), F32)
    z_conv = jnp.zeros((N_GDN, bp, GDN_CONV - 1, GDN_CONV_CH), F32)
    z_s5 = jnp.zeros((N_S5, bp, S5_GROUPS, S5_STATE), F32)
    y_prompt, p_gla, p_gdn, p_gdn_conv, p_s5_re, p_s5_im = trunk(
        x_prompt, z_gla, z_gdn, z_conv, z_s5, z_s5, wts)
    y_sample, s_gla, s_gdn, s_gdn_conv, s_s5_re, s_s5_im = trunk(
        x_sample, state_gla, state_gdn, state_gdn_conv, state_s5_re, state_s5_im, wts)
    return (y_prompt, y_sample, p_gla, p_gdn, p_gdn_conv, p_s5_re, p_s5_im,
            s_gla, s_gdn, s_gdn_conv, s_s5_re, s_s5_im)
```

```python
import math
from contextlib import ExitStack
import numpy as np
import concourse.bass as bass
import concourse.mybir as mybir
from concourse.bass_utils import run_bass_kernel_spmd

F32 = mybir.dt.float32
BF16 = mybir.dt.bfloat16
I32 = mybir.dt.int32
F32R = mybir.dt.float32r
AF = mybir.ActivationFunctionType
ALU = mybir.AluOpType

D = 1024
DEPTH = 4
ALPHA = (2 * DEPTH) ** 0.25
LN_EPS = 1e-5
NORM_EPS = 1e-6
NCORES = 8
GLA_IN = 3088
GDN_IN = 4112
TWO_PI = 2.0 * math.pi
CW = 6 * 128 + 16 + 256


class Buf:
    __slots__ = ("name", "w", "r", "ex")

    def __init__(self, name, ex=False):
        self.name = name
        self.w = None
        self.r = []
        self.ex = ex


class TB:
    __slots__ = ("t", "b")

    def __init__(self, t, b):
        self.t = t
        self.b = b


class Ring:
    def __init__(self, items):
        self.items = items
        self.i = 0

    def next(self):
        it = self.items[self.i % len(self.items)]
        self.i += 1
        return it


ENGS = ("pe", "act", "dve", "pool", "sp")


class Prog:
    def __init__(self, nc, es):
        self.nc = nc
        self.es = es
        self.ops = {e: [] for e in ENGS}
        self.semobjs = []
        self.eng_sem = {}
        self.cnt = {}
        self.seen = {e: {} for e in ENGS}
        for e in ("pe", "act", "dve", "pool"):
            self.eng_sem[e] = self._newsem("c_" + e)
            self.cnt[e] = 0
        self.dslots = {}
        for q, k in (("sp", 16), ("pool", 8)):
            self.dslots[q] = [[self._newsem(f"d_{q}{i}"), 0] for i in range(k)]
        self.dcur = {"sp": 0, "pool": 0}
        allb = [TB(es.enter_context(nc.psum_tensor(f"psb{i}", [128, 512], F32)), Buf(f"psb{i}", ex=True)) for i in range(8)]
        self.banks = Ring(allb[0:5])
        self.lbanks = Ring(allb[5:8])

    def _newsem(self, name):
        s = self.es.enter_context(self.nc.semaphore(name))
        self.semobjs.append(s)
        return len(self.semobjs) - 1

    def sb(self, name, shape, dtype):
        t = self.es.enter_context(self.nc.sbuf_tensor(name, list(shape), dtype))
        return TB(t, Buf(name))

    def ring(self, name, shape, dtype, n):
        return Ring([self.sb(f"{name}{i}", shape, dtype) for i in range(n)])

    def bank(self):
        return self.banks.next()

    def lbank(self):
        return self.lbanks.next()

    def _deps(self, reads, writes, own=None):
        deps = {}
        for tb in reads:
            b = tb.b if isinstance(tb, TB) else tb
            if b.w is not None:
                s, v = b.w
                if deps.get(s, 0) < v:
                    deps[s] = v
            if b.ex:
                for (s, v) in b.r:
                    if s != own and deps.get(s, 0) < v:
                        deps[s] = v
        for tb in writes:
            b = tb.b if isinstance(tb, TB) else tb
            if b.w is not None:
                s, v = b.w
                if deps.get(s, 0) < v:
                    deps[s] = v
            for (s, v) in b.r:
                if deps.get(s, 0) < v:
                    deps[s] = v
        return deps

    def _commit(self, ev, reads, writes):
        for tb in writes:
            b = tb.b if isinstance(tb, TB) else tb
            b.w = ev
            b.r = []
        for tb in reads:
            b = tb.b if isinstance(tb, TB) else tb
            if b.w is not ev:
                b.r.append(ev)

    def _waits(self, eng, deps):
        w = []
        seen = self.seen[eng]
        for s, v in deps.items():
            if seen.get(s, 0) < v:
                seen[s] = v
                w.append((s, v))
        return w

    def op(self, eng, meth, *args, reads=(), writes=(), **kw):
        w = self._waits(eng, self._deps(reads, writes, self.eng_sem[eng]))
        self.cnt[eng] += 1
        ev = (self.eng_sem[eng], self.cnt[eng])
        self.ops[eng].append((w, (meth, args, kw), self.eng_sem[eng], 1))
        self._commit(ev, reads, writes)

    def dma(self, q, out, in_, reads=(), writes=(), **kw):
        deps = self._deps(reads, writes)
        slot = self.dslots[q][self.dcur[q] % len(self.dslots[q])]
        self.dcur[q] += 1
        if slot[1] > 0:
            if deps.get(slot[0], 0) < slot[1]:
                deps[slot[0]] = slot[1]
        w = self._waits(q, deps)
        slot[1] += 16
        ev = (slot[0], slot[1])
        kw = dict(kw); kw["out"] = out; kw["in_"] = in_
        self.ops[q].append((w, ("dma_start", (), kw), slot[0], 16))
        self._commit(ev, reads, writes)

    def barrier(self):
        evs = {}
        for e, sid in self.eng_sem.items():
            if self.cnt[e] > 0:
                evs[sid] = self.cnt[e]
        for q in self.dslots:
            for sid, c in self.dslots[q]:
                if c > 0:
                    evs[sid] = c
        for e in ENGS:
            w = self._waits(e, dict(evs))
            if w:
                self.ops[e].append((w, None, None, 0))

    def carve(self, arena, off, shape, dtype=F32, name="c"):
        n = 1
        for d in shape[1:]:
            n *= d
        if dtype == BF16:
            ap = arena.t[:, off:off + (n + 1) // 2].bitcast(BF16)[:, 0:n]
            used = (n + 1) // 2
        else:
            ap = arena.t[:, off:off + n]
            used = n
        if len(shape) == 3:
            ap = ap.rearrange("p (a b) -> p a b", a=shape[1])
        return TB(ap, Buf(name)), off + used

    def finish(self):
        w = []
        for q in self.dslots:
            for s, c in self.dslots[q]:
                if c > 0:
                    w.append((s, c))
        self.ops["sp"].append((w, None, None, 0))

    def emit(self):
        nc = self.nc
        block = self.es.enter_context(nc.Block())
        semobjs = self.semobjs

        def replay(lst):
            def f(e):
                for (w, fn, s, inc) in lst:
                    if fn is None:
                        for (ws, wv) in w:
                            e.wait_ge(semobjs[ws], wv)
                        continue
                    for (ws, wv) in w[:-1]:
                        e.wait_ge(semobjs[ws], wv)
                    ins = getattr(e, fn[0])(*fn[1], **fn[2])
                    if w:
                        ins.wait_op(semobjs[w[-1][0]], w[-1][1], "sem-ge")
                    ins.then_inc(semobjs[s], inc)
            return f

        block.tensor(replay(self.ops["pe"]))
        block.scalar(replay(self.ops["act"]))
        block.vector(replay(self.ops["dve"]))
        block.gpsimd(replay(self.ops["pool"]))
        block.sync(replay(self.ops["sp"]))


def build(n_pt=16, layers=(0, 1, 2, 3), has_sample=True):
    nc = bass.Bass("TRN2", target_bir_lowering=False)
    NT = n_pt + (2 if has_sample else 0)
    LP = n_pt * 128

    def din(name, shape, dt=F32):
        return nc.dram_tensor(name, list(shape), dt, kind="ExternalInput").ap()

    def dout(name, shape, dt=F32):
        return nc.dram_tensor(name, list(shape), dt, kind="ExternalOutput").ap()

    x_in = din("x_in", [NT * 128, D])
    y_out = dout("y_out", [NT * 128, D])
    xscr = [nc.dram_tensor(f"xscr{i}", [NT * 128, D], F32, kind="Internal").ap() for i in range(2)]
    consts = din("consts", [128, CW])
    ln_g = din("ln_g", [DEPTH, D])
    ln_b = din("ln_b", [DEPTH, D])
    gla_w_in = din("gla_w_in", [2, D, GLA_IN])
    gla_w_a2 = din("gla_w_a2", [2, 16, 512])
    gla_b_a = din("gla_b_a", [2, 512])
    gla_norm_g = din("gla_norm_g", [2, 256])
    gla_w_out = din("gla_w_out", [2, D, D])
    st_gla = din("st_gla", [2, 16, 4, 128, 256])
    p_gla = dout("p_gla", [2, 4, 128, 256])
    s_gla = dout("s_gla", [2, 16, 4, 128, 256])

    gdn_w_in = din("gdn_w_in", [1, D, GDN_IN])
    gdn_w_conv = din("gdn_w_conv", [1, 4, 3072])
    gdn_a_log = din("gdn_a_log", [1, 8])
    gdn_dt_bias = din("gdn_dt_bias", [1, 8])
    gdn_norm_g = din("gdn_norm_g", [1, 128])
    gdn_w_out = din("gdn_w_out", [1, D, D])
    st_gdn = din("st_gdn", [1, 16, 8, 128, 128])
    st_conv = din("st_conv", [1, 16, 3, 3072])
    p_gdn = dout("p_gdn", [1, 8, 128, 128])
    s_gdn = dout("s_gdn", [1, 16, 8, 128, 128])
    p_conv = dout("p_conv", [1, 3, 3072])
    s_conv = dout("s_conv", [1, 16, 3, 3072])

    s5_w_in = din("s5_w_in", [1, D, 2048])
    s5_lam_re = din("s5_lam_re", [1, 64, 64])
    s5_lam_im = din("s5_lam_im", [1, 64, 64])
    s5_log_dt = din("s5_log_dt", [1, 64])
    s5_bt = din("s5_bt", [128, 8 * 2 * 2 * 128])
    s5_ct = din("s5_ct", [128, 32 * 2 * 32])
    s5_d = din("s5_d", [1, 64, 16])
    s5_w_glu = din("s5_w_glu", [1, D, 2048])
    s5_b_glu = din("s5_b_glu", [1, 2048])
    s5_w_out = din("s5_w_out", [1, D, D])
    st_re = din("st_re", [1, 16, 64, 64])
    st_im = din("st_im", [1, 16, 64, 64])
    p_re = dout("p_re", [1, 64, 64])
    p_im = dout("p_im", [1, 64, 64])
    s_re = dout("s_re", [1, 16, 64, 64])
    s_im = dout("s_im", [1, 16, 64, 64])

    es = ExitStack()
    with es:
        P = Prog(nc, es)
        cst = P.sb("cst", [128, CW], F32)
        P.dma("sp", cst.t[:], consts, writes=[cst])
        ident = cst.t[:, 0:128]
        ones = cst.t[:, 128:256]
        MU = {1: cst.t[:, 256:384], 16: cst.t[:, 512:640]}
        ML = {1: cst.t[:, 384:512], 16: cst.t[:, 640:768]}
        seqind = cst.t[:, 768:784]
        kkrow = cst.t[:, 784:912]
        notfirst = cst.t[:, 912:1040]
        identb = P.sb("identb", [128, 128], BF16)
        P.op("act", "activation", out=identb.t[:], in_=ident, func=AF.Copy, reads=[cst], writes=[identb])
        identr = P.sb("identr", [128, 128], F32)
        P.op("act", "activation", out=identr.t[:].bitcast(F32R), in_=ident, func=AF.Copy, reads=[cst], writes=[identr])
        frt = es.enter_context(nc.sbuf_tensor("frt", [128, 12 * 128], F32))
        onesb = P.sb("onesb", [128, 128], BF16)
        P.op("act", "activation", out=onesb.t[:], in_=ones, func=AF.Copy, reads=[cst], writes=[onesb])
        sbf_r = P.ring("sbf", [128, 256], BF16, 2)

        winf = P.sb("win", [128, 8 * GDN_IN], BF16)
        win = TB(winf.t[:, 0:8 * GDN_IN].rearrange("p (k e) -> p k e", k=8), winf.b)
        win_gla = TB(winf.t[:, 0:8 * GLA_IN].rearrange("p (k e) -> p k e", k=8), winf.b)
        win_s5 = TB(winf.t[:, 0:8 * 2048].rearrange("p (k e) -> p k e", k=8), winf.b)
        wout = P.sb("wout", [128, 8, D], BF16)
        big = P.sb("big", [128, 8 * 1024], F32)
        sstate = big.t[:].rearrange("p (s e) -> p s e", s=8)
        pstate = P.sb("pstate", [128, 1024], F32)
        lng = P.sb("lng", [128, D], F32)
        lnb = P.sb("lnb", [128, D], F32)

        xs_r = P.ring("xs", [128, D], F32, 2)
        xb_r = P.ring("xb", [128, D], BF16, 1)
        xT_r = P.ring("xT", [128, 8, 128], BF16, 2)
        z_r = P.ring("zln", [128, D], F32, 1)
        st_r = P.ring("stat", [128, 16], F32, 2)
        ogT_r = P.ring("ogT", [128, 8, 128], BF16, 2)
        ARENA = 12600
        arena = P.sb("arena", [128, ARENA], F32)

        def carve_ring(off, n, shape, dtype=F32, name="r"):
            items = []
            for i in range(n):
                tb, off = P.carve(arena, off, shape, dtype, f"{name}{i}")
                items.append(tb)
            return Ring(items), off

        def load_w(dst, col0, src2d, ncols):
            v = src2d.rearrange("(kc p) e -> p kc e", p=128)
            for kc in range(8):
                c = 0
                while c < ncols:
                    w = min(1024, ncols - c)
                    P.dma("pool", dst.t[:, kc, col0 + c:col0 + c + w], v[:, kc, c:c + w], writes=[dst])
                    c += w

        def load_x(src, t):
            xs = xs_r.next()
            P.dma("sp", xs.t[:], src[t * 128:(t + 1) * 128, :], writes=[xs])
            return xs

        def make_xT(xs):
            xb = xb_r.next()
            P.op("act", "activation", out=xb.t[:], in_=xs.t[:], func=AF.Copy, reads=[xs], writes=[xb])
            bk = P.bank()
            pb = bk.t[:].bitcast(BF16).rearrange("p (k t) -> p k t", k=8)
            for kc in range(8):
                P.op("pe", "transpose", pb[:, kc, :], xb.t[:, kc * 128:(kc + 1) * 128], identb.t[:],
                     reads=[xb, identb], writes=[bk])
            xT = xT_r.next()
            P.op("dve", "tensor_copy", out=xT.t[:], in_=pb, reads=[bk], writes=[xT])
            return xT

        def proj_tm(xT, w, c0, ncols, out_ap, bk):
            for kc in range(8):
                P.op("pe", "matmul", out_ap, lhsT=xT.t[:, kc, :], rhs=w.t[:, kc, c0:c0 + ncols],
                                                     start=(kc == 0), stop=(kc == 7),
                     reads=[xT, w], writes=[bk])

        def proj_fm(xT, w, c0, ncols, out_ap, bk):
            for kc in range(8):
                P.op("pe", "matmul", out_ap, lhsT=w.t[:, kc, c0:c0 + ncols], rhs=xT.t[:, kc, :],
                                                     start=(kc == 0), stop=(kc == 7),
                     reads=[xT, w], writes=[bk])

        def out_proj_ln(ogT, xs, layer, dst, t):
            z = z_r.next()
            for half in range(2):
                bk = P.bank()
                for kc in range(8):
                    P.op("pe", "matmul",
                        bk.t[:, :], lhsT=ogT.t[:, kc, :], rhs=wout.t[:, kc, half * 512:(half + 1) * 512],
                        start=(kc == 0), stop=(kc == 7), reads=[ogT, wout], writes=[bk])
                P.op("dve", "scalar_tensor_tensor",
                    out=z.t[:, half * 512:(half + 1) * 512], in0=xs.t[:, half * 512:(half + 1) * 512], scalar=ALPHA,
                    in1=bk.t[:, :], op0=ALU.mult, op1=ALU.add, reads=[xs, bk], writes=[z])
            st = st_r.next()
            P.op("dve", "bn_stats", out=st.t[:, 0:6], in_=z.t[:, 0:512], reads=[z], writes=[st])
            P.op("dve", "bn_stats", out=st.t[:, 6:12], in_=z.t[:, 512:1024], reads=[z], writes=[st])
            P.op("dve", "bn_aggr", out=st.t[:, 12:14], in_=st.t[:, 0:12], reads=[st], writes=[st])
            P.op("act", "activation", out=st.t[:, 14:15], in_=st.t[:, 13:14], func=AF.Sqrt, bias=epsb.t[:, 0:1],
                 reads=[st, epsb], writes=[st])
            P.op("dve", "reciprocal", out=st.t[:, 15:16], in_=st.t[:, 14:15], reads=[st], writes=[st])
            P.op("dve", "tensor_scalar", out=z.t[:], in0=z.t[:], scalar1=st.t[:, 12:13], scalar2=st.t[:, 15:16],
                                                  op0=ALU.subtract, op1=ALU.mult, reads=[z, st], writes=[z])
            P.op("pool", "tensor_tensor", out=z.t[:], in0=z.t[:], in1=lng.t[:], op=ALU.mult, reads=[z, lng], writes=[z])
            P.op("pool", "tensor_tensor", out=z.t[:], in0=z.t[:], in1=lnb.t[:], op=ALU.add, reads=[z, lnb], writes=[z])
            P.dma("sp", dst[t * 128:(t + 1) * 128, :], z.t[:], reads=[z])

        epsb = P.sb("epsb", [128, 4], F32)
        P.op("pool", "memset", epsb.t[:, 0:1], LN_EPS, writes=[epsb])
        P.op("pool", "memset", epsb.t[:, 1:2], NORM_EPS, writes=[epsb])
        P.op("pool", "memset", epsb.t[:, 2:3], 1.0, writes=[epsb])
        P.op("pool", "memset", epsb.t[:, 3:4], -math.pi, writes=[epsb])

        def load_ln(layer):
            P.dma("sp", lng.t[:], ln_g[layer:layer + 1, :].partition_broadcast(128), writes=[lng])
            P.dma("sp", lnb.t[:], ln_b[layer:layer + 1, :].partition_broadcast(128), writes=[lnb])

        def gla_layer(j, layer, src, dst):
            win = win_gla
            P.barrier()
            off = 0
            w512, off = carve_ring(off, 6, [128, 512], name="w512_")
            v_r, off = carve_ring(off, 2, [128, D], name="vtm")
            sring, skz = [], []
            for st_ in range(2):
                r_, off = carve_ring(off, 12, [128, 128], name=f"gA{st_}_")
                k_, off = carve_ring(off, 3, [128, 128], name=f"kz{st_}_")
                sring.append(r_)
                skz.append(k_)
            stg_r, off = carve_ring(off, 8, [128, 512], BF16, name="pstg")
            lrT_, off = P.carve(arena, off, [128, 128], F32, "lrT")
            wa2_, off = P.carve(arena, off, [128, 512], F32, "wa2")
            gcol, off = P.carve(arena, off, [128, 2], F32, "gcol")
            lrT = TB(lrT_.t[0:33, :], lrT_.b)
            wa2 = TB(wa2_.t[0:33, :], wa2_.b)
            P.op("pool", "memset", lrT.t[:], 0.0, writes=[lrT])
            P.op("pool", "memset", lrT.t[32:33, :], 1.0, writes=[lrT])
            assert off <= ARENA

            def b16(tb, n=128):
                return TB(tb.t[:].bitcast(BF16)[:, 0:n], tb.b)
            load_w(win, 0, gla_w_in[j], GLA_IN)
            load_w(wout, 0, gla_w_out[j], D)
            load_ln(layer)
            P.op("pool", "memset", wa2.t[:], 0.0, writes=[wa2])
            P.dma("sp", wa2.t[0:16, :], gla_w_a2[j], writes=[wa2])
            P.dma("sp", wa2.t[32:33, :], gla_b_a[j:j + 1, :], writes=[wa2])
            P.dma("sp", gcol.t[:], gla_norm_g[j].rearrange("(vb p) -> p vb", p=128), writes=[gcol], allow_slow_non_contiguous=True)
            P.op("pool", "memset", pstate.t[:], 0.0, writes=[pstate])
            order = ([n_pt, n_pt + 1] if has_sample else []) + list(range(n_pt))
            for t in order:
                samp = (t >= n_pt)
                nseq = 16 if samp else 1
                nreal = 8 if samp else 1
                L = 128 // nseq
                stb = big if samp else pstate
                if samp:
                    for s in range(8):
                        P.dma("sp", sstate[:, s, :].rearrange("p (h v) -> p h v", h=4),
                              st_gla[j, (t - n_pt) * 8 + s].rearrange("h d v -> d h v"), writes=[big])

                def S_ap(s, lo, hi):
                    return sstate[:, s, lo:hi] if samp else pstate.t[:, lo:hi]

                xs = load_x(src, t)
                xT = make_xT(xs)
                bk = P.bank()
                proj_fm(xT, win, 3072, 16, bk.t[0:16, 0:128], bk)
                P.op("act", "activation", out=lrT.t[0:16, :], in_=bk.t[0:16, 0:128], func=AF.Copy,
                     reads=[bk], writes=[lrT])
                bk = P.bank()
                P.op("pe", "matmul", bk.t[:, :], lhsT=lrT.t[:, :], rhs=wa2.t[:, :], start=True, stop=True,
                     reads=[lrT, wa2], writes=[bk])
                la = w512.next()
                P.op("act", "activation", out=la.t[:], in_=bk.t[:, :], func=AF.Exp, scale=-1.0, reads=[bk], writes=[la])
                P.op("act", "activation", out=la.t[:], in_=la.t[:], func=AF.Ln, bias=epsb.t[:, 2:3], reads=[la, epsb], writes=[la])
                bk = P.bank()
                P.op("pe", "matmul", bk.t[:, :], lhsT=ML[nseq], rhs=la.t[:], start=True, stop=True,
                     reads=[cst, la], writes=[bk])
                ek = w512.next()
                P.op("act", "activation", out=ek.t[:], in_=bk.t[:, :], func=AF.Exp, scale=-1.0 / 16, reads=[bk], writes=[ek])
                bk = P.bank()
                proj_tm(xT, win, 512, 512, bk.t[:, :], bk)
                kS = stg_r.next()
                P.op("act", "activation", out=kS.t[:], in_=bk.t[:, :], func=AF.Copy, reads=[bk], writes=[kS])
                kend = b16(w512.next(), 512)
                P.op("dve", "tensor_tensor", out=kend.t[:], in0=bk.t[:, :], in1=ek.t[:], op=ALU.mult,
                     reads=[bk, ek], writes=[kend])
                bk = P.bank()
                proj_tm(xT, win, 0, 512, bk.t[:, :], bk)
                qS = stg_r.next()
                P.op("act", "activation", out=qS.t[:], in_=bk.t[:, :], func=AF.Copy, reads=[bk], writes=[qS])
                rS = []
                for half in range(2):
                    bk = P.bank()
                    proj_tm(xT, win, 2048 + half * 512, 512, bk.t[:, :], bk)
                    r_ = stg_r.next()
                    P.op("dve", "tensor_copy", out=r_.t[:], in_=bk.t[:, :], reads=[bk], writes=[r_])
                    rS.append(r_)
                v = b16(v_r.next(), 1024)
                for half in range(2):
                    bk = P.bank()
                    proj_tm(xT, win, 1024 + half * 512, 512, bk.t[:, :], bk)
                    P.op("act", "activation", out=v.t[:, half * 512:(half + 1) * 512], in_=bk.t[:, :], func=AF.Copy,
                         reads=[bk], writes=[v])
                ogT = ogT_r.next()
                def gla_head(h, w128, kz_r):
                    bkB = P.bank()
                    P.op("pe", "matmul", bkB.t[:, 0:128], lhsT=la.t[:, h * 128:(h + 1) * 128], rhs=MU[nseq], start=True, stop=True,
                         reads=[la, cst], writes=[bkB])
                    e1 = w128.next()
                    e2 = w128.next()
                    P.op("act", "activation", out=e1.t[:], in_=bkB.t[:, 0:128], func=AF.Exp, scale=-1.0 / 16, reads=[bkB], writes=[e1])
                    P.op("act", "activation", out=e2.t[:], in_=bkB.t[:, 0:128], func=AF.Exp, scale=1.0 / 16, reads=[bkB], writes=[e2])
                    yield
                    bkq = P.bank()
                    bkqb = bkq.t[:].bitcast(BF16)
                    P.op("pe", "transpose", bkqb[:, 0:128], qS.t[:, h * 128:(h + 1) * 128], identb.t[:], reads=[qS, identb], writes=[bkq])
                    P.op("pe", "transpose", bkqb[:, 128:256], kS.t[:, h * 128:(h + 1) * 128], identb.t[:], reads=[kS, identb], writes=[bkq])
                    qd = b16(w128.next())
                    P.op("dve", "scalar_tensor_tensor", out=qd.t[:], in0=bkqb[:, 0:128], scalar=128.0 ** -0.5, in1=e1.t[:],
                         op0=ALU.mult, op1=ALU.mult, reads=[bkq, e1], writes=[qd])
                    ki = b16(w128.next())
                    P.op("dve", "tensor_tensor", out=ki.t[:], in0=bkqb[:, 128:256], in1=e2.t[:], op=ALU.mult,
                         reads=[bkq, e2], writes=[ki])
                    yield
                    bka = P.bank()
                    P.op("pe", "matmul", bka.t[:, 0:128], lhsT=ki.t[:], rhs=qd.t[:], start=True, stop=True,
                         reads=[ki, qd], writes=[bka])
                    att = b16(w128.next())
                    P.op("dve", "tensor_tensor", out=att.t[:], in0=bka.t[:, 0:128], in1=MU[nseq], op=ALU.mult,
                         reads=[bka, cst], writes=[att])
                    yield
                    obk = P.lbank()
                    first = True
                    for s in range(nreal):
                        sb_ = sbf_r.next()
                        P.op("act", "activation", out=sb_.t[:, 0:256], in_=S_ap(s, h * 256, (h + 1) * 256), func=AF.Copy, reads=[stb], writes=[sb_])
                        for vb in range(2):
                            P.op("pe", "matmul", obk.t[:, vb * 128 + s * L:vb * 128 + (s + 1) * L], lhsT=sb_.t[:, vb * 128:(vb + 1) * 128],
                                 rhs=qd.t[:, s * L:(s + 1) * L], start=first, stop=False, skip_group_check=True, reads=[sb_, qd], writes=[obk])
                            first = False
                    for vb in range(2):
                        c0 = h * 256 + vb * 128
                        P.op("pe", "matmul", obk.t[:, vb * 128:(vb + 1) * 128], lhsT=v.t[:, c0:c0 + 128], rhs=att.t[:], start=False, stop=(vb == 1),
                             skip_group_check=True, reads=[v, att], writes=[obk])
                    yield
                    for s in range(nreal):
                        if samp:
                            kz = b16(kz_r.next())
                            P.op("dve", "tensor_scalar", out=kz.t[:], in0=kend.t[:, h * 128:(h + 1) * 128],
                                 scalar1=seqind[:, s:s + 1], scalar2=None, op0=ALU.mult, reads=[kend, cst], writes=[kz])
                            kzap = kz.t[:]
                            kzb = kz
                        else:
                            kzap = kend.t[:, h * 128:(h + 1) * 128]
                            kzb = kend
                        bks = P.bank()
                        P.op("pe", "matmul", bks.t[:, 0:256], lhsT=kzap, rhs=v.t[:, h * 256:(h + 1) * 256], start=True, stop=True,
                             reads=[kzb, v], writes=[bks])
                        P.op("dve", "scalar_tensor_tensor",
                             out=S_ap(s, h * 256, (h + 1) * 256), in0=S_ap(s, h * 256, (h + 1) * 256),
                             scalar=e1.t[:, (s + 1) * L - 1:(s + 1) * L], in1=bks.t[:, 0:256], op0=ALU.mult, op1=ALU.add,
                             reads=[stb, e1, bks], writes=[stb])
                        yield
                    osq = b16(w128.next(), 256)
                    P.op("act", "activation", out=osq.t[:, 0:256], in_=obk.t[:, 0:256], func=AF.Square, reads=[obk], writes=[osq])
                    yield
                    bkn = P.bank()
                    for vb in range(2):
                        P.op("pe", "matmul", bkn.t[:, 0:128], lhsT=onesb.t[:], rhs=osq.t[:, vb * 128:(vb + 1) * 128], start=(vb == 0), stop=(vb == 1),
                             reads=[onesb, osq], writes=[bkn])
                    rs = w128.next()
                    P.op("act", "activation", out=rs.t[:], in_=bkn.t[:, 0:128], func=AF.Sqrt, scale=1.0 / 256, bias=epsb.t[:, 1:2],
                         reads=[bkn, epsb], writes=[rs])
                    yield
                    P.op("dve", "reciprocal", out=rs.t[:], in_=rs.t[:], reads=[rs], writes=[rs])
                    for vb in range(2):
                        bkr = P.bank()
                        bkrb = bkr.t[:].bitcast(BF16)
                        rc = h * 256 + vb * 128
                        P.op("pe", "transpose", bkrb[:, 0:128], rS[rc // 512].t[:, rc % 512:rc % 512 + 128], identb.t[:],
                             reads=[rS[rc // 512], identb], writes=[bkr])
                        sr = w128.next()
                        P.op("act", "activation", out=sr.t[:], in_=bkrb[:, 0:128], func=AF.Silu, reads=[bkr], writes=[sr])
                        t1 = w128.next()
                        P.op("dve", "scalar_tensor_tensor", out=t1.t[:], in0=obk.t[:, vb * 128:(vb + 1) * 128], scalar=gcol.t[:, vb:vb + 1], in1=rs.t[:],
                             op0=ALU.mult, op1=ALU.mult, reads=[obk, gcol, rs], writes=[t1])
                        P.op("pool", "tensor_tensor", out=ogT.t[:, h * 2 + vb, :], in0=t1.t[:], in1=sr.t[:], op=ALU.mult,
                             reads=[t1, sr], writes=[ogT])
                        yield

                def chain2(g1, g2):
                    yield from g1
                    yield from g2

                streams = [chain2(gla_head(0, sring[0], skz[0]), gla_head(1, sring[0], skz[0])),
                           chain2(gla_head(2, sring[1], skz[1]), gla_head(3, sring[1], skz[1]))]
                while streams:
                    for g_ in list(streams):
                        try:
                            next(g_)
                        except StopIteration:
                            streams.remove(g_)
                out_proj_ln(ogT, xs, layer, dst, t)
                if samp:
                    for s in range(8):
                        P.dma("sp", s_gla[j, (t - n_pt) * 8 + s].rearrange("h d v -> d h v"), sstate[:, s, :].rearrange("p (h v) -> p h v", h=4), reads=[big])
            P.dma("sp", p_gla[j].rearrange("h d v -> d h v"), pstate.t[:].rearrange("p (h v) -> p h v", h=4), reads=[pstate])


        GSTOP = 99

        def gdn_layer(j, layer, src, dst):
            P.barrier()
            off = 0
            role = {}
            for nm in ("kTM", "vTM", "qdT", "egcb", "atts", "nwT", "vn"):
                role[nm] = []
                for hl in range(4):
                    tb, off = P.carve(arena, off, [128, 128], F32, f"{nm}{hl}")
                    role[nm].append(tb)
            for nm in ("kTM", "vTM", "qdT", "atts", "nwT", "vn"):
                role[nm] = [TB(tb.t[:].bitcast(BF16)[:, 0:128], tb.b) for tb in role[nm]]
            for i_, nm in enumerate(("X", "XT", "TT")):
                role[nm] = [TB(frt[:, (i_ * 4 + hl) * 128:(i_ * 4 + hl + 1) * 128].bitcast(F32R), Buf(f"{nm}{hl}")) for hl in range(4)]
            w128, off = carve_ring(off, 8, [128, 128], name="g128_")
            gstg, off = carve_ring(off, 4, [128, 512], BF16, name="gstg")
            sring, sext = [], []
            for st_ in range(2):
                r_, off = carve_ring(off, 12, [128, 128], name=f"sA{st_}_")
                e_, off = carve_ring(off, 3, [128, 176], name=f"ext{st_}_")
                sring.append(r_)
                sext.append(e_)

            def b16(tb):
                return TB(tb.t[:].bitcast(BF16)[:, 0:128], tb.b)

            def f32r(tb):
                return tb
            cvs_r, off = carve_ring(off, 2, [128, 512], name="cvs")
            scal, off = P.carve(arena, off, [128, 64], F32, "scal")
            abt, off = P.carve(arena, off, [128, 16], F32, "abt")
            cvT, off = P.carve(arena, off, [128, 24 * 48], F32, "cvT")
            hal, off = P.carve(arena, off, [128, 72], F32, "hal")
            wcc, off = P.carve(arena, off, [128, 96], F32, "wcc")
            gb, off = P.carve(arena, off, [128, 24], F32, "gb")
            gcolg, off = P.carve(arena, off, [128, 2], F32, "gcolg")
            assert off <= ARENA, off
            cvT4 = cvT.t[:].rearrange("p (b s r) -> p b s r", b=24, s=16)
            hal3 = hal.t[:].rearrange("p (b r) -> p b r", b=24)
            Xr = [role["X"], role["X"]]
            XTr = [role["XT"], role["XT"]]
            TTr = [role["TT"], role["TT"]]

            load_w(win, 0, gdn_w_in[j], GDN_IN)
            load_w(wout, 0, gdn_w_out[j], D)
            load_ln(layer)

            def load_T(src2d, nrows, dst_fn, dst_tb):
                for c in range(6):
                    stg = cvs_r.next()
                    P.dma("sp", stg.t[0:nrows, :], src2d[:, c * 512:(c + 1) * 512], writes=[stg])
                    bk = P.bank()
                    for b4 in range(4):
                        P.op("pe", "matmul", bk.t[:, b4 * 32:b4 * 32 + nrows], lhsT=stg.t[0:nrows, b4 * 128:(b4 + 1) * 128],
                             rhs=ident[0:nrows, 0:nrows], start=True, stop=True, reads=[stg, cst], writes=[bk])
                    for b4 in range(4):
                        P.op("act", "activation", out=dst_fn(c * 4 + b4), in_=bk.t[:, b4 * 32:b4 * 32 + nrows], func=AF.Copy,
                             reads=[bk], writes=[dst_tb])

            load_T(gdn_w_conv[j], 4, lambda blk: wcc.t[:, blk * 4:(blk + 1) * 4], wcc)
            P.dma("sp", gb.t[:, 0:8], gdn_a_log[j:j + 1, :].partition_broadcast(128), writes=[gb])
            P.dma("sp", gb.t[:, 8:16], gdn_dt_bias[j:j + 1, :].partition_broadcast(128), writes=[gb])
            P.op("act", "activation", out=gb.t[:, 16:24], in_=gb.t[:, 0:8], func=AF.Exp, reads=[gb], writes=[gb])
            P.dma("sp", gcolg.t[:, 0:1], gdn_norm_g[j].rearrange("(p o) -> p o", o=1), writes=[gcolg], allow_slow_non_contiguous=True)
            P.op("pool", "memset", hal.t[:], 0.0, writes=[hal])
            P.op("pool", "memset", cvT.t[:], 0.0, writes=[cvT])
            P.op("pool", "memset", pstate.t[:], 0.0, writes=[pstate])

            order = ([n_pt, n_pt + 1] if has_sample else []) + list(range(n_pt))
            if GSTOP <= 1:
                order = []
            for t in order:
                samp = (t >= n_pt)
                nseq = 16 if samp else 1
                nreal = 8 if samp else 1
                L = 128 // nseq
                stb = big if samp else pstate
                g0 = (t - n_pt) * 8

                def S_ap(s, lo, hi):
                    return sstate[:, s, lo:hi] if samp else pstate.t[:, lo:hi]

                xs = load_x(src, t)
                xT = make_xT(xs)
                if samp:
                    for s in range(8):
                        P.dma("sp", sstate[:, s, :].rearrange("p (h v) -> p h v", h=8),
                              st_gdn[j, g0 + s].rearrange("h d v -> d h v"), writes=[big])
                    load_T(st_conv[j, g0:g0 + 8].rearrange("s r c -> (s r) c"), 24,
                           lambda blk: cvT4[:, blk, 0:8, :], cvT)
                bk = P.bank()
                proj_tm(xT, win, 4096, 16, bk.t[:, 0:16], bk)
                P.op("act", "activation", out=abt.t[:], in_=bk.t[:, 0:16], func=AF.Copy, reads=[bk], writes=[abt])
                sc = scal.t
                P.op("dve", "tensor_tensor", out=sc[:, 0:8], in0=abt.t[:, 0:8], in1=gb.t[:, 8:16], op=ALU.add, reads=[abt, gb], writes=[scal])
                P.op("act", "activation", out=sc[:, 0:8], in_=sc[:, 0:8], func=AF.Exp, reads=[scal], writes=[scal])
                P.op("act", "activation", out=sc[:, 0:8], in_=sc[:, 0:8], func=AF.Ln, bias=epsb.t[:, 2:3], reads=[scal, epsb], writes=[scal])
                P.op("dve", "tensor_tensor", out=sc[:, 0:8], in0=sc[:, 0:8], in1=gb.t[:, 16:24], op=ALU.mult, reads=[scal, gb], writes=[scal])
                P.op("act", "activation", out=sc[:, 8:16], in_=abt.t[:, 8:16], func=AF.Exp, scale=-1.0, reads=[abt], writes=[scal])
                P.op("dve", "tensor_scalar", out=sc[:, 8:16], in0=sc[:, 8:16], scalar1=1.0, scalar2=None, op0=ALU.add, reads=[scal], writes=[scal])
                P.op("dve", "reciprocal", out=sc[:, 8:16], in_=sc[:, 8:16], reads=[scal], writes=[scal])
                P.op("dve", "tensor_scalar", out=sc[:, 16:24], in0=sc[:, 8:16], scalar1=-1.0, scalar2=None, op0=ALU.mult, reads=[scal], writes=[scal])
                bk = P.bank()
                P.op("pe", "matmul", bk.t[:, 0:8], lhsT=MU[nseq], rhs=sc[:, 0:8], start=True, stop=True, reads=[cst, scal], writes=[bk])
                P.op("pe", "matmul", bk.t[:, 8:16], lhsT=ML[nseq], rhs=sc[:, 0:8], start=True, stop=True, reads=[cst, scal], writes=[bk])
                P.op("dve", "tensor_copy", out=sc[:, 24:32], in_=bk.t[:, 0:8], reads=[bk], writes=[scal])
                P.op("act", "activation", out=sc[:, 32:48], in_=bk.t[:, 0:16], func=AF.Exp, scale=-1.0, reads=[bk], writes=[scal])
                P.op("dve", "tensor_tensor", out=sc[:, 48:56], in0=sc[:, 8:16], in1=sc[:, 32:40], op=ALU.mult, reads=[scal], writes=[scal])
                G_, BETA_, NBETA_, GC_, EGC_, KSUF_, BG_ = 0, 8, 16, 24, 32, 40, 48

                ogT = ogT_r.next()
                if GSTOP < 99:
                    P.op("pool", "memset", ogT.t[:], 0.0, writes=[ogT])
                pendingG = None
                for grp in range(2):
                    stgP = []
                    for pi in range(3):
                        bk = P.bank()
                        proj_tm(xT, win, pi * 1024 + grp * 512, 512, bk.t[:, :], bk)
                        st_ = gstg.next()
                        if pi == 1:
                            P.op("dve", "tensor_copy", out=st_.t[:], in_=bk.t[:, :], reads=[bk], writes=[st_])
                        else:
                            P.op("act", "activation", out=st_.t[:], in_=bk.t[:, :], func=AF.Copy, reads=[bk], writes=[st_])
                        stgP.append(st_)

                    def headABC(hl, r128, rext):
                        h = grp * 4 + hl
                        bkp = P.bank()
                        bkpb = bkp.t[:].bitcast(BF16)
                        for pi in range(3):
                            P.op("pe", "transpose", bkpb[:, pi * 128:(pi + 1) * 128], stgP[pi].t[:, hl * 128:(hl + 1) * 128], identb.t[:],
                                 reads=[stgP[pi], identb], writes=[bkp])
                        exts = []
                        for pi in range(3):
                            blk = pi * 8 + h
                            ext = rext.next()
                            ev = ext.t[:, 0:nseq * (L + 3)].rearrange("p (s l) -> p s l", s=nseq)
                            P.op("act", "activation", out=ev[:, :, 3:3 + L],
                                 in_=bkpb[:, pi * 128:(pi + 1) * 128].rearrange("p (s l) -> p s l", s=nseq), func=AF.Copy,
                                 reads=[bkp], writes=[ext])
                            if samp:
                                P.op("pool", "tensor_copy", out=ev[:, :, 0:3], in_=cvT4[:, blk, :, :], reads=[cvT], writes=[ext])
                            else:
                                P.op("pool", "tensor_copy", out=ev[:, 0, 0:3], in_=hal3[:, blk, :], reads=[hal], writes=[ext])
                                P.op("pool", "tensor_copy", out=hal3[:, blk, :], in_=ev[:, 0, L:L + 3], reads=[ext], writes=[hal])
                            exts.append((ext, ev))
                        yield
                        qT = kT = vT = None
                        for pi in range(3):
                            blk = pi * 8 + h
                            ext, ev = exts[pi]
                            acc = r128.next()
                            av = acc.t[:].rearrange("p (s l) -> p s l", s=nseq)
                            P.op("dve", "tensor_scalar", out=av, in0=ev[:, :, 0:L], scalar1=wcc.t[:, blk * 4:blk * 4 + 1], scalar2=None,
                                 op0=ALU.mult, reads=[ext, wcc], writes=[acc])
                            for tau in range(1, 4):
                                P.op("dve", "scalar_tensor_tensor", out=av, in0=ev[:, :, tau:tau + L], scalar=wcc.t[:, blk * 4 + tau:blk * 4 + tau + 1],
                                     in1=av, op0=ALU.mult, op1=ALU.add, reads=[ext, wcc, acc], writes=[acc])
                            y = r128.next()
                            if pi == 2:
                                y = b16(y)
                            P.op("act", "activation", out=y.t[:], in_=acc.t[:], func=AF.Silu, reads=[acc], writes=[y])
                            if pi < 2:
                                sq = b16(r128.next())
                                P.op("pool", "tensor_tensor", out=sq.t[:], in0=y.t[:], in1=y.t[:], op=ALU.mult, reads=[y], writes=[sq])
                                yield
                                bkn = P.bank()
                                P.op("pe", "matmul", bkn.t[:, 0:128], lhsT=onesb.t[:], rhs=sq.t[:], start=True, stop=True, reads=[onesb, sq], writes=[bkn])
                                rn = r128.next()
                                P.op("act", "activation", out=rn.t[:], in_=bkn.t[:, 0:128], func=AF.Sqrt, bias=epsb.t[:, 1:2], reads=[bkn, epsb], writes=[rn])
                                yield
                                P.op("dve", "reciprocal", out=rn.t[:], in_=rn.t[:], reads=[rn], writes=[rn])
                                o_ = b16(r128.next())
                                if pi == 0:
                                    P.op("dve", "scalar_tensor_tensor", out=o_.t[:], in0=y.t[:], scalar=128.0 ** -0.5, in1=rn.t[:],
                                         op0=ALU.mult, op1=ALU.mult, reads=[y, rn], writes=[o_])
                                    qT = o_
                                else:
                                    P.op("dve", "tensor_tensor", out=o_.t[:], in0=y.t[:], in1=rn.t[:], op=ALU.mult, reads=[y, rn], writes=[o_])
                                    kT = o_
                            else:
                                vT = y
                            yield
                        bkt = P.bank()
                        bktb = bkt.t[:].bitcast(BF16)
                        kTM = role["kTM"][hl]
                        vTM = role["vTM"][hl]
                        P.op("pe", "transpose", bktb[:, 0:128], kT.t[:], identb.t[:], reads=[kT, identb], writes=[bkt])
                        P.op("pe", "transpose", bktb[:, 128:256], vT.t[:], identb.t[:], reads=[vT, identb], writes=[bkt])
                        P.op("act", "activation", out=kTM.t[:], in_=bktb[:, 0:128], func=AF.Copy, reads=[bkt], writes=[kTM])
                        P.op("dve", "tensor_copy", out=vTM.t[:], in_=bktb[:, 128:256], reads=[bkt], writes=[vTM])
                        mug = r128.next()
                        P.op("dve", "tensor_scalar", out=mug.t[:], in0=MU[nseq], scalar1=sc[:, G_ + h:G_ + h + 1], scalar2=None, op0=ALU.mult,
                             reads=[cst, scal], writes=[mug])
                        yield
                        bkg = P.bank()
                        P.op("pe", "matmul", bkg.t[:, 0:128], lhsT=ones, rhs=mug.t[:], start=True, stop=True, reads=[cst, mug], writes=[bkg])
                        dL = r128.next()
                        P.op("dve", "tensor_scalar", out=dL.t[:], in0=bkg.t[:, 0:128], scalar1=sc[:, GC_ + h:GC_ + h + 1], scalar2=0.0,
                             op0=ALU.subtract, op1=ALU.min, reads=[bkg, scal], writes=[dL])
                        dT = r128.next()
                        P.op("dve", "tensor_scalar", out=dT.t[:], in0=bkg.t[:, 0:128], scalar1=sc[:, GC_ + h:GC_ + h + 1], scalar2=0.0,
                             op0=ALU.subtract, op1=ALU.max, reads=[bkg, scal], writes=[dT])
                        egcb = role["egcb"][hl]
                        P.op("act", "activation", out=egcb.t[:], in_=bkg.t[:, 0:128], func=AF.Exp, scale=-1.0, reads=[bkg], writes=[egcb])
                        yield
                        P.op("act", "activation", out=dL.t[:], in_=dL.t[:], func=AF.Exp, reads=[dL], writes=[dL])
                        P.op("act", "activation", out=dT.t[:], in_=dT.t[:], func=AF.Exp, scale=-1.0, reads=[dT], writes=[dT])
                        P.op("dve", "scalar_tensor_tensor", out=dL.t[:], in0=dL.t[:], scalar=sc[:, NBETA_ + h:NBETA_ + h + 1], in1=ML[nseq],
                             op0=ALU.mult, op1=ALU.mult, reads=[dL, scal, cst], writes=[dL])
                        P.op("pool", "tensor_tensor", out=dT.t[:], in0=dT.t[:], in1=MU[nseq], op=ALU.mult, reads=[dT, cst], writes=[dT])
                        qdT = role["qdT"][hl]
                        P.op("dve", "tensor_tensor", out=qdT.t[:], in0=qT.t[:], in1=egcb.t[:], op=ALU.mult, reads=[qT, egcb], writes=[qdT])
                        yield
                        bkk = P.bank()
                        P.op("pe", "matmul", bkk.t[:, 0:128], lhsT=kT.t[:], rhs=kT.t[:], start=True, stop=True, reads=[kT], writes=[bkk])
                        P.op("pe", "matmul", bkk.t[:, 128:256], lhsT=kT.t[:], rhs=qT.t[:], start=True, stop=True, reads=[kT, qT], writes=[bkk])
                        X0 = Xr[0][hl]
                        P.op("dve", "tensor_tensor", out=X0.t[:], in0=bkk.t[:, 0:128], in1=dL.t[:], op=ALU.mult, reads=[bkk, dL], writes=[X0])
                        atts = role["atts"][hl]
                        P.op("dve", "tensor_tensor", out=atts.t[:], in0=bkk.t[:, 128:256], in1=dT.t[:], op=ALU.mult, reads=[bkk, dT], writes=[atts])
                        yield
                        bkx = P.bank()
                        P.op("pe", "matmul", bkx.t[:, 0:128], lhsT=X0.t[:], rhs=identr.t[:].bitcast(F32R), start=True, stop=True, reads=[X0, identr], writes=[bkx])
                        P.op("act", "activation", out=XTr[0][hl].t[:], in_=bkx.t[:, 0:128], func=AF.Copy, reads=[bkx], writes=[XTr[0][hl]])
                        P.op("dve", "tensor_tensor", out=TTr[0][hl].t[:], in0=bkx.t[:, 0:128], in1=ident, op=ALU.add, reads=[bkx, cst], writes=[TTr[0][hl]])
                        yield

                    def chain2(g1, g2):
                        yield from g1
                        yield from g2

                    streams = [chain2(headABC(0, sring[0], sext[0]), headABC(1, sring[0], sext[0])),
                               chain2(headABC(2, sring[1], sext[1]), headABC(3, sring[1], sext[1]))]
                    if pendingG is not None:
                        streams.append(pendingG)
                        pendingG = None
                    while streams:
                        for g_ in list(streams):
                            try:
                                next(g_)
                            except StopIteration:
                                streams.remove(g_)
                    if GSTOP <= 3:
                        continue
                    nit = 2 if samp else 6
                    for k in range(1, nit + 1):
                        cur, nxt = (k - 1) % 2, k % 2
                        bkA = P.bank()
                        for hl in range(4):
                            P.op("pe", "matmul", bkA.t[:, hl * 128:(hl + 1) * 128], lhsT=XTr[cur][hl].t[:], rhs=Xr[cur][hl].t[:], start=True, stop=True,
                                 reads=[XTr[cur][hl], Xr[cur][hl]], writes=[bkA])
                        if k < nit:
                            bkB = P.bank()
                            for hl in range(4):
                                P.op("pe", "matmul", bkB.t[:, hl * 128:(hl + 1) * 128], lhsT=Xr[cur][hl].t[:], rhs=XTr[cur][hl].t[:], start=True, stop=True,
                                     reads=[XTr[cur][hl], Xr[cur][hl]], writes=[bkB])
                        for hl in range(4):
                            P.op("act", "activation", out=Xr[nxt][hl].t[:], in_=bkA.t[:, hl * 128:(hl + 1) * 128], func=AF.Copy,
                                 reads=[bkA], writes=[Xr[nxt][hl]])
                        if k < nit:
                            for hl in range(4):
                                P.op("dve", "tensor_copy", out=XTr[nxt][hl].t[:], in_=bkB.t[:, hl * 128:(hl + 1) * 128],
                                     reads=[bkB], writes=[XTr[nxt][hl]])
                        bkC = P.bank()
                        for hl in range(4):
                            P.op("pe", "matmul", bkC.t[:, hl * 128:(hl + 1) * 128], lhsT=Xr[nxt][hl].t[:], rhs=TTr[cur][hl].t[:], start=True, stop=True,
                                 reads=[Xr[nxt][hl], TTr[cur][hl]], writes=[bkC])
                        for hl in range(4):
                            P.op("dve", "tensor_tensor", out=TTr[nxt][hl].t[:], in0=TTr[cur][hl].t[:], in1=bkC.t[:, hl * 128:(hl + 1) * 128], op=ALU.add,
                                 reads=[TTr[cur][hl], bkC], writes=[TTr[nxt][hl]])
                    TT = TTr[0]
                    if GSTOP <= 4:
                        continue
                    rhsu = []
                    for hl in range(4):
                        h = grp * 4 + hl
                        tu = b16(w128.next())
                        tw = b16(w128.next())
                        P.op("dve", "tensor_scalar", out=tu.t[:], in0=TT[hl].t[:].bitcast(F32), scalar1=sc[:, BETA_ + h:BETA_ + h + 1], scalar2=None, op0=ALU.mult,
                             reads=[TT[hl], scal], writes=[tu])
                        P.op("dve", "tensor_scalar", out=tw.t[:], in0=TT[hl].t[:].bitcast(F32), scalar1=sc[:, BG_ + h:BG_ + h + 1], scalar2=None, op0=ALU.mult,
                             reads=[TT[hl], scal], writes=[tw])
                        rhsu.append(tu)
                        bkw = P.bank()
                        P.op("pe", "matmul", bkw.t[:, 0:128], lhsT=role["kTM"][hl].t[:], rhs=tw.t[:], start=True, stop=True, reads=[role["kTM"][hl], tw], writes=[bkw])
                        P.op("act", "activation", out=role["nwT"][hl].t[:], in_=bkw.t[:, 0:128], func=AF.Copy, scale=-1.0, reads=[bkw], writes=[role["nwT"][hl]])
                    bkv = P.lbank()
                    bko = P.lbank()
                    for hl in range(4):
                        h = grp * 4 + hl
                        for s in range(nreal):
                            sb_ = sbf_r.next()
                            P.op("act", "activation", out=sb_.t[:, 0:128], in_=S_ap(s, h * 128, (h + 1) * 128), func=AF.Copy, reads=[stb], writes=[sb_])
                            P.op("pe", "matmul", bkv.t[:, hl * 128 + s * L:hl * 128 + (s + 1) * L], lhsT=sb_.t[:, 0:128],
                                 rhs=role["nwT"][hl].t[:, s * L:(s + 1) * L], start=(s == 0), stop=False, skip_group_check=True,
                                 reads=[sb_, role["nwT"][hl]], writes=[bkv])
                            P.op("pe", "matmul", bko.t[:, hl * 128 + s * L:hl * 128 + (s + 1) * L], lhsT=sb_.t[:, 0:128],
                                 rhs=role["qdT"][hl].t[:, s * L:(s + 1) * L], start=(s == 0 and hl == 0), stop=False, skip_group_check=True,
                                 reads=[sb_, role["qdT"][hl]], writes=[bko])
                        P.op("pe", "matmul", bkv.t[:, hl * 128:(hl + 1) * 128], lhsT=role["vTM"][hl].t[:], rhs=rhsu[hl].t[:], start=False, stop=True,
                             skip_group_check=True, reads=[rhsu[hl], role["vTM"][hl]], writes=[bkv])
                    vts = []
                    for hl in range(4):
                        vt_ = b16(w128.next())
                        P.op("act", "activation", out=vt_.t[:], in_=bkv.t[:, hl * 128:(hl + 1) * 128], func=AF.Copy, reads=[bkv], writes=[vt_])
                        vts.append(vt_)
                    bkt = P.bank()
                    bktb = bkt.t[:].bitcast(BF16)
                    for hl in range(4):
                        P.op("pe", "transpose", bktb[:, hl * 128:(hl + 1) * 128], vts[hl].t[:], identb.t[:], reads=[vts[hl], identb], writes=[bkt])
                    for hl in range(4):
                        P.op("dve", "tensor_copy", out=role["vn"][hl].t[:], in_=bktb[:, hl * 128:(hl + 1) * 128], reads=[bkt], writes=[role["vn"][hl]])
                    if GSTOP <= 5:
                        continue
                    for hl in range(4):
                        h = grp * 4 + hl
                        P.op("pe", "matmul", bko.t[:, hl * 128:(hl + 1) * 128], lhsT=role["vn"][hl].t[:], rhs=role["atts"][hl].t[:], start=False, stop=True,
                             skip_group_check=True, reads=[role["vn"][hl], role["atts"][hl]], writes=[bko])
                    for hl in range(4):
                        h = grp * 4 + hl
                        for s in range(nreal):
                            kz = b16(w128.next())
                            if samp:
                                P.op("dve", "tensor_scalar", out=kz.t[:], in0=role["kTM"][hl].t[:], scalar1=sc[:, KSUF_ + h:KSUF_ + h + 1],
                                     scalar2=seqind[:, s:s + 1], op0=ALU.mult, op1=ALU.mult, reads=[role["kTM"][hl], scal, cst], writes=[kz])
                            else:
                                P.op("dve", "tensor_scalar", out=kz.t[:], in0=role["kTM"][hl].t[:], scalar1=sc[:, KSUF_ + h:KSUF_ + h + 1],
                                     scalar2=None, op0=ALU.mult, reads=[role["kTM"][hl], scal], writes=[kz])
                            bks = P.bank()
                            P.op("pe", "matmul", bks.t[:, 0:128], lhsT=kz.t[:], rhs=role["vn"][hl].t[:], start=True, stop=True, reads=[kz, role["vn"][hl]], writes=[bks])
                            P.op("dve", "scalar_tensor_tensor", out=S_ap(s, h * 128, (h + 1) * 128), in0=S_ap(s, h * 128, (h + 1) * 128),
                                 scalar=role["egcb"][hl].t[:, (s + 1) * L - 1:(s + 1) * L], in1=bks.t[:, 0:128], op0=ALU.mult, op1=ALU.add,
                                 reads=[stb, role["egcb"][hl], bks], writes=[stb])
                    if GSTOP <= 6:
                        continue
                    def stageG(grp, bko):
                        bk = P.bank()
                        proj_tm(xT, win, 3072 + grp * 512, 512, bk.t[:, :], bk)
                        zst = gstg.next()
                        P.op("act", "activation", out=zst.t[:], in_=bk.t[:, :], func=AF.Copy, reads=[bk], writes=[zst])
                        yield
                        for hl in range(4):
                            h = grp * 4 + hl
                            osq = b16(w128.next())
                            P.op("act", "activation", out=osq.t[:], in_=bko.t[:, hl * 128:(hl + 1) * 128], func=AF.Square, reads=[bko], writes=[osq])
                            yield
                            bkn = P.bank()
                            P.op("pe", "matmul", bkn.t[:, 0:128], lhsT=onesb.t[:], rhs=osq.t[:], start=True, stop=True, reads=[onesb, osq], writes=[bkn])
                            rs = w128.next()
                            P.op("act", "activation", out=rs.t[:], in_=bkn.t[:, 0:128], func=AF.Sqrt, scale=1.0 / 128, bias=epsb.t[:, 1:2], reads=[bkn, epsb], writes=[rs])
                            yield
                            P.op("dve", "reciprocal", out=rs.t[:], in_=rs.t[:], reads=[rs], writes=[rs])
                            bkz = P.bank()
                            bkzb = bkz.t[:].bitcast(BF16)
                            P.op("pe", "transpose", bkzb[:, 0:128], zst.t[:, hl * 128:(hl + 1) * 128], identb.t[:], reads=[zst, identb], writes=[bkz])
                            sz = w128.next()
                            P.op("act", "activation", out=sz.t[:], in_=bkzb[:, 0:128], func=AF.Silu, reads=[bkz], writes=[sz])
                            yield
                            t1 = w128.next()
                            P.op("dve", "scalar_tensor_tensor", out=t1.t[:], in0=bko.t[:, hl * 128:(hl + 1) * 128], scalar=gcolg.t[:, 0:1], in1=rs.t[:],
                                 op0=ALU.mult, op1=ALU.mult, reads=[bko, gcolg, rs], writes=[t1])
                            P.op("pool", "tensor_tensor", out=ogT.t[:, h, :], in0=t1.t[:], in1=sz.t[:], op=ALU.mult, reads=[t1, sz], writes=[ogT])
                            yield

                    if grp == 0:
                        pendingG = stageG(grp, bko)
                    else:
                        for _ in stageG(grp, bko):
                            pass
                if (samp or t == n_pt - 1) and GSTOP > 7:
                    for c in range(6):
                        bk = P.bank()
                        proj_tm(xT, win, c * 512, 512, bk.t[:, :], bk)
                        stg = cvs_r.next()
                        P.op("act", "activation", out=stg.t[:], in_=bk.t[:, :], func=AF.Copy, reads=[bk], writes=[stg])
                        if samp:
                            for s in range(8):
                                P.dma("sp", s_conv[j, g0 + s, :, c * 512:(c + 1) * 512], stg.t[s * 8 + 5:s * 8 + 8, :], reads=[stg])
                        else:
                            P.dma("sp", p_conv[j, :, c * 512:(c + 1) * 512], stg.t[125:128, :], reads=[stg])
                out_proj_ln(ogT, xs, layer, dst, t)
                if samp:
                    for s in range(8):
                        P.dma("sp", s_gdn[j, g0 + s].rearrange("h d v -> d h v"), sstate[:, s, :].rearrange("p (h v) -> p h v", h=8), reads=[big])
            P.dma("sp", p_gdn[j].rearrange("h d v -> d h v"), pstate.t[:].rearrange("p (h v) -> p h v", h=8), reads=[pstate])


        def s5_layer(j, layer, src, dst):
            P.barrier()
            win = win_s5
            wglu = TB(big.t[:, 0:8192].bitcast(BF16).rearrange("p (k e) -> p k e", k=8), big.b)
            tabf = winf.t[:, 16384:32768].bitcast(F32)
            ck = tabf[:, 0:4096].rearrange("p (c k) -> p c k", c=32)
            sk = tabf[:, 4096:8192].rearrange("p (c k) -> p c k", c=32)
            tab = Buf("tab")
            off = 0
            btw, off = P.carve(arena, off, [128, 4096], BF16, "btw")
            ctb, off = P.carve(arena, off, [128, 4096], BF16, "ctb")
            pv, off = P.carve(arena, off, [128, 14, 32], F32, "pv")
            gir, off = P.carve(arena, off, [128, 32], F32, "gir")
            gii, off = P.carve(arena, off, [128, 32], F32, "gii")
            dcol, off = P.carve(arena, off, [128, 8], F32, "dcol")
            bglu, off = P.carve(arena, off, [128, 16], F32, "bglu")
            ring_off = off
            uT = TB(pstate.t[:].rearrange("p (a b) -> p a b", a=8), pstate.b)
            uTb, off = P.carve(arena, off, [128, 8, 128], BF16, "uTb")
            szT, off = P.carve(arena, off, [128, 8, 128], BF16, "szT")
            gyT, off = P.carve(arena, off, [128, 8, 128], BF16, "gyT")
            hlr, off = P.carve(arena, off, [128, 32, 8], F32, "hlr")
            hli, off = P.carve(arena, off, [128, 32, 8], F32, "hli")
            adr, off = P.carve(arena, off, [128, 32, 8], F32, "adr")
            adi, off = P.carve(arena, off, [128, 32, 8], F32, "adi")
            r0t, off = P.carve(arena, off, [128, 128], F32, "r0t")
            sstg, off = carve_ring(off, 2, [128, 512], BF16, name="sstg")
            w512, off = carve_ring(off, 9, [128, 512], name="s512_")
            assert off <= ARENA, off
            so = ring_off
            tmr, so = carve_ring(so, 5, [128, 1024], name="stmp")
            ctw, so = P.carve(arena, so, [128, 2048], F32, "ctw")
            assert so <= ARENA, so
            BT = btw.t[:].rearrange("p (k r q c) -> p k r q c", k=8, r=2, q=2)
            CT = ctw.t[:].rearrange("p (c r m) -> p c r m", c=32, r=2)
            CTB = ctb.t[:].rearrange("p (a q r m) -> p a q r m", a=16, q=2, r=2)
            R_, COS_, SIN_, ARE_, AIM_, CRE_, CIM_, ICR_, ICI_, TH_, C128_, S128_ = range(12)

            load_w(win, 0, s5_w_in[j], 2048)
            load_w(wglu, 0, s5_w_glu[j], 2048)
            load_w(wout, 0, s5_w_out[j], D)
            load_ln(layer)
            P.dma("pool", btw.t[:], s5_bt, writes=[btw])
            P.op("pool", "memset", ctb.t[:], 0.0, writes=[ctb])
            P.dma("sp", ctw.t[:], s5_ct, writes=[ctw])
            P.dma("sp", dcol.t[:], s5_d[j].rearrange("g c -> (g c)").rearrange("(kb r) -> r kb", r=128), writes=[dcol], allow_slow_non_contiguous=True)
            P.dma("sp", bglu.t[:], s5_b_glu[j].rearrange("(blk r) -> r blk", r=128), writes=[bglu], allow_slow_non_contiguous=True)

            def sincos(u_ap, out_ap, tmpA, tmpI, rd, wr, shape_p):
                P.op("dve", "tensor_copy", out=tmpI, in_=u_ap, reads=rd, writes=[tmpI_tb])
                P.op("dve", "tensor_copy", out=tmpA, in_=tmpI, reads=[tmpI_tb], writes=[tmpA_tb])
                P.op("dve", "tensor_tensor", out=tmpA, in0=u_ap, in1=tmpA, op=ALU.subtract, reads=rd + [tmpA_tb], writes=[tmpA_tb])
                P.op("dve", "tensor_scalar", out=tmpI.bitcast(F32), in0=tmpA, scalar1=0.0, scalar2=None, op0=ALU.is_lt, reads=[tmpA_tb], writes=[tmpI_tb])
                P.op("dve", "tensor_tensor", out=tmpA, in0=tmpA, in1=tmpI.bitcast(F32), op=ALU.add, reads=[tmpA_tb, tmpI_tb], writes=[tmpA_tb])
                P.op("act", "activation", out=out_ap, in_=tmpA, func=AF.Sin, scale=TWO_PI, bias=epsb.t[0:shape_p, 3:4], reads=[tmpA_tb, epsb], writes=wr)

            T = [tmr.next() for _ in range(5)]
            tA, tB, tC, tD, tE = T
            tmpA_tb, tmpI_tb = tD, tE
            def tv(tb, i):
                return tb.t[0:32, i * 128:(i + 1) * 128]
            lamre, lamim, dtb, lr, th, rr, cs, sn = (tv(tA, i) for i in range(8))
            are, aim, num, den, cre, cim, icr, ici = (tv(tB, i) for i in range(8))
            u_s, u_c, t0_, t1_, thp = (tv(tC, i) for i in range(5))
            P.dma("sp", lamre, s5_lam_re[j].rearrange("(cb gl) p -> cb (gl p)", gl=2), writes=[tA])
            P.dma("sp", lamim, s5_lam_im[j].rearrange("(cb gl) p -> cb (gl p)", gl=2), writes=[tA])
            P.dma("sp", tC.t[0:32, 896:898], s5_log_dt[j].rearrange("(cb gl) -> cb gl", gl=2), writes=[tC])
            P.op("act", "activation", out=tC.t[0:32, 896:898], in_=tC.t[0:32, 896:898], func=AF.Exp, reads=[tC], writes=[tC])
            for gl in range(2):
                P.op("dve", "tensor_scalar", out=tA.t[0:32, 256 + gl * 64:256 + (gl + 1) * 64], in0=ones[0:32, 0:64],
                     scalar1=tC.t[0:32, 896 + gl:897 + gl], scalar2=None, op0=ALU.mult, reads=[cst, tC], writes=[tA])
            def tt(o, a, b, op, R, W):
                P.op("dve", "tensor_tensor", out=o, in0=a, in1=b, op=op, reads=R, writes=W)
            tt(lr, lamre, dtb, ALU.mult, [tA], [tA])
            tt(th, lamim, dtb, ALU.mult, [tA], [tA])
            P.op("act", "activation", out=rr, in_=lr, func=AF.Exp, reads=[tA], writes=[tA])
            P.op("dve", "tensor_scalar", out=thp, in0=th, scalar1=1.0 / TWO_PI, scalar2=None, op0=ALU.mult, reads=[tA], writes=[tC])
            P.op("dve", "tensor_scalar", out=u_s, in0=thp, scalar1=8.5, scalar2=None, op0=ALU.add, reads=[tC], writes=[tC])
            P.op("dve", "tensor_scalar", out=u_c, in0=thp, scalar1=8.75, scalar2=None, op0=ALU.add, reads=[tC], writes=[tC])
            sincos(u_s, sn, tD.t[0:32, 0:128], tE.t[0:32, 0:128].bitcast(I32), [tC], [tA], 32)
            sincos(u_c, cs, tD.t[0:32, 0:128], tE.t[0:32, 0:128].bitcast(I32), [tC], [tA], 32)
            tt(are, rr, cs, ALU.mult, [tA], [tB])
            tt(aim, rr, sn, ALU.mult, [tA], [tB])
            P.op("dve", "tensor_scalar", out=num, in0=are, scalar1=-1.0, scalar2=None, op0=ALU.add, reads=[tB], writes=[tB])
            tt(den, lamre, lamre, ALU.mult, [tA], [tB])
            tt(t0_, lamim, lamim, ALU.mult, [tA], [tC])
            tt(den, den, t0_, ALU.add, [tB, tC], [tB])
            P.op("dve", "reciprocal", out=den, in_=den, reads=[tB], writes=[tB])
            tt(cre, num, lamre, ALU.mult, [tA, tB], [tB])
            tt(t0_, aim, lamim, ALU.mult, [tA, tB], [tC])
            tt(cre, cre, t0_, ALU.add, [tB, tC], [tB])
            tt(cre, cre, den, ALU.mult, [tB], [tB])
            tt(cim, aim, lamre, ALU.mult, [tA, tB], [tB])
            tt(t0_, num, lamim, ALU.mult, [tA, tB], [tC])
            tt(cim, cim, t0_, ALU.subtract, [tB, tC], [tB])
            tt(cim, cim, den, ALU.mult, [tB], [tB])
            tt(t0_, cre, cre, ALU.mult, [tB], [tC])
            tt(t1_, cim, cim, ALU.mult, [tB], [tC])
            tt(t0_, t0_, t1_, ALU.add, [tC], [tC])
            P.op("dve", "reciprocal", out=t0_, in_=t0_, reads=[tC], writes=[tC])
            tt(icr, cre, t0_, ALU.mult, [tB, tC], [tB])
            P.op("dve", "scalar_tensor_tensor", out=ici, in0=cim, scalar=-1.0, in1=t0_, op0=ALU.mult, op1=ALU.mult, reads=[tB, tC], writes=[tB])
            srcs = [(rr, tA), (cs, tA), (sn, tA), (are, tB), (aim, tB), (cre, tB), (cim, tB), (icr, tB), (ici, tB), (thp, tC)]
            for i, (ap_, tb_) in enumerate(srcs):
                bk = P.bank()
                P.op("pe", "matmul", bk.t[:, 0:32], lhsT=ap_, rhs=ident[0:32, 0:32], start=True, stop=True, reads=[tb_, cst], writes=[bk])
                P.op("act", "activation", out=pv.t[:, i, :], in_=bk.t[:, 0:32], func=AF.Copy, reads=[bk], writes=[pv])
            for ch in range(4):
                c0 = ch * 8
                uu = tA.t[:].rearrange("p (c k) -> p c k", c=8)
                P.op("dve", "tensor_tensor", out=uu, in0=kkrow[:, None, :].broadcast_to([128, 8, 128]),
                     in1=pv.t[:, TH_, c0:c0 + 8][:, :, None].broadcast_to([128, 8, 128]), op=ALU.mult, reads=[cst, pv], writes=[tA])
                P.op("dve", "tensor_scalar", out=tB.t[:], in0=tA.t[:], scalar1=8.5, scalar2=None, op0=ALU.add, reads=[tA], writes=[tB])
                P.op("dve", "tensor_scalar", out=tC.t[:], in0=tA.t[:], scalar1=8.75, scalar2=None, op0=ALU.add, reads=[tA], writes=[tC])
                sincos(tB.t[:], sk[:, c0:c0 + 8, :].rearrange("p c k -> p (c k)"), tD.t[:], tE.t[:].bitcast(I32), [tB], [tab], 128)
                sincos(tC.t[:], ck[:, c0:c0 + 8, :].rearrange("p c k -> p (c k)"), tD.t[:], tE.t[:].bitcast(I32), [tC], [tab], 128)
            pvv = pv.t
            x1 = tA.t[:, 0:32]; x2 = tA.t[:, 32:64]
            tt(x1, ck[:, :, 127], pvv[:, COS_, :], ALU.mult, [tab, pv], [tA])
            tt(x2, sk[:, :, 127], pvv[:, SIN_, :], ALU.mult, [tab, pv], [tA])
            tt(pvv[:, C128_, :], x1, x2, ALU.subtract, [tA], [pv])
            tt(x1, sk[:, :, 127], pvv[:, COS_, :], ALU.mult, [tab, pv], [tA])
            tt(x2, ck[:, :, 127], pvv[:, SIN_, :], ALU.mult, [tab, pv], [tA])
            tt(pvv[:, S128_, :], x1, x2, ALU.add, [tA], [pv])
            cf = lambda i: pvv[:, i, :][:, :, None].broadcast_to([128, 32, 32])
            c_re = CT[:, :, 0, :]
            c_im = CT[:, :, 1, :]
            y1 = tB.t[:].rearrange("p (c m) -> p c m", c=32)
            y2 = tC.t[:].rearrange("p (c m) -> p c m", c=32)
            y3 = tD.t[:].rearrange("p (c m) -> p c m", c=32)
            tt(y1, c_re, cf(CRE_), ALU.mult, [ctw, pv], [tB])
            tt(y2, c_im, cf(CIM_), ALU.mult, [ctw, pv], [tC])
            tt(y1, y1, y2, ALU.subtract, [tB, tC], [tB])
            tt(y2, c_re, cf(CIM_), ALU.mult, [ctw, pv], [tC])
            tt(y3, c_im, cf(CRE_), ALU.mult, [ctw, pv], [tD])
            tt(y2, y2, y3, ALU.add, [tC, tD], [tC])
            y1p = tB.t[:].rearrange("p (a q m) -> p a q m", a=16, q=2)
            y2p = tC.t[:].rearrange("p (a q m) -> p a q m", a=16, q=2)
            for qq in range(2):
                P.op("dve", "tensor_copy", out=CTB[:, :, qq, 0, qq * 32:(qq + 1) * 32], in_=y1p[:, :, qq, :], reads=[tB], writes=[ctb])
                P.op("dve", "tensor_scalar", out=CTB[:, :, qq, 1, qq * 32:(qq + 1) * 32], in0=y2p[:, :, qq, :], scalar1=-1.0, scalar2=None, op0=ALU.mult,
                     reads=[tC], writes=[ctb])
            P.op("pool", "memset", gir.t[:], 0.0, writes=[gir])
            P.op("pool", "memset", gii.t[:], 0.0, writes=[gii])
            P.barrier()

            def cmul(o_re, o_im, a_re, a_im, b_re, b_im, R, t1, t2, eng="dve"):
                pass

            order = ([n_pt, n_pt + 1] if has_sample else []) + list(range(n_pt))
            for t in order:
                samp = (t >= n_pt)
                nseq = 16 if samp else 1
                nreal = 8 if samp else 1
                L = 128 // nseq
                g0 = (t - n_pt) * 8
                xs = load_x(src, t)
                xT = make_xT(xs)

                def tabv(tb_ap, kb):
                    if samp:
                        return tb_ap[:, 4 * kb:4 * kb + 4, 0:8][:, :, None, :].broadcast_to([128, 4, 16, 8])
                    return tb_ap[:, 4 * kb:4 * kb + 4, :]

                def gv(tb):
                    if samp:
                        return tb.t[:].rearrange("p (c s l) -> p c s l", c=4, s=16)
                    return tb.t[:].rearrange("p (c k) -> p c k", c=4)

                if samp:
                    for (srcst, dstt) in ((st_re, hlr), (st_im, hli)):
                        for c4 in range(8):
                            stg = w512.next()
                            P.dma("sp", stg.t[0:8, :], srcst[j, g0:g0 + 8].rearrange("s g p -> s (g p)")[:, c4 * 512:(c4 + 1) * 512], writes=[stg])
                            bk = P.bank()
                            for q in range(4):
                                P.op("pe", "matmul", bk.t[:, q * 8:(q + 1) * 8], lhsT=stg.t[0:8, q * 128:(q + 1) * 128], rhs=ident[0:8, 0:8],
                                     start=True, stop=True, reads=[stg, cst], writes=[bk])
                            P.op("act", "activation", out=dstt.t[:, c4 * 4:(c4 + 1) * 4, :],
                                 in_=bk.t[:, 0:32].rearrange("p (q s) -> p q s", q=4), func=AF.Copy, reads=[bk], writes=[dstt])
                    bc = lambda i: pvv[:, i, :][:, :, None].broadcast_to([128, 32, 8])
                    w1 = w512.next(); w2 = w512.next(); w3 = w512.next(); w4 = w512.next()
                    v1 = w1.t[:, 0:256].rearrange("p (c s) -> p c s", c=32)
                    v2 = w2.t[:, 0:256].rearrange("p (c s) -> p c s", c=32)
                    v3 = w3.t[:, 0:256].rearrange("p (c s) -> p c s", c=32)
                    v4 = w4.t[:, 0:256].rearrange("p (c s) -> p c s", c=32)
                    tt(v1, hlr.t[:], bc(ICR_), ALU.mult, [hlr, pv], [w1])
                    tt(v2, hli.t[:], bc(ICI_), ALU.mult, [hli, pv], [w2])
                    tt(v3, v1, v2, ALU.subtract, [w1, w2], [w3])
                    tt(v1, hlr.t[:], bc(ICI_), ALU.mult, [hlr, pv], [w1])
                    tt(v2, hli.t[:], bc(ICR_), ALU.mult, [hli, pv], [w2])
                    tt(v4, v1, v2, ALU.add, [w1, w2], [w4])
                    tt(v1, v3, bc(ARE_), ALU.mult, [w3, pv], [w1])
                    tt(v2, v4, bc(AIM_), ALU.mult, [w4, pv], [w2])
                    tt(adr.t[:], v1, v2, ALU.subtract, [w1, w2], [adr])
                    tt(v1, v3, bc(AIM_), ALU.mult, [w3, pv], [w1])
                    tt(v2, v4, bc(ARE_), ALU.mult, [w4, pv], [w2])
                    tt(adi.t[:], v1, v2, ALU.add, [w1, w2], [adi])

                for part in range(2):
                    bkT = P.bank()
                    bkTb = bkT.t[:].bitcast(BF16)
                    for half in range(2):
                        bk = P.bank()
                        proj_tm(xT, win, part * 1024 + half * 512, 512, bk.t[:, :], bk)
                        st_ = sstg.next()
                        if half == 0:
                            P.op("act", "activation", out=st_.t[:], in_=bk.t[:, :], func=AF.Copy, reads=[bk], writes=[st_])
                        else:
                            P.op("dve", "tensor_copy", out=st_.t[:], in_=bk.t[:, :], reads=[bk], writes=[st_])
                        for q in range(4):
                            blk = half * 4 + q
                            P.op("pe", "transpose", bkTb[:, blk * 128:(blk + 1) * 128], st_.t[:, q * 128:(q + 1) * 128], identb.t[:],
                                 reads=[st_, identb], writes=[bkT])
                    src3 = bkTb.rearrange("p (a k) -> p a k", a=8)
                    if part == 0:
                        P.op("act", "activation", out=uT.t[:], in_=src3, func=AF.Copy, reads=[bkT], writes=[uT])
                        P.op("dve", "tensor_copy", out=uTb.t[:], in_=src3, reads=[bkT], writes=[uTb])
                    else:
                        P.op("act", "activation", out=szT.t[:], in_=src3, func=AF.Silu, reads=[bkT], writes=[szT])

                for kb in range(8):
                    bkr = P.bank()
                    bki = P.bank()
                    for ri, bkx in ((0, bkr), (1, bki)):
                        for q in range(4):
                            hh, qq = q // 2, q % 2
                            P.op("pe", "matmul", bkx.t[:, q * 128:(q + 1) * 128], lhsT=BT[hh * 64:(hh + 1) * 64, kb, ri, qq, :],
                                 rhs=uTb.t[hh * 64:(hh + 1) * 64, kb, :], start=True, stop=True, reads=[btw, uTb], writes=[bkx])
                    CK = tabv(ck, kb)
                    SK = tabv(sk, kb)
                    a1 = w512.next(); a2 = w512.next(); rre = w512.next(); rim = w512.next()
                    pr = gv(bkr) if False else (bkr.t[:, :].rearrange("p (c s l) -> p c s l", c=4, s=16) if samp else bkr.t[:, :].rearrange("p (c k) -> p c k", c=4))
                    pi_ = (bki.t[:, :].rearrange("p (c s l) -> p c s l", c=4, s=16) if samp else bki.t[:, :].rearrange("p (c k) -> p c k", c=4))
                    tt(gv(a1), pr, CK, ALU.mult, [bkr, tab], [a1])
                    tt(gv(a2), pi_, SK, ALU.mult, [bki, tab], [a2])
                    P.op("pool", "tensor_tensor", out=rre.t[:], in0=a1.t[:], in1=a2.t[:], op=ALU.add, reads=[a1, a2], writes=[rre])
                    a3 = w512.next(); a4 = w512.next()
                    tt(gv(a3), pi_, CK, ALU.mult, [bki, tab], [a3])
                    tt(gv(a4), pr, SK, ALU.mult, [bkr, tab], [a4])
                    P.op("pool", "tensor_tensor", out=rim.t[:], in0=a3.t[:], in1=a4.t[:], op=ALU.subtract, reads=[a3, a4], writes=[rim])
                    if samp:
                        rv = rre.t[:].rearrange("p (c s l) -> p c s l", c=4, s=16)
                        iv = rim.t[:].rearrange("p (c s l) -> p c s l", c=4, s=16)
                        tt(rv[:, :, 0:8, 0], rv[:, :, 0:8, 0], adr.t[:, 4 * kb:4 * kb + 4, :], ALU.add, [rre, adr], [rre])
                        tt(iv[:, :, 0:8, 0], iv[:, :, 0:8, 0], adi.t[:, 4 * kb:4 * kb + 4, :], ALU.add, [rim, adi], [rim])
                    gre = w512.next(); gim = w512.next()
                    for q in range(4):
                        cb_ = 4 * kb + q
                        if samp:
                            P.op("dve", "tensor_scalar", out=r0t.t[:], in0=notfirst, scalar1=pvv[:, R_, cb_:cb_ + 1], scalar2=None, op0=ALU.mult,
                                 reads=[cst, pv], writes=[r0t])
                            d0 = r0t.t[:]
                            rd0 = [r0t]
                            ini_r = 0.0
                            ini_i = 0.0
                            rdi = []
                        else:
                            d0 = pvv[:, R_, cb_:cb_ + 1].broadcast_to([128, 128])
                            rd0 = [pv]
                            ini_r = gir.t[:, cb_:cb_ + 1]
                            ini_i = gii.t[:, cb_:cb_ + 1]
                            rdi = [gir, gii]
                        P.op("dve", "tensor_tensor_scan", out=gre.t[:, q * 128:(q + 1) * 128], data0=d0, data1=rre.t[:, q * 128:(q + 1) * 128],
                             initial=ini_r, op0=ALU.mult, op1=ALU.add, reads=rd0 + [rre] + rdi, writes=[gre])
                        P.op("dve", "tensor_tensor_scan", out=gim.t[:, q * 128:(q + 1) * 128], data0=d0, data1=rim.t[:, q * 128:(q + 1) * 128],
                             initial=ini_i, op0=ALU.mult, op1=ALU.add, reads=rd0 + [rim] + rdi, writes=[gim])
                    if not samp:
                        gl_r = gre.t[:].rearrange("p (c k) -> p c k", c=4)[:, :, 127]
                        gl_i = gim.t[:].rearrange("p (c k) -> p c k", c=4)[:, :, 127]
                        c128 = pvv[:, C128_, 4 * kb:4 * kb + 4]
                        s128 = pvv[:, S128_, 4 * kb:4 * kb + 4]
                        x1 = a1.t[:, 0:4]; x2 = a2.t[:, 0:4]
                        tt(x1, gl_r, c128, ALU.mult, [gre, pv], [a1])
                        tt(x2, gl_i, s128, ALU.mult, [gim, pv], [a2])
                        tt(gir.t[:, 4 * kb:4 * kb + 4], x1, x2, ALU.subtract, [a1, a2], [gir])
                        tt(x1, gl_r, s128, ALU.mult, [gre, pv], [a1])
                        tt(x2, gl_i, c128, ALU.mult, [gim, pv], [a2])
                        tt(gii.t[:, 4 * kb:4 * kb + 4], x1, x2, ALU.add, [a1, a2], [gii])
                    hre = w512.next(); him = w512.next()
                    b1 = w512.next(); b2 = w512.next()
                    hreb = hre.t[:].bitcast(BF16)[:, 0:512]
                    himb = him.t[:].bitcast(BF16)[:, 0:512]
                    tt(gv(b1), gv(gre), CK, ALU.mult, [gre, tab], [b1])
                    P.op("pool", "tensor_tensor", out=gv(b2), in0=gv(gim), in1=SK, op=ALU.mult, reads=[gim, tab], writes=[b2])
                    tt(hreb, b1.t[:], b2.t[:], ALU.subtract, [b1, b2], [hre])
                    b3 = w512.next(); b4 = w512.next()
                    tt(gv(b3), gv(gre), SK, ALU.mult, [gre, tab], [b3])
                    P.op("pool", "tensor_tensor", out=gv(b4), in0=gv(gim), in1=CK, op=ALU.mult, reads=[gim, tab], writes=[b4])
                    tt(himb, b3.t[:], b4.t[:], ALU.add, [b3, b4], [him])
                    if samp or t == n_pt - 1:
                        ns = 8 if samp else 1
                        if samp:
                            glr = gre.t[:].rearrange("p (c s l) -> p c s l", c=4, s=16)[:, :, 0:8, 7]
                            gli = gim.t[:].rearrange("p (c s l) -> p c s l", c=4, s=16)[:, :, 0:8, 7]
                            cl = ck[:, 4 * kb:4 * kb + 4, 7:8].broadcast_to([128, 4, 8])
                            sl_ = sk[:, 4 * kb:4 * kb + 4, 7:8].broadcast_to([128, 4, 8])
                        else:
                            glr = gre.t[:].rearrange("p (c k) -> p c k", c=4)[:, :, 127:128]
                            gli = gim.t[:].rearrange("p (c k) -> p c k", c=4)[:, :, 127:128]
                            cl = ck[:, 4 * kb:4 * kb + 4, 127:128]
                            sl_ = sk[:, 4 * kb:4 * kb + 4, 127:128]
                        x1 = b1.t[:, 0:4 * ns].rearrange("p (c s) -> p c s", c=4)
                        x2 = b2.t[:, 0:4 * ns].rearrange("p (c s) -> p c s", c=4)
                        tt(x1, glr, cl, ALU.mult, [gre, tab], [b1])
                        tt(x2, gli, sl_, ALU.mult, [gim, tab], [b2])
                        tt(hlr.t[:, 4 * kb:4 * kb + 4, 0:ns], x1, x2, ALU.subtract, [b1, b2], [hlr])
                        tt(x1, glr, sl_, ALU.mult, [gre, tab], [b1])
                        tt(x2, gli, cl, ALU.mult, [gim, tab], [b2])
                        tt(hli.t[:, 4 * kb:4 * kb + 4, 0:ns], x1, x2, ALU.add, [b1, b2], [hli])
                    bky = P.bank()
                    for hh in range(2):
                        n_ = 0
                        for qq in range(2):
                            q = hh * 2 + qq
                            for ri, hb, htb in ((0, hreb, hre), (1, himb, him)):
                                P.op("pe", "matmul", bky.t[hh * 64:(hh + 1) * 64, 0:128], lhsT=CTB[:, 2 * kb + hh, qq, ri, :],
                                     rhs=hb[:, q * 128:(q + 1) * 128], start=(n_ == 0), stop=(n_ == 3), reads=[ctb, htb], writes=[bky])
                                n_ += 1
                    yt = w512.next()
                    P.op("dve", "scalar_tensor_tensor", out=yt.t[:, 0:128], in0=uT.t[:, kb, :], scalar=dcol.t[:, kb:kb + 1], in1=bky.t[:, 0:128],
                         op0=ALU.mult, op1=ALU.add, reads=[uT, dcol, bky], writes=[yt])
                    P.op("act", "activation", out=gyT.t[:, kb, :], in_=yt.t[:, 0:128], func=AF.Gelu, reads=[yt], writes=[gyT])
                ogT = ogT_r.next()
                for blk in range(8):
                    bk1 = P.bank()
                    for kc in range(8):
                        P.op("pe", "matmul", bk1.t[:, 0:128], lhsT=wglu.t[:, kc, blk * 128:(blk + 1) * 128], rhs=gyT.t[:, kc, :],
                             start=(kc == 0), stop=(kc == 7), reads=[wglu, gyT], writes=[bk1])
                    for kc in range(8):
                        P.op("pe", "matmul", bk1.t[:, 128:256], lhsT=wglu.t[:, kc, 1024 + blk * 128:1024 + (blk + 1) * 128], rhs=gyT.t[:, kc, :],
                             start=(kc == 0), stop=(kc == 7), reads=[wglu, gyT], writes=[bk1])
                    s2 = w512.next()
                    P.op("act", "activation", out=s2.t[:, 0:128], in_=bk1.t[:, 128:256], func=AF.Sigmoid, bias=bglu.t[:, 8 + blk:9 + blk],
                         reads=[bk1, bglu], writes=[s2])
                    P.op("dve", "scalar_tensor_tensor", out=s2.t[:, 128:256], in0=bk1.t[:, 0:128], scalar=bglu.t[:, blk:blk + 1], in1=s2.t[:, 0:128],
                         op0=ALU.add, op1=ALU.mult, reads=[bk1, bglu, s2], writes=[s2])
                    P.op("pool", "tensor_tensor", out=ogT.t[:, blk, :], in0=s2.t[:, 128:256], in1=szT.t[:, blk, :], op=ALU.mult,
                         reads=[s2, szT], writes=[ogT])
                out_proj_ln(ogT, xs, layer, dst, t)
                if samp or t == n_pt - 1:
                    ns = 8 if samp else 1
                    bc = lambda i: pvv[:, i, :][:, :, None].broadcast_to([128, 32, ns])
                    w1 = w512.next(); w2 = w512.next(); w3 = w512.next(); w4 = w512.next()
                    v1 = w1.t[:, 0:32 * ns].rearrange("p (c s) -> p c s", c=32)
                    v2 = w2.t[:, 0:32 * ns].rearrange("p (c s) -> p c s", c=32)
                    v3 = w3.t[:, 0:32 * ns].rearrange("p (c s) -> p c s", c=32)
                    v4 = w4.t[:, 0:32 * ns].rearrange("p (c s) -> p c s", c=32)
                    hr_ = hlr.t[:, :, 0:ns]
                    hi_ = hli.t[:, :, 0:ns]
                    tt(v1, hr_, bc(CRE_), ALU.mult, [hlr, pv], [w1])
                    tt(v2, hi_, bc(CIM_), ALU.mult, [hli, pv], [w2])
                    v3 = adr.t[:, :, 0:ns]
                    v4 = adi.t[:, :, 0:ns]
                    tt(v3, v1, v2, ALU.subtract, [w1, w2], [adr])
                    tt(v1, hr_, bc(CIM_), ALU.mult, [hlr, pv], [w1])
                    tt(v2, hi_, bc(CRE_), ALU.mult, [hli, pv], [w2])
                    tt(v4, v1, v2, ALU.add, [w1, w2], [adi])
                    for (vv, wv, dsts, dstp) in ((v3, adr, s_re, p_re), (v4, adi, s_im, p_im)):
                        for c4 in range(8):
                            bk = P.bank()
                            for q in range(4):
                                P.op("pe", "matmul", bk.t[0:ns, q * 128:(q + 1) * 128], lhsT=vv[:, c4 * 4 + q, :], rhs=ident, start=True, stop=True, reads=[wv, cst], writes=[bk])
                            stg = w512.next()
                            P.op("act", "activation", out=stg.t[0:ns, :], in_=bk.t[0:ns, :], func=AF.Copy, reads=[bk], writes=[stg])
                            if samp:
                                P.dma("sp", dsts[j, g0:g0 + 8].rearrange("s g p -> s (g p)")[:, c4 * 512:(c4 + 1) * 512], stg.t[0:8, :], reads=[stg])
                            else:
                                P.dma("sp", dstp[j].rearrange("g p -> (g p)").rearrange("(o f) -> o f", o=1)[:, c4 * 512:(c4 + 1) * 512], stg.t[0:1, :], reads=[stg])

        cur = x_in
        li = 0
        for layer in layers:
            last = (layer == layers[-1])
            dst = y_out if last else xscr[li % 2]
            kind = layer % 3
            j = layer // 3
            if kind == 0:
                gla_layer(j, layer, cur, dst)
            elif kind == 1:
                gdn_layer(j, layer, cur, dst)
            else:
                s5_layer(j, layer, cur, dst)
            cur = dst
            li += 1
        P.finish()
        P.emit()
    return nc


def make_consts():
    c = np.zeros((128, CW), np.float32)
    p = np.arange(128)[:, None]
    f = np.arange(128)[None, :]
    c[:, 0:128] = (p == f)
    c[:, 128:256] = 1.0
    c[:, 256:384] = (p <= f)
    c[:, 384:512] = (f < p)
    same = (p // 8) == (f // 8)
    c[:, 512:640] = (p <= f) & same
    c[:, 640:768] = (f < p) & same
    c[:, 768:784] = (np.arange(128)[:, None] // 8) == np.arange(16)[None, :]
    c[:, 784:912] = np.arange(128)[None, :]
    c[:, 912:1040] = (np.arange(128)[None, :] % 8) != 0
    return c


def core_inputs(inp, core, n_pt=16, has_sample=True):
    sl = slice(16 * core, 16 * core + 16)
    xp = np.asarray(inp["x_prompt"][core, :n_pt * 128], np.float32)
    parts = [xp]
    if has_sample:
        xsm = np.asarray(inp["x_sample"][sl], np.float32).reshape(2, 64, D)
        zpad = np.zeros((64, D), np.float32)
        parts += [xsm[0], zpad, xsm[1], zpad]
    m = {
        "x_in": np.ascontiguousarray(np.concatenate(parts, 0)),
        "consts": make_consts(),
        "ln_g": inp["ln_g"], "ln_b": inp["ln_b"],
        "gla_w_in": inp["gla_w_in"], "gla_w_a2": inp["gla_w_a2"], "gla_b_a": inp["gla_b_a"],
        "gla_norm_g": inp["gla_norm_g"], "gla_w_out": inp["gla_w_out"],
        "st_gla": np.ascontiguousarray(inp["state_gla"][:, sl]),
        "gdn_w_in": inp["gdn_w_in"], "gdn_w_conv": inp["gdn_w_conv"], "gdn_a_log": inp["gdn_a_log"],
        "gdn_dt_bias": inp["gdn_dt_bias"], "gdn_norm_g": inp["gdn_norm_g"], "gdn_w_out": inp["gdn_w_out"],
        "s5_w_in": inp["s5_w_in"], "s5_lam_re": inp["s5_lam_re"], "s5_lam_im": inp["s5_lam_im"], "s5_log_dt": inp["s5_log_dt"],
        "s5_bt": s5_bt_layout(inp["s5_b_re"][0], inp["s5_b_im"][0]), "s5_ct": s5_ct_layout(inp["s5_c_re"][0], inp["s5_c_im"][0]),
        "s5_d": inp["s5_d"], "s5_w_glu": inp["s5_w_glu"], "s5_b_glu": inp["s5_b_glu"], "s5_w_out": inp["s5_w_out"],
        "st_re": np.ascontiguousarray(inp["state_s5_re"][:, sl]), "st_im": np.ascontiguousarray(inp["state_s5_im"][:, sl]),
        "st_gdn": np.ascontiguousarray(inp["state_gdn"][:, sl]), "st_conv": np.ascontiguousarray(inp["state_gdn_conv"][:, sl]),
    }
    return {k: np.ascontiguousarray(np.asarray(v, np.float32)) for k, v in m.items()}


def s5_bt_layout(b_re, b_im):
    out = np.zeros((128, 8, 2, 2, 128), np.float32)
    for ri, b in enumerate((np.asarray(b_re, np.float32), np.asarray(b_im, np.float32))):
        bb = b.reshape(8, 4, 2, 64, 16)
        for q in range(4):
            for gl in range(2):
                r0 = q * 32 + gl * 16
                out[r0:r0 + 16, :, ri, q % 2, gl * 64:(gl + 1) * 64] = bb[:, q, gl].transpose(2, 0, 1)
    return out.reshape(128, 8 * 2 * 2 * 128)


def s5_ct_layout(c_re, c_im):
    out = np.zeros((128, 32, 2, 32), np.float32)
    for ri, c in enumerate((np.asarray(c_re, np.float32), np.asarray(c_im, np.float32))):
        cc = c.reshape(32, 2, 16, 64)
        for gl in range(2):
            out[gl * 64:(gl + 1) * 64, :, ri, gl * 16:(gl + 1) * 16] = cc[:, gl].transpose(2, 0, 1)
    return out.reshape(128, 32 * 2 * 32)


_NC_CACHE = {}


def kernel(**inputs):
    inp = {k: np.asarray(v) for k, v in inputs.items()}
    if "full" not in _NC_CACHE:
        _NC_CACHE["full"] = build(n_pt=16, layers=(0, 1, 2, 3), has_sample=True)
    nc = _NC_CACHE["full"]
    in_maps = [core_inputs(inp, c, 16) for c in range(NCORES)]
    res = run_bass_kernel_spmd(nc, in_maps, core_ids=list(range(NCORES))).results
    f32 = lambda a: np.ascontiguousarray(np.asarray(a, dtype=np.float32))
    LP = 2048
    y_prompt = np.stack([f32(r["y_out"])[:LP] for r in res], 0)
    y_sample = np.concatenate([np.concatenate([f32(r["y_out"])[LP:LP + 64], f32(r["y_out"])[LP + 128:LP + 192]], 0).reshape(16, 8, D)
                               for r in res], 0)
    catb = lambda k: np.stack([f32(r[k]) for r in res], 1)
    cats = lambda k: np.concatenate([f32(r[k]) for r in res], 1)
    return (y_prompt, y_sample,
            catb("p_gla"), catb("p_gdn"), catb("p_conv"), catb("p_re"), catb("p_im"),
            cats("s_gla"), cats("s_gdn"), cats("s_conv"), cats("s_re"), cats("s_im"))
```

```python
import math
from contextlib import ExitStack
import numpy as np
import concourse.bass as bass
import concourse.mybir as mybir
from concourse.bass_utils import run_bass_kernel_spmd

F32 = mybir.dt.float32
BF16 = mybir.dt.bfloat16
I32 = mybir.dt.int32
F32R = mybir.dt.float32r
AF = mybir.ActivationFunctionType
ALU = mybir.AluOpType

D = 1024
DEPTH = 4
ALPHA = (2 * DEPTH) ** 0.25
LN_EPS = 1e-5
NORM_EPS = 1e-6
NCORES = 8
GLA_IN = 3088
GDN_IN = 4112
TWO_PI = 2.0 * math.pi
CW = 6 * 128 + 16 + 256


class Buf:
    __slots__ = ("name", "w", "r", "ex", "ws")

    def __init__(self, name, ex=False):
        self.name = name
        self.w = None
        self.ws = []
        self.r = []
        self.ex = ex


class TB:
    __slots__ = ("t", "b")

    def __init__(self, t, b):
        self.t = t
        self.b = b


class Ring:
    def __init__(self, items):
        self.items = items
        self.i = 0

    def next(self):
        it = self.items[self.i % len(self.items)]
        self.i += 1
        return it


ENGS = ("pe", "act", "dve", "pool", "sp")


class Prog:
    def __init__(self, nc, es):
        self.nc = nc
        self.es = es
        self.ops = {e: [] for e in ENGS}
        self.semobjs = []
        self.eng_sem = {}
        self.cnt = {}
        self.seen = {e: {} for e in ENGS}
        for e in ("pe", "act", "dve", "pool"):
            self.eng_sem[e] = self._newsem("c_" + e)
            self.cnt[e] = 0
        self.dslots = {}
        for q, k in (("sp", 16), ("pool", 8)):
            self.dslots[q] = [[self._newsem(f"d_{q}{i}"), 0] for i in range(k)]
        self.dcur = {"sp": 0, "pool": 0}
        allb = [TB(es.enter_context(nc.psum_tensor(f"psb{i}", [128, 512], F32)), Buf(f"psb{i}", ex=True)) for i in range(8)]
        self.banks = Ring(allb[0:5])
        self.lbanks = Ring(allb[5:8])

    def _newsem(self, name):
        s = self.es.enter_context(self.nc.semaphore(name))
        self.semobjs.append(s)
        return len(self.semobjs) - 1

    def sb(self, name, shape, dtype):
        t = self.es.enter_context(self.nc.sbuf_tensor(name, list(shape), dtype))
        return TB(t, Buf(name))

    def ring(self, name, shape, dtype, n):
        return Ring([self.sb(f"{name}{i}", shape, dtype) for i in range(n)])

    def bank(self):
        return self.banks.next()

    def lbank(self):
        return self.lbanks.next()

    def _deps(self, reads, writes, own=None, nowaw=False):
        deps = {}
        for tb in list(reads) + ([] if nowaw else list(writes)):
            b = tb.b if isinstance(tb, TB) else tb
            for (s, v) in b.ws:
                if deps.get(s, 0) < v:
                    deps[s] = v
        for tb in reads:
            b = tb.b if isinstance(tb, TB) else tb
            if b.w is not None:
                s, v = b.w
                if deps.get(s, 0) < v:
                    deps[s] = v
            if b.ex:
                for (s, v) in b.r:
                    if s != own and deps.get(s, 0) < v:
                        deps[s] = v
        for tb in writes:
            b = tb.b if isinstance(tb, TB) else tb
            if b.w is not None and not nowaw:
                s, v = b.w
                if deps.get(s, 0) < v:
                    deps[s] = v
            for (s, v) in b.r:
                if deps.get(s, 0) < v:
                    deps[s] = v
        return deps

    def _commit(self, ev, reads, writes):
        for tb in writes:
            b = tb.b if isinstance(tb, TB) else tb
            b.w = ev
            b.ws = []
            b.r = []
        for tb in reads:
            b = tb.b if isinstance(tb, TB) else tb
            if b.w is not ev:
                b.r.append(ev)

    def _waits(self, eng, deps):
        w = []
        seen = self.seen[eng]
        for s, v in deps.items():
            if seen.get(s, 0) < v:
                seen[s] = v
                w.append((s, v))
        return w

    def op(self, eng, meth, *args, reads=(), writes=(), **kw):
        w = self._waits(eng, self._deps(reads, writes, self.eng_sem[eng]))
        self.cnt[eng] += 1
        ev = (self.eng_sem[eng], self.cnt[eng])
        self.ops[eng].append((w, (meth, args, kw), self.eng_sem[eng], 1))
        self._commit(ev, reads, writes)

    def dma(self, q, out, in_, reads=(), writes=(), nowaw=False, **kw):
        deps = self._deps(reads, writes, nowaw=nowaw)
        slot = self.dslots[q][self.dcur[q] % len(self.dslots[q])]
        self.dcur[q] += 1
        if slot[1] > 0:
            if deps.get(slot[0], 0) < slot[1]:
                deps[slot[0]] = slot[1]
        w = self._waits(q, deps)
        slot[1] += 16
        ev = (slot[0], slot[1])
        kw = dict(kw); kw["out"] = out; kw["in_"] = in_
        self.ops[q].append((w, ("dma_start", (), kw), slot[0], 16))
        if nowaw:
            for tb in writes:
                b = tb.b if isinstance(tb, TB) else tb
                b.ws.append(ev)
            self._commit(ev, reads, ())
        else:
            self._commit(ev, reads, writes)

    def barrier(self):
        evs = {}
        for e, sid in self.eng_sem.items():
            if self.cnt[e] > 0:
                evs[sid] = self.cnt[e]
        for q in self.dslots:
            for sid, c in self.dslots[q]:
                if c > 0:
                    evs[sid] = c
        for e in ENGS:
            w = self._waits(e, dict(evs))
            if w:
                self.ops[e].append((w, None, None, 0))

    def carve(self, arena, off, shape, dtype=F32, name="c"):
        n = 1
        for d in shape[1:]:
            n *= d
        if dtype == BF16:
            ap = arena.t[:, off:off + (n + 1) // 2].bitcast(BF16)[:, 0:n]
            used = (n + 1) // 2
        else:
            ap = arena.t[:, off:off + n]
            used = n
        if len(shape) == 3:
            ap = ap.rearrange("p (a b) -> p a b", a=shape[1])
        return TB(ap, Buf(name)), off + used

    def finish(self):
        w = []
        for q in self.dslots:
            for s, c in self.dslots[q]:
                if c > 0:
                    w.append((s, c))
        self.ops["sp"].append((w, None, None, 0))

    def emit(self):
        nc = self.nc
        block = self.es.enter_context(nc.Block())
        semobjs = self.semobjs

        def replay(lst):
            def f(e):
                for (w, fn, s, inc) in lst:
                    if fn is None:
                        for (ws, wv) in w:
                            e.wait_ge(semobjs[ws], wv)
                        continue
                    for (ws, wv) in w[:-1]:
                        e.wait_ge(semobjs[ws], wv)
                    ins = getattr(e, fn[0])(*fn[1], **fn[2])
                    if w:
                        ins.wait_op(semobjs[w[-1][0]], w[-1][1], "sem-ge")
                    ins.then_inc(semobjs[s], inc)
            return f

        block.tensor(replay(self.ops["pe"]))
        block.scalar(replay(self.ops["act"]))
        block.vector(replay(self.ops["dve"]))
        block.gpsimd(replay(self.ops["pool"]))
        block.sync(replay(self.ops["sp"]))


def build(n_pt=16, layers=(0, 1, 2, 3), has_sample=True):
    nc = bass.Bass("TRN2", target_bir_lowering=False)
    NT = n_pt + (2 if has_sample else 0)
    LP = n_pt * 128

    def din(name, shape, dt=F32):
        return nc.dram_tensor(name, list(shape), dt, kind="ExternalInput").ap()

    def dout(name, shape, dt=F32):
        return nc.dram_tensor(name, list(shape), dt, kind="ExternalOutput").ap()

    x_in = din("x_in", [NT * 128, D])
    y_out = dout("y_out", [NT * 128, D])
    xscr = [nc.dram_tensor(f"xscr{i}", [NT * 128, D], F32, kind="Internal").ap() for i in range(2)]
    consts = din("consts", [128, CW])
    ln_g = din("ln_g", [DEPTH, D])
    ln_b = din("ln_b", [DEPTH, D])
    gla_w_in = din("gla_w_in", [2, D, GLA_IN])
    gla_w_a2 = din("gla_w_a2", [2, 16, 512])
    gla_b_a = din("gla_b_a", [2, 512])
    gla_norm_g = din("gla_norm_g", [2, 256])
    gla_w_out = din("gla_w_out", [2, D, D])
    st_gla = din("st_gla", [2, 16, 4, 128, 256])
    p_gla = dout("p_gla", [2, 4, 128, 256])
    s_gla = dout("s_gla", [2, 16, 4, 128, 256])

    gdn_w_in = din("gdn_w_in", [1, D, GDN_IN])
    gdn_w_conv = din("gdn_w_conv", [1, 4, 3072])
    gdn_a_log = din("gdn_a_log", [1, 8])
    gdn_dt_bias = din("gdn_dt_bias", [1, 8])
    gdn_norm_g = din("gdn_norm_g", [1, 128])
    gdn_w_out = din("gdn_w_out", [1, D, D])
    st_gdn = din("st_gdn", [1, 16, 8, 128, 128])
    st_conv = din("st_conv", [1, 16, 3, 3072])
    p_gdn = dout("p_gdn", [1, 8, 128, 128])
    s_gdn = dout("s_gdn", [1, 16, 8, 128, 128])
    p_conv = dout("p_conv", [1, 3, 3072])
    s_conv = dout("s_conv", [1, 16, 3, 3072])

    s5_w_in = din("s5_w_in", [1, D, 2048])
    s5_lam_re = din("s5_lam_re", [1, 64, 64])
    s5_lam_im = din("s5_lam_im", [1, 64, 64])
    s5_log_dt = din("s5_log_dt", [1, 64])
    s5_bt = din("s5_bt", [128, 8 * 2 * 2 * 128])
    s5_ct = din("s5_ct", [128, 32 * 2 * 32])
    s5_d = din("s5_d", [1, 64, 16])
    s5_w_glu = din("s5_w_glu", [1, D, 2048])
    s5_b_glu = din("s5_b_glu", [1, 2048])
    s5_w_out = din("s5_w_out", [1, D, D])
    st_re = din("st_re", [1, 16, 64, 64])
    st_im = din("st_im", [1, 16, 64, 64])
    p_re = dout("p_re", [1, 64, 64])
    p_im = dout("p_im", [1, 64, 64])
    s_re = dout("s_re", [1, 16, 64, 64])
    s_im = dout("s_im", [1, 16, 64, 64])

    es = ExitStack()
    with es:
        P = Prog(nc, es)
        cst = P.sb("cst", [128, CW], F32)
        P.dma("sp", cst.t[:], consts, writes=[cst])
        ident = cst.t[:, 0:128]
        ones = cst.t[:, 128:256]
        MU = {1: cst.t[:, 256:384], 16: cst.t[:, 512:640]}
        ML = {1: cst.t[:, 384:512], 16: cst.t[:, 640:768]}
        seqind = cst.t[:, 768:784]
        kkrow = cst.t[:, 784:912]
        notfirst = cst.t[:, 912:1040]
        identb = P.sb("identb", [128, 128], BF16)
        P.op("act", "activation", out=identb.t[:], in_=ident, func=AF.Copy, reads=[cst], writes=[identb])
        identr = P.sb("identr", [128, 128], F32)
        P.op("act", "activation", out=identr.t[:].bitcast(F32R), in_=ident, func=AF.Copy, reads=[cst], writes=[identr])
        frt = es.enter_context(nc.sbuf_tensor("frt", [128, 12 * 128], F32))
        onesb = P.sb("onesb", [128, 128], BF16)
        P.op("act", "activation", out=onesb.t[:], in_=ones, func=AF.Copy, reads=[cst], writes=[onesb])
        sbf_r = P.ring("sbf", [128, 256], BF16, 2)

        winf = P.sb("win", [128, 8 * GDN_IN], BF16)
        win = TB(winf.t[:, 0:8 * GDN_IN].rearrange("p (k e) -> p k e", k=8), winf.b)
        win_gla = TB(winf.t[:, 0:8 * GLA_IN].rearrange("p (k e) -> p k e", k=8), winf.b)
        win_s5 = TB(winf.t[:, 0:8 * 2048].rearrange("p (k e) -> p k e", k=8), winf.b)
        wout = P.sb("wout", [128, 8, D], BF16)
        big = P.sb("big", [128, 8 * 1024], F32)
        sstate = big.t[:].rearrange("p (s e) -> p s e", s=8)
        pstate = P.sb("pstate", [128, 1024], F32)
        lng = P.sb("lng", [128, D], F32)
        lnb = P.sb("lnb", [128, D], F32)

        xs_r = P.ring("xs", [128, D], F32, 2)
        xb_r = P.ring("xb", [128, D], BF16, 1)
        xT_r = P.ring("xT", [128, 8, 128], BF16, 2)
        z_r = P.ring("zln", [128, D], F32, 1)
        st_r = P.ring("stat", [128, 16], F32, 2)
        ogT_r = P.ring("ogT", [128, 8, 128], BF16, 2)
        ARENA = 12600
        arena = P.sb("arena", [128, ARENA], F32)

        def carve_ring(off, n, shape, dtype=F32, name="r"):
            items = []
            for i in range(n):
                tb, off = P.carve(arena, off, shape, dtype, f"{name}{i}")
                items.append(tb)
            return Ring(items), off

        def load_w(dst, col0, src2d, ncols):
            v = src2d.rearrange("(kc p) e -> p kc e", p=128)
            for kc in range(8):
                c = 0
                while c < ncols:
                    w = min(1024, ncols - c)
                    P.dma("pool", dst.t[:, kc, col0 + c:col0 + c + w], v[:, kc, c:c + w], writes=[dst], nowaw=True)
                    c += w

        def load_x(src, t):
            xs = xs_r.next()
            P.dma("sp", xs.t[:], src[t * 128:(t + 1) * 128, :], writes=[xs])
            return xs

        def make_xT(xs):
            xb = xb_r.next()
            P.op("act", "activation", out=xb.t[:], in_=xs.t[:], func=AF.Copy, reads=[xs], writes=[xb])
            bk = P.bank()
            pb = bk.t[:].bitcast(BF16).rearrange("p (k t) -> p k t", k=8)
            for kc in range(8):
                P.op("pe", "transpose", pb[:, kc, :], xb.t[:, kc * 128:(kc + 1) * 128], identb.t[:],
                     reads=[xb, identb], writes=[bk])
            xT = xT_r.next()
            P.op("dve", "tensor_copy", out=xT.t[:], in_=pb, reads=[bk], writes=[xT])
            return xT

        def proj_tm(xT, w, c0, ncols, out_ap, bk):
            for kc in range(8):
                P.op("pe", "matmul", out_ap, lhsT=xT.t[:, kc, :], rhs=w.t[:, kc, c0:c0 + ncols],
                                                     start=(kc == 0), stop=(kc == 7),
                     reads=[xT, w], writes=[bk])

        def proj_fm(xT, w, c0, ncols, out_ap, bk):
            for kc in range(8):
                P.op("pe", "matmul", out_ap, lhsT=w.t[:, kc, c0:c0 + ncols], rhs=xT.t[:, kc, :],
                                                     start=(kc == 0), stop=(kc == 7),
                     reads=[xT, w], writes=[bk])

        def out_proj_ln(ogT, xs, layer, dst, t):
            z = z_r.next()
            for half in range(2):
                bk = P.bank()
                for kc in range(8):
                    P.op("pe", "matmul",
                        bk.t[:, :], lhsT=ogT.t[:, kc, :], rhs=wout.t[:, kc, half * 512:(half + 1) * 512],
                        start=(kc == 0), stop=(kc == 7), reads=[ogT, wout], writes=[bk])
                P.op("dve", "scalar_tensor_tensor",
                    out=z.t[:, half * 512:(half + 1) * 512], in0=xs.t[:, half * 512:(half + 1) * 512], scalar=ALPHA,
                    in1=bk.t[:, :], op0=ALU.mult, op1=ALU.add, reads=[xs, bk], writes=[z])
            st = st_r.next()
            P.op("dve", "bn_stats", out=st.t[:, 0:6], in_=z.t[:, 0:512], reads=[z], writes=[st])
            P.op("dve", "bn_stats", out=st.t[:, 6:12], in_=z.t[:, 512:1024], reads=[z], writes=[st])
            P.op("dve", "bn_aggr", out=st.t[:, 12:14], in_=st.t[:, 0:12], reads=[st], writes=[st])
            P.op("act", "activation", out=st.t[:, 14:15], in_=st.t[:, 13:14], func=AF.Sqrt, bias=epsb.t[:, 0:1],
                 reads=[st, epsb], writes=[st])
            P.op("dve", "reciprocal", out=st.t[:, 15:16], in_=st.t[:, 14:15], reads=[st], writes=[st])
            P.op("dve", "tensor_scalar", out=z.t[:], in0=z.t[:], scalar1=st.t[:, 12:13], scalar2=st.t[:, 15:16],
                                                  op0=ALU.subtract, op1=ALU.mult, reads=[z, st], writes=[z])
            P.op("pool", "tensor_tensor", out=z.t[:], in0=z.t[:], in1=lng.t[:], op=ALU.mult, reads=[z, lng], writes=[z])
            P.op("pool", "tensor_tensor", out=z.t[:], in0=z.t[:], in1=lnb.t[:], op=ALU.add, reads=[z, lnb], writes=[z])
            P.dma("sp", dst[t * 128:(t + 1) * 128, :], z.t[:], reads=[z])

        epsb = P.sb("epsb", [128, 4], F32)
        P.op("pool", "memset", epsb.t[:, 0:1], LN_EPS, writes=[epsb])
        P.op("pool", "memset", epsb.t[:, 1:2], NORM_EPS, writes=[epsb])
        P.op("pool", "memset", epsb.t[:, 2:3], 1.0, writes=[epsb])
        P.op("pool", "memset", epsb.t[:, 3:4], -math.pi, writes=[epsb])

        def load_ln(layer):
            P.dma("sp", lng.t[:], ln_g[layer:layer + 1, :].partition_broadcast(128), writes=[lng])
            P.dma("sp", lnb.t[:], ln_b[layer:layer + 1, :].partition_broadcast(128), writes=[lnb])

        def gla_layer(j, layer, src, dst):
            win = win_gla
            P.barrier()
            off = 0
            w512, off = carve_ring(off, 6, [128, 512], name="w512_")
            v_r, off = carve_ring(off, 2, [128, D], name="vtm")
            sring, skz = [], []
            for st_ in range(2):
                r_, off = carve_ring(off, 12, [128, 128], name=f"gA{st_}_")
                k_, off = carve_ring(off, 3, [128, 128], name=f"kz{st_}_")
                sring.append(r_)
                skz.append(k_)
            stg_r, off = carve_ring(off, 8, [128, 512], BF16, name="pstg")
            lrT_, off = P.carve(arena, off, [128, 128], F32, "lrT")
            wa2_, off = P.carve(arena, off, [128, 512], F32, "wa2")
            gcol, off = P.carve(arena, off, [128, 2], F32, "gcol")
            lrT = TB(lrT_.t[0:33, :], lrT_.b)
            wa2 = TB(wa2_.t[0:33, :], wa2_.b)
            P.op("pool", "memset", lrT.t[:], 0.0, writes=[lrT])
            P.op("pool", "memset", lrT.t[32:33, :], 1.0, writes=[lrT])
            assert off <= ARENA

            def b16(tb, n=128):
                return TB(tb.t[:].bitcast(BF16)[:, 0:n], tb.b)
            load_w(win, 0, gla_w_in[j], GLA_IN)
            load_w(wout, 0, gla_w_out[j], D)
            load_ln(layer)
            P.op("pool", "memset", wa2.t[:], 0.0, writes=[wa2])
            P.dma("sp", wa2.t[0:16, :], gla_w_a2[j], writes=[wa2])
            P.dma("sp", wa2.t[32:33, :], gla_b_a[j:j + 1, :], writes=[wa2])
            P.dma("sp", gcol.t[:], gla_norm_g[j].rearrange("(vb p) -> p vb", p=128), writes=[gcol], allow_slow_non_contiguous=True)
            P.op("pool", "memset", pstate.t[:], 0.0, writes=[pstate])
            order = ([n_pt, n_pt + 1] if has_sample else []) + list(range(n_pt))
            for t in order:
                samp = (t >= n_pt)
                nseq = 16 if samp else 1
                nreal = 8 if samp else 1
                L = 128 // nseq
                stb = big if samp else pstate
                if samp:
                    for s in range(8):
                        P.dma("sp", sstate[:, s, :].rearrange("p (h v) -> p h v", h=4),
                              st_gla[j, (t - n_pt) * 8 + s].rearrange("h d v -> d h v"), writes=[big], nowaw=True)

                def S_ap(s, lo, hi):
                    return sstate[:, s, lo:hi] if samp else pstate.t[:, lo:hi]

                xs = load_x(src, t)
                xT = make_xT(xs)
                bk = P.bank()
                proj_fm(xT, win, 3072, 16, bk.t[0:16, 0:128], bk)
                P.op("act", "activation", out=lrT.t[0:16, :], in_=bk.t[0:16, 0:128], func=AF.Copy,
                     reads=[bk], writes=[lrT])
                bk = P.bank()
                P.op("pe", "matmul", bk.t[:, :], lhsT=lrT.t[:, :], rhs=wa2.t[:, :], start=True, stop=True,
                     reads=[lrT, wa2], writes=[bk])
                la = w512.next()
                P.op("act", "activation", out=la.t[:], in_=bk.t[:, :], func=AF.Exp, scale=-1.0, reads=[bk], writes=[la])
                P.op("act", "activation", out=la.t[:], in_=la.t[:], func=AF.Ln, bias=epsb.t[:, 2:3], reads=[la, epsb], writes=[la])
                bk = P.bank()
                P.op("pe", "matmul", bk.t[:, :], lhsT=ML[nseq], rhs=la.t[:], start=True, stop=True,
                     reads=[cst, la], writes=[bk])
                ek = w512.next()
                P.op("act", "activation", out=ek.t[:], in_=bk.t[:, :], func=AF.Exp, scale=-1.0 / 16, reads=[bk], writes=[ek])
                bk = P.bank()
                proj_tm(xT, win, 512, 512, bk.t[:, :], bk)
                kS = stg_r.next()
                P.op("act", "activation", out=kS.t[:], in_=bk.t[:, :], func=AF.Copy, reads=[bk], writes=[kS])
                kend = b16(w512.next(), 512)
                P.op("dve", "tensor_tensor", out=kend.t[:], in0=bk.t[:, :], in1=ek.t[:], op=ALU.mult,
                     reads=[bk, ek], writes=[kend])
                bk = P.bank()
                proj_tm(xT, win, 0, 512, bk.t[:, :], bk)
                qS = stg_r.next()
                P.op("act", "activation", out=qS.t[:], in_=bk.t[:, :], func=AF.Copy, reads=[bk], writes=[qS])
                rS = []
                for half in range(2):
                    bk = P.bank()
                    proj_tm(xT, win, 2048 + half * 512, 512, bk.t[:, :], bk)
                    r_ = stg_r.next()
                    P.op("dve", "tensor_copy", out=r_.t[:], in_=bk.t[:, :], reads=[bk], writes=[r_])
                    rS.append(r_)
                v = b16(v_r.next(), 1024)
                for half in range(2):
                    bk = P.bank()
                    proj_tm(xT, win, 1024 + half * 512, 512, bk.t[:, :], bk)
                    P.op("act", "activation", out=v.t[:, half * 512:(half + 1) * 512], in_=bk.t[:, :], func=AF.Copy,
                         reads=[bk], writes=[v])
                ogT = ogT_r.next()
                def gla_head(h, w128, kz_r):
                    bkB = P.bank()
                    P.op("pe", "matmul", bkB.t[:, 0:128], lhsT=la.t[:, h * 128:(h + 1) * 128], rhs=MU[nseq], start=True, stop=True,
                         reads=[la, cst], writes=[bkB])
                    e1 = w128.next()
                    e2 = w128.next()
                    P.op("act", "activation", out=e1.t[:], in_=bkB.t[:, 0:128], func=AF.Exp, scale=-1.0 / 16, reads=[bkB], writes=[e1])
                    P.op("act", "activation", out=e2.t[:], in_=bkB.t[:, 0:128], func=AF.Exp, scale=1.0 / 16, reads=[bkB], writes=[e2])
                    yield
                    bkq = P.bank()
                    bkqb = bkq.t[:].bitcast(BF16)
                    P.op("pe", "transpose", bkqb[:, 0:128], qS.t[:, h * 128:(h + 1) * 128], identb.t[:], reads=[qS, identb], writes=[bkq])
                    P.op("pe", "transpose", bkqb[:, 128:256], kS.t[:, h * 128:(h + 1) * 128], identb.t[:], reads=[kS, identb], writes=[bkq])
                    qd = b16(w128.next())
                    P.op("dve", "scalar_tensor_tensor", out=qd.t[:], in0=bkqb[:, 0:128], scalar=128.0 ** -0.5, in1=e1.t[:],
                         op0=ALU.mult, op1=ALU.mult, reads=[bkq, e1], writes=[qd])
                    ki = b16(w128.next())
                    P.op("dve", "tensor_tensor", out=ki.t[:], in0=bkqb[:, 128:256], in1=e2.t[:], op=ALU.mult,
                         reads=[bkq, e2], writes=[ki])
                    yield
                    bka = P.bank()
                    P.op("pe", "matmul", bka.t[:, 0:128], lhsT=ki.t[:], rhs=qd.t[:], start=True, stop=True,
                         reads=[ki, qd], writes=[bka])
                    att = b16(w128.next())
                    P.op("dve", "tensor_tensor", out=att.t[:], in0=bka.t[:, 0:128], in1=MU[nseq], op=ALU.mult,
                         reads=[bka, cst], writes=[att])
                    yield
                    obk = P.lbank()
                    first = True
                    for s in range(nreal):
                        sb_ = sbf_r.next()
                        P.op("act", "activation", out=sb_.t[:, 0:256], in_=S_ap(s, h * 256, (h + 1) * 256), func=AF.Copy, reads=[stb], writes=[sb_])
                        for vb in range(2):
                            P.op("pe", "matmul", obk.t[:, vb * 128 + s * L:vb * 128 + (s + 1) * L], lhsT=sb_.t[:, vb * 128:(vb + 1) * 128],
                                 rhs=qd.t[:, s * L:(s + 1) * L], start=first, stop=False, skip_group_check=True, reads=[sb_, qd], writes=[obk])
                            first = False
                    for vb in range(2):
                        c0 = h * 256 + vb * 128
                        P.op("pe", "matmul", obk.t[:, vb * 128:(vb + 1) * 128], lhsT=v.t[:, c0:c0 + 128], rhs=att.t[:], start=False, stop=(vb == 1),
                             skip_group_check=True, reads=[v, att], writes=[obk])
                    yield
                    for s in range(nreal):
                        if samp:
                            kz = b16(kz_r.next())
                            P.op("dve", "tensor_scalar", out=kz.t[:], in0=kend.t[:, h * 128:(h + 1) * 128],
                                 scalar1=seqind[:, s:s + 1], scalar2=None, op0=ALU.mult, reads=[kend, cst], writes=[kz])
                            kzap = kz.t[:]
                            kzb = kz
                        else:
                            kzap = kend.t[:, h * 128:(h + 1) * 128]
                            kzb = kend
                        bks = P.bank()
                        P.op("pe", "matmul", bks.t[:, 0:256], lhsT=kzap, rhs=v.t[:, h * 256:(h + 1) * 256], start=True, stop=True,
                             reads=[kzb, v], writes=[bks])
                        P.op("dve", "scalar_tensor_tensor",
                             out=S_ap(s, h * 256, (h + 1) * 256), in0=S_ap(s, h * 256, (h + 1) * 256),
                             scalar=e1.t[:, (s + 1) * L - 1:(s + 1) * L], in1=bks.t[:, 0:256], op0=ALU.mult, op1=ALU.add,
                             reads=[stb, e1, bks], writes=[stb])
                        yield
                    osq = b16(w128.next(), 256)
                    P.op("act", "activation", out=osq.t[:, 0:256], in_=obk.t[:, 0:256], func=AF.Square, reads=[obk], writes=[osq])
                    yield
                    bkn = P.bank()
                    for vb in range(2):
                        P.op("pe", "matmul", bkn.t[:, 0:128], lhsT=onesb.t[:], rhs=osq.t[:, vb * 128:(vb + 1) * 128], start=(vb == 0), stop=(vb == 1),
                             reads=[onesb, osq], writes=[bkn])
                    rs = w128.next()
                    P.op("act", "activation", out=rs.t[:], in_=bkn.t[:, 0:128], func=AF.Sqrt, scale=1.0 / 256, bias=epsb.t[:, 1:2],
                         reads=[bkn, epsb], writes=[rs])
                    yield
                    P.op("dve", "reciprocal", out=rs.t[:], in_=rs.t[:], reads=[rs], writes=[rs])
                    for vb in range(2):
                        bkr = P.bank()
                        bkrb = bkr.t[:].bitcast(BF16)
                        rc = h * 256 + vb * 128
                        P.op("pe", "transpose", bkrb[:, 0:128], rS[rc // 512].t[:, rc % 512:rc % 512 + 128], identb.t[:],
                             reads=[rS[rc // 512], identb], writes=[bkr])
                        sr = w128.next()
                        P.op("act", "activation", out=sr.t[:], in_=bkrb[:, 0:128], func=AF.Silu, reads=[bkr], writes=[sr])
                        t1 = w128.next()
                        P.op("dve", "scalar_tensor_tensor", out=t1.t[:], in0=obk.t[:, vb * 128:(vb + 1) * 128], scalar=gcol.t[:, vb:vb + 1], in1=rs.t[:],
                             op0=ALU.mult, op1=ALU.mult, reads=[obk, gcol, rs], writes=[t1])
                        P.op("pool", "tensor_tensor", out=ogT.t[:, h * 2 + vb, :], in0=t1.t[:], in1=sr.t[:], op=ALU.mult,
                             reads=[t1, sr], writes=[ogT])
                        yield

                def chain2(g1, g2):
                    yield from g1
                    yield from g2

                streams = [chain2(gla_head(0, sring[0], skz[0]), gla_head(1, sring[0], skz[0])),
                           chain2(gla_head(2, sring[1], skz[1]), gla_head(3, sring[1], skz[1]))]
                while streams:
                    for g_ in list(streams):
                        try:
                            next(g_)
                        except StopIteration:
                            streams.remove(g_)
                out_proj_ln(ogT, xs, layer, dst, t)
                if samp:
                    for s in range(8):
                        P.dma("sp", s_gla[j, (t - n_pt) * 8 + s].rearrange("h d v -> d h v"), sstate[:, s, :].rearrange("p (h v) -> p h v", h=4), reads=[big])
            P.dma("sp", p_gla[j].rearrange("h d v -> d h v"), pstate.t[:].rearrange("p (h v) -> p h v", h=4), reads=[pstate])


        GSTOP = 99

        def gdn_layer(j, layer, src, dst):
            P.barrier()
            off = 0
            role = {}
            for nm in ("kTM", "vTM", "qdT", "egcb", "atts", "nwT", "vn"):
                role[nm] = []
                for hl in range(4):
                    tb, off = P.carve(arena, off, [128, 128], F32, f"{nm}{hl}")
                    role[nm].append(tb)
            for nm in ("kTM", "vTM", "qdT", "atts", "nwT", "vn"):
                role[nm] = [TB(tb.t[:].bitcast(BF16)[:, 0:128], tb.b) for tb in role[nm]]
            for i_, nm in enumerate(("X", "XT", "TT")):
                role[nm] = [TB(frt[:, (i_ * 4 + hl) * 128:(i_ * 4 + hl + 1) * 128].bitcast(F32R), Buf(f"{nm}{hl}")) for hl in range(4)]
            w128, off = carve_ring(off, 8, [128, 128], name="g128_")
            gstg, off = carve_ring(off, 4, [128, 512], BF16, name="gstg")
            sring, sext = [], []
            for st_ in range(2):
                r_, off = carve_ring(off, 12, [128, 128], name=f"sA{st_}_")
                e_, off = carve_ring(off, 3, [128, 176], name=f"ext{st_}_")
                sring.append(r_)
                sext.append(e_)

            def b16(tb):
                return TB(tb.t[:].bitcast(BF16)[:, 0:128], tb.b)

            def f32r(tb):
                return tb
            cvs_r, off = carve_ring(off, 2, [128, 512], name="cvs")
            scal, off = P.carve(arena, off, [128, 64], F32, "scal")
            abt, off = P.carve(arena, off, [128, 16], F32, "abt")
            cvT, off = P.carve(arena, off, [128, 24 * 48], F32, "cvT")
            hal, off = P.carve(arena, off, [128, 72], F32, "hal")
            wcc, off = P.carve(arena, off, [128, 96], F32, "wcc")
            gb, off = P.carve(arena, off, [128, 24], F32, "gb")
            gcolg, off = P.carve(arena, off, [128, 2], F32, "gcolg")
            assert off <= ARENA, off
            cvT4 = cvT.t[:].rearrange("p (b s r) -> p b s r", b=24, s=16)
            hal3 = hal.t[:].rearrange("p (b r) -> p b r", b=24)
            Xr = [role["X"], role["X"]]
            XTr = [role["XT"], role["XT"]]
            TTr = [role["TT"], role["TT"]]

            load_w(win, 0, gdn_w_in[j], GDN_IN)
            load_w(wout, 0, gdn_w_out[j], D)
            load_ln(layer)

            def load_T(src2d, nrows, dst_fn, dst_tb):
                for c in range(6):
                    stg = cvs_r.next()
                    P.dma("sp", stg.t[0:nrows, :], src2d[:, c * 512:(c + 1) * 512], writes=[stg])
                    bk = P.bank()
                    for b4 in range(4):
                        P.op("pe", "matmul", bk.t[:, b4 * 32:b4 * 32 + nrows], lhsT=stg.t[0:nrows, b4 * 128:(b4 + 1) * 128],
                             rhs=ident[0:nrows, 0:nrows], start=True, stop=True, reads=[stg, cst], writes=[bk])
                    for b4 in range(4):
                        P.op("act", "activation", out=dst_fn(c * 4 + b4), in_=bk.t[:, b4 * 32:b4 * 32 + nrows], func=AF.Copy,
                             reads=[bk], writes=[dst_tb])

            load_T(gdn_w_conv[j], 4, lambda blk: wcc.t[:, blk * 4:(blk + 1) * 4], wcc)
            P.dma("sp", gb.t[:, 0:8], gdn_a_log[j:j + 1, :].partition_broadcast(128), writes=[gb])
            P.dma("sp", gb.t[:, 8:16], gdn_dt_bias[j:j + 1, :].partition_broadcast(128), writes=[gb])
            P.op("act", "activation", out=gb.t[:, 16:24], in_=gb.t[:, 0:8], func=AF.Exp, reads=[gb], writes=[gb])
            P.dma("sp", gcolg.t[:, 0:1], gdn_norm_g[j].rearrange("(p o) -> p o", o=1), writes=[gcolg], allow_slow_non_contiguous=True)
            P.op("pool", "memset", hal.t[:], 0.0, writes=[hal])
            P.op("pool", "memset", cvT.t[:], 0.0, writes=[cvT])
            P.op("pool", "memset", pstate.t[:], 0.0, writes=[pstate])

            order = ([n_pt, n_pt + 1] if has_sample else []) + list(range(n_pt))
            if GSTOP <= 1:
                order = []
            for t in order:
                samp = (t >= n_pt)
                nseq = 16 if samp else 1
                nreal = 8 if samp else 1
                L = 128 // nseq
                stb = big if samp else pstate
                g0 = (t - n_pt) * 8

                def S_ap(s, lo, hi):
                    return sstate[:, s, lo:hi] if samp else pstate.t[:, lo:hi]

                xs = load_x(src, t)
                xT = make_xT(xs)
                if samp:
                    for s in range(8):
                        P.dma("sp", sstate[:, s, :].rearrange("p (h v) -> p h v", h=8),
                              st_gdn[j, g0 + s].rearrange("h d v -> d h v"), writes=[big], nowaw=True)
                    load_T(st_conv[j, g0:g0 + 8].rearrange("s r c -> (s r) c"), 24,
                           lambda blk: cvT4[:, blk, 0:8, :], cvT)
                bk = P.bank()
                proj_tm(xT, win, 4096, 16, bk.t[:, 0:16], bk)
                P.op("act", "activation", out=abt.t[:], in_=bk.t[:, 0:16], func=AF.Copy, reads=[bk], writes=[abt])
                sc = scal.t
                P.op("dve", "tensor_tensor", out=sc[:, 0:8], in0=abt.t[:, 0:8], in1=gb.t[:, 8:16], op=ALU.add, reads=[abt, gb], writes=[scal])
                P.op("act", "activation", out=sc[:, 0:8], in_=sc[:, 0:8], func=AF.Exp, reads=[scal], writes=[scal])
                P.op("act", "activation", out=sc[:, 0:8], in_=sc[:, 0:8], func=AF.Ln, bias=epsb.t[:, 2:3], reads=[scal, epsb], writes=[scal])
                P.op("dve", "tensor_tensor", out=sc[:, 0:8], in0=sc[:, 0:8], in1=gb.t[:, 16:24], op=ALU.mult, reads=[scal, gb], writes=[scal])
                P.op("act", "activation", out=sc[:, 8:16], in_=abt.t[:, 8:16], func=AF.Exp, scale=-1.0, reads=[abt], writes=[scal])
                P.op("dve", "tensor_scalar", out=sc[:, 8:16], in0=sc[:, 8:16], scalar1=1.0, scalar2=None, op0=ALU.add, reads=[scal], writes=[scal])
                P.op("dve", "reciprocal", out=sc[:, 8:16], in_=sc[:, 8:16], reads=[scal], writes=[scal])
                P.op("dve", "tensor_scalar", out=sc[:, 16:24], in0=sc[:, 8:16], scalar1=-1.0, scalar2=None, op0=ALU.mult, reads=[scal], writes=[scal])
                bk = P.bank()
                P.op("pe", "matmul", bk.t[:, 0:8], lhsT=MU[nseq], rhs=sc[:, 0:8], start=True, stop=True, reads=[cst, scal], writes=[bk])
                P.op("pe", "matmul", bk.t[:, 8:16], lhsT=ML[nseq], rhs=sc[:, 0:8], start=True, stop=True, reads=[cst, scal], writes=[bk])
                P.op("dve", "tensor_copy", out=sc[:, 24:32], in_=bk.t[:, 0:8], reads=[bk], writes=[scal])
                P.op("act", "activation", out=sc[:, 32:48], in_=bk.t[:, 0:16], func=AF.Exp, scale=-1.0, reads=[bk], writes=[scal])
                P.op("dve", "tensor_tensor", out=sc[:, 48:56], in0=sc[:, 8:16], in1=sc[:, 32:40], op=ALU.mult, reads=[scal], writes=[scal])
                G_, BETA_, NBETA_, GC_, EGC_, KSUF_, BG_ = 0, 8, 16, 24, 32, 40, 48

                ogT = ogT_r.next()
                if GSTOP < 99:
                    P.op("pool", "memset", ogT.t[:], 0.0, writes=[ogT])
                pendingG = None
                for grp in range(2):
                    stgP = []
                    for pi in range(3):
                        bk = P.bank()
                        proj_tm(xT, win, pi * 1024 + grp * 512, 512, bk.t[:, :], bk)
                        st_ = gstg.next()
                        if pi == 1:
                            P.op("dve", "tensor_copy", out=st_.t[:], in_=bk.t[:, :], reads=[bk], writes=[st_])
                        else:
                            P.op("act", "activation", out=st_.t[:], in_=bk.t[:, :], func=AF.Copy, reads=[bk], writes=[st_])
                        stgP.append(st_)

                    def headABC(hl, r128, rext):
                        h = grp * 4 + hl
                        bkp = P.bank()
                        bkpb = bkp.t[:].bitcast(BF16)
                        for pi in range(3):
                            P.op("pe", "transpose", bkpb[:, pi * 128:(pi + 1) * 128], stgP[pi].t[:, hl * 128:(hl + 1) * 128], identb.t[:],
                                 reads=[stgP[pi], identb], writes=[bkp])
                        exts = []
                        for pi in range(3):
                            blk = pi * 8 + h
                            ext = rext.next()
                            ev = ext.t[:, 0:nseq * (L + 3)].rearrange("p (s l) -> p s l", s=nseq)
                            P.op("act", "activation", out=ev[:, :, 3:3 + L],
                                 in_=bkpb[:, pi * 128:(pi + 1) * 128].rearrange("p (s l) -> p s l", s=nseq), func=AF.Copy,
                                 reads=[bkp], writes=[ext])
                            if samp:
                                P.op("pool", "tensor_copy", out=ev[:, :, 0:3], in_=cvT4[:, blk, :, :], reads=[cvT], writes=[ext])
                            else:
                                P.op("pool", "tensor_copy", out=ev[:, 0, 0:3], in_=hal3[:, blk, :], reads=[hal], writes=[ext])
                                P.op("pool", "tensor_copy", out=hal3[:, blk, :], in_=ev[:, 0, L:L + 3], reads=[ext], writes=[hal])
                            exts.append((ext, ev))
                        yield
                        qT = kT = vT = None
                        for pi in range(3):
                            blk = pi * 8 + h
                            ext, ev = exts[pi]
                            acc = r128.next()
                            av = acc.t[:].rearrange("p (s l) -> p s l", s=nseq)
                            P.op("dve", "tensor_scalar", out=av, in0=ev[:, :, 0:L], scalar1=wcc.t[:, blk * 4:blk * 4 + 1], scalar2=None,
                                 op0=ALU.mult, reads=[ext, wcc], writes=[acc])
                            for tau in range(1, 4):
                                P.op("dve", "scalar_tensor_tensor", out=av, in0=ev[:, :, tau:tau + L], scalar=wcc.t[:, blk * 4 + tau:blk * 4 + tau + 1],
                                     in1=av, op0=ALU.mult, op1=ALU.add, reads=[ext, wcc, acc], writes=[acc])
                            y = r128.next()
                            if pi == 2:
                                y = b16(y)
                            P.op("act", "activation", out=y.t[:], in_=acc.t[:], func=AF.Silu, reads=[acc], writes=[y])
                            if pi < 2:
                                sq = b16(r128.next())
                                P.op("pool", "tensor_tensor", out=sq.t[:], in0=y.t[:], in1=y.t[:], op=ALU.mult, reads=[y], writes=[sq])
                                yield
                                bkn = P.bank()
                                P.op("pe", "matmul", bkn.t[:, 0:128], lhsT=onesb.t[:], rhs=sq.t[:], start=True, stop=True, reads=[onesb, sq], writes=[bkn])
                                rn = r128.next()
                                P.op("act", "activation", out=rn.t[:], in_=bkn.t[:, 0:128], func=AF.Sqrt, bias=epsb.t[:, 1:2], reads=[bkn, epsb], writes=[rn])
                                yield
                                P.op("dve", "reciprocal", out=rn.t[:], in_=rn.t[:], reads=[rn], writes=[rn])
                                o_ = b16(r128.next())
                                if pi == 0:
                                    P.op("dve", "scalar_tensor_tensor", out=o_.t[:], in0=y.t[:], scalar=128.0 ** -0.5, in1=rn.t[:],
                                         op0=ALU.mult, op1=ALU.mult, reads=[y, rn], writes=[o_])
                                    qT = o_
                                else:
                                    P.op("dve", "tensor_tensor", out=o_.t[:], in0=y.t[:], in1=rn.t[:], op=ALU.mult, reads=[y, rn], writes=[o_])
                                    kT = o_
                            else:
                                vT = y
                            yield
                        bkt = P.bank()
                        bktb = bkt.t[:].bitcast(BF16)
                        kTM = role["kTM"][hl]
                        vTM = role["vTM"][hl]
                        P.op("pe", "transpose", bktb[:, 0:128], kT.t[:], identb.t[:], reads=[kT, identb], writes=[bkt])
                        P.op("pe", "transpose", bktb[:, 128:256], vT.t[:], identb.t[:], reads=[vT, identb], writes=[bkt])
                        P.op("act", "activation", out=kTM.t[:], in_=bktb[:, 0:128], func=AF.Copy, reads=[bkt], writes=[kTM])
                        P.op("dve", "tensor_copy", out=vTM.t[:], in_=bktb[:, 128:256], reads=[bkt], writes=[vTM])
                        mug = r128.next()
                        P.op("dve", "tensor_scalar", out=mug.t[:], in0=MU[nseq], scalar1=sc[:, G_ + h:G_ + h + 1], scalar2=None, op0=ALU.mult,
                             reads=[cst, scal], writes=[mug])
                        yield
                        bkg = P.bank()
                        P.op("pe", "matmul", bkg.t[:, 0:128], lhsT=ones, rhs=mug.t[:], start=True, stop=True, reads=[cst, mug], writes=[bkg])
                        dL = r128.next()
                        P.op("dve", "tensor_scalar", out=dL.t[:], in0=bkg.t[:, 0:128], scalar1=sc[:, GC_ + h:GC_ + h + 1], scalar2=0.0,
                             op0=ALU.subtract, op1=ALU.min, reads=[bkg, scal], writes=[dL])
                        dT = r128.next()
                        P.op("dve", "tensor_scalar", out=dT.t[:], in0=bkg.t[:, 0:128], scalar1=sc[:, GC_ + h:GC_ + h + 1], scalar2=0.0,
                             op0=ALU.subtract, op1=ALU.max, reads=[bkg, scal], writes=[dT])
                        egcb = role["egcb"][hl]
                        P.op("act", "activation", out=egcb.t[:], in_=bkg.t[:, 0:128], func=AF.Exp, scale=-1.0, reads=[bkg], writes=[egcb])
                        yield
                        P.op("act", "activation", out=dL.t[:], in_=dL.t[:], func=AF.Exp, reads=[dL], writes=[dL])
                        P.op("act", "activation", out=dT.t[:], in_=dT.t[:], func=AF.Exp, scale=-1.0, reads=[dT], writes=[dT])
                        P.op("dve", "scalar_tensor_tensor", out=dL.t[:], in0=dL.t[:], scalar=sc[:, NBETA_ + h:NBETA_ + h + 1], in1=ML[nseq],
                             op0=ALU.mult, op1=ALU.mult, reads=[dL, scal, cst], writes=[dL])
                        P.op("pool", "tensor_tensor", out=dT.t[:], in0=dT.t[:], in1=MU[nseq], op=ALU.mult, reads=[dT, cst], writes=[dT])
                        qdT = role["qdT"][hl]
                        P.op("dve", "tensor_tensor", out=qdT.t[:], in0=qT.t[:], in1=egcb.t[:], op=ALU.mult, reads=[qT, egcb], writes=[qdT])
                        yield
                        bkk = P.bank()
                        P.op("pe", "matmul", bkk.t[:, 0:128], lhsT=kT.t[:], rhs=kT.t[:], start=True, stop=True, reads=[kT], writes=[bkk])
                        P.op("pe", "matmul", bkk.t[:, 128:256], lhsT=kT.t[:], rhs=qT.t[:], start=True, stop=True, reads=[kT, qT], writes=[bkk])
                        X0 = Xr[0][hl]
                        P.op("dve", "tensor_tensor", out=X0.t[:], in0=bkk.t[:, 0:128], in1=dL.t[:], op=ALU.mult, reads=[bkk, dL], writes=[X0])
                        atts = role["atts"][hl]
                        P.op("dve", "tensor_tensor", out=atts.t[:], in0=bkk.t[:, 128:256], in1=dT.t[:], op=ALU.mult, reads=[bkk, dT], writes=[atts])
                        yield
                        bkx = P.bank()
                        P.op("pe", "matmul", bkx.t[:, 0:128], lhsT=X0.t[:], rhs=identr.t[:].bitcast(F32R), start=True, stop=True, reads=[X0, identr], writes=[bkx])
                        P.op("act", "activation", out=XTr[0][hl].t[:], in_=bkx.t[:, 0:128], func=AF.Copy, reads=[bkx], writes=[XTr[0][hl]])
                        P.op("dve", "tensor_tensor", out=TTr[0][hl].t[:], in0=bkx.t[:, 0:128], in1=ident, op=ALU.add, reads=[bkx, cst], writes=[TTr[0][hl]])
                        yield

                    def chain2(g1, g2):
                        yield from g1
                        yield from g2

                    streams = [chain2(headABC(0, sring[0], sext[0]), headABC(1, sring[0], sext[0])),
                               chain2(headABC(2, sring[1], sext[1]), headABC(3, sring[1], sext[1]))]
                    if pendingG is not None:
                        streams.append(pendingG)
                        pendingG = None
                    while streams:
                        for g_ in list(streams):
                            try:
                                next(g_)
                            except StopIteration:
                                streams.remove(g_)
                    if GSTOP <= 3:
                        continue
                    nit = 2 if samp else 6
                    for k in range(1, nit + 1):
                        cur, nxt = (k - 1) % 2, k % 2
                        bkA = P.bank()
                        for hl in range(4):
                            P.op("pe", "matmul", bkA.t[:, hl * 128:(hl + 1) * 128], lhsT=XTr[cur][hl].t[:], rhs=Xr[cur][hl].t[:], start=True, stop=True,
                                 reads=[XTr[cur][hl], Xr[cur][hl]], writes=[bkA])
                        if k < nit:
                            bkB = P.bank()
                            for hl in range(4):
                                P.op("pe", "matmul", bkB.t[:, hl * 128:(hl + 1) * 128], lhsT=Xr[cur][hl].t[:], rhs=XTr[cur][hl].t[:], start=True, stop=True,
                                     reads=[XTr[cur][hl], Xr[cur][hl]], writes=[bkB])
                        for hl in range(4):
                            P.op("act", "activation", out=Xr[nxt][hl].t[:], in_=bkA.t[:, hl * 128:(hl + 1) * 128], func=AF.Copy,
                                 reads=[bkA], writes=[Xr[nxt][hl]])
                        if k < nit:
                            for hl in range(4):
                                P.op("dve", "tensor_copy", out=XTr[nxt][hl].t[:], in_=bkB.t[:, hl * 128:(hl + 1) * 128],
                                     reads=[bkB], writes=[XTr[nxt][hl]])
                        bkC = P.bank()
                        for hl in range(4):
                            P.op("pe", "matmul", bkC.t[:, hl * 128:(hl + 1) * 128], lhsT=Xr[nxt][hl].t[:], rhs=TTr[cur][hl].t[:], start=True, stop=True,
                                 reads=[Xr[nxt][hl], TTr[cur][hl]], writes=[bkC])
                        for hl in range(4):
                            P.op("dve", "tensor_tensor", out=TTr[nxt][hl].t[:], in0=TTr[cur][hl].t[:], in1=bkC.t[:, hl * 128:(hl + 1) * 128], op=ALU.add,
                                 reads=[TTr[cur][hl], bkC], writes=[TTr[nxt][hl]])
                    TT = TTr[0]
                    if GSTOP <= 4:
                        continue
                    rhsu = []
                    for hl in range(4):
                        h = grp * 4 + hl
                        tu = b16(w128.next())
                        tw = b16(w128.next())
                        P.op("dve", "tensor_scalar", out=tu.t[:], in0=TT[hl].t[:].bitcast(F32), scalar1=sc[:, BETA_ + h:BETA_ + h + 1], scalar2=None, op0=ALU.mult,
                             reads=[TT[hl], scal], writes=[tu])
                        P.op("dve", "tensor_scalar", out=tw.t[:], in0=TT[hl].t[:].bitcast(F32), scalar1=sc[:, BG_ + h:BG_ + h + 1], scalar2=None, op0=ALU.mult,
                             reads=[TT[hl], scal], writes=[tw])
                        rhsu.append(tu)
                        bkw = P.bank()
                        P.op("pe", "matmul", bkw.t[:, 0:128], lhsT=role["kTM"][hl].t[:], rhs=tw.t[:], start=True, stop=True, reads=[role["kTM"][hl], tw], writes=[bkw])
                        P.op("act", "activation", out=role["nwT"][hl].t[:], in_=bkw.t[:, 0:128], func=AF.Copy, scale=-1.0, reads=[bkw], writes=[role["nwT"][hl]])
                    bkv = P.lbank()
                    bko = P.lbank()
                    for hl in range(4):
                        h = grp * 4 + hl
                        for s in range(nreal):
                            sb_ = sbf_r.next()
                            P.op("act", "activation", out=sb_.t[:, 0:128], in_=S_ap(s, h * 128, (h + 1) * 128), func=AF.Copy, reads=[stb], writes=[sb_])
                            P.op("pe", "matmul", bkv.t[:, hl * 128 + s * L:hl * 128 + (s + 1) * L], lhsT=sb_.t[:, 0:128],
                                 rhs=role["nwT"][hl].t[:, s * L:(s + 1) * L], start=(s == 0), stop=False, skip_group_check=True,
                                 reads=[sb_, role["nwT"][hl]], writes=[bkv])
                            P.op("pe", "matmul", bko.t[:, hl * 128 + s * L:hl * 128 + (s + 1) * L], lhsT=sb_.t[:, 0:128],
                                 rhs=role["qdT"][hl].t[:, s * L:(s + 1) * L], start=(s == 0 and hl == 0), stop=False, skip_group_check=True,
                                 reads=[sb_, role["qdT"][hl]], writes=[bko])
                        P.op("pe", "matmul", bkv.t[:, hl * 128:(hl + 1) * 128], lhsT=role["vTM"][hl].t[:], rhs=rhsu[hl].t[:], start=False, stop=True,
                             skip_group_check=True, reads=[rhsu[hl], role["vTM"][hl]], writes=[bkv])
                    vts = []
                    for hl in range(4):
                        vt_ = b16(w128.next())
                        P.op("act", "activation", out=vt_.t[:], in_=bkv.t[:, hl * 128:(hl + 1) * 128], func=AF.Copy, reads=[bkv], writes=[vt_])
                        vts.append(vt_)
                    bkt = P.bank()
                    bktb = bkt.t[:].bitcast(BF16)
                    for hl in range(4):
                        P.op("pe", "transpose", bktb[:, hl * 128:(hl + 1) * 128], vts[hl].t[:], identb.t[:], reads=[vts[hl], identb], writes=[bkt])
                    for hl in range(4):
                        P.op("dve", "tensor_copy", out=role["vn"][hl].t[:], in_=bktb[:, hl * 128:(hl + 1) * 128], reads=[bkt], writes=[role["vn"][hl]])
                    if GSTOP <= 5:
                        continue
                    for hl in range(4):
                        h = grp * 4 + hl
                        P.op("pe", "matmul", bko.t[:, hl * 128:(hl + 1) * 128], lhsT=role["vn"][hl].t[:], rhs=role["atts"][hl].t[:], start=False, stop=True,
                             skip_group_check=True, reads=[role["vn"][hl], role["atts"][hl]], writes=[bko])
                    for hl in range(4):
                        h = grp * 4 + hl
                        for s in range(nreal):
                            kz = b16(w128.next())
                            if samp:
                                P.op("dve", "tensor_scalar", out=kz.t[:], in0=role["kTM"][hl].t[:], scalar1=sc[:, KSUF_ + h:KSUF_ + h + 1],
                                     scalar2=seqind[:, s:s + 1], op0=ALU.mult, op1=ALU.mult, reads=[role["kTM"][hl], scal, cst], writes=[kz])
                            else:
                                P.op("dve", "tensor_scalar", out=kz.t[:], in0=role["kTM"][hl].t[:], scalar1=sc[:, KSUF_ + h:KSUF_ + h + 1],
                                     scalar2=None, op0=ALU.mult, reads=[role["kTM"][hl], scal], writes=[kz])
                            bks = P.bank()
                            P.op("pe", "matmul", bks.t[:, 0:128], lhsT=kz.t[:], rhs=role["vn"][hl].t[:], start=True, stop=True, reads=[kz, role["vn"][hl]], writes=[bks])
                            P.op("dve", "scalar_tensor_tensor", out=S_ap(s, h * 128, (h + 1) * 128), in0=S_ap(s, h * 128, (h + 1) * 128),
                                 scalar=role["egcb"][hl].t[:, (s + 1) * L - 1:(s + 1) * L], in1=bks.t[:, 0:128], op0=ALU.mult, op1=ALU.add,
                                 reads=[stb, role["egcb"][hl], bks], writes=[stb])
                    if GSTOP <= 6:
                        continue
                    def stageG(grp, bko):
                        bk = P.bank()
                        proj_tm(xT, win, 3072 + grp * 512, 512, bk.t[:, :], bk)
                        zst = gstg.next()
                        P.op("act", "activation", out=zst.t[:], in_=bk.t[:, :], func=AF.Copy, reads=[bk], writes=[zst])
                        yield
                        for hl in range(4):
                            h = grp * 4 + hl
                            osq = b16(w128.next())
                            P.op("act", "activation", out=osq.t[:], in_=bko.t[:, hl * 128:(hl + 1) * 128], func=AF.Square, reads=[bko], writes=[osq])
                            yield
                            bkn = P.bank()
                            P.op("pe", "matmul", bkn.t[:, 0:128], lhsT=onesb.t[:], rhs=osq.t[:], start=True, stop=True, reads=[onesb, osq], writes=[bkn])
                            rs = w128.next()
                            P.op("act", "activation", out=rs.t[:], in_=bkn.t[:, 0:128], func=AF.Sqrt, scale=1.0 / 128, bias=epsb.t[:, 1:2], reads=[bkn, epsb], writes=[rs])
                            yield
                            P.op("dve", "reciprocal", out=rs.t[:], in_=rs.t[:], reads=[rs], writes=[rs])
                            bkz = P.bank()
                            bkzb = bkz.t[:].bitcast(BF16)
                            P.op("pe", "transpose", bkzb[:, 0:128], zst.t[:, hl * 128:(hl + 1) * 128], identb.t[:], reads=[zst, identb], writes=[bkz])
                            sz = w128.next()
                            P.op("act", "activation", out=sz.t[:], in_=bkzb[:, 0:128], func=AF.Silu, reads=[bkz], writes=[sz])
                            yield
                            t1 = w128.next()
                            P.op("dve", "scalar_tensor_tensor", out=t1.t[:], in0=bko.t[:, hl * 128:(hl + 1) * 128], scalar=gcolg.t[:, 0:1], in1=rs.t[:],
                                 op0=ALU.mult, op1=ALU.mult, reads=[bko, gcolg, rs], writes=[t1])
                            P.op("pool", "tensor_tensor", out=ogT.t[:, h, :], in0=t1.t[:], in1=sz.t[:], op=ALU.mult, reads=[t1, sz], writes=[ogT])
                            yield

                    if grp == 0:
                        pendingG = stageG(grp, bko)
                    else:
                        for _ in stageG(grp, bko):
                            pass
                if (samp or t == n_pt - 1) and GSTOP > 7:
                    for c in range(6):
                        bk = P.bank()
                        proj_tm(xT, win, c * 512, 512, bk.t[:, :], bk)
                        stg = cvs_r.next()
                        P.op("act", "activation", out=stg.t[:], in_=bk.t[:, :], func=AF.Copy, reads=[bk], writes=[stg])
                        if samp:
                            for s in range(8):
                                P.dma("sp", s_conv[j, g0 + s, :, c * 512:(c + 1) * 512], stg.t[s * 8 + 5:s * 8 + 8, :], reads=[stg])
                        else:
                            P.dma("sp", p_conv[j, :, c * 512:(c + 1) * 512], stg.t[125:128, :], reads=[stg])
                out_proj_ln(ogT, xs, layer, dst, t)
                if samp:
                    for s in range(8):
                        P.dma("sp", s_gdn[j, g0 + s].rearrange("h d v -> d h v"), sstate[:, s, :].rearrange("p (h v) -> p h v", h=8), reads=[big])
            P.dma("sp", p_gdn[j].rearrange("h d v -> d h v"), pstate.t[:].rearrange("p (h v) -> p h v", h=8), reads=[pstate])


        def s5_layer(j, layer, src, dst):
            P.barrier()
            win = win_s5
            wglu = TB(big.t[:, 0:8192].bitcast(BF16).rearrange("p (k e) -> p k e", k=8), big.b)
            tabf = winf.t[:, 16384:32768].bitcast(F32)
            ck = tabf[:, 0:4096].rearrange("p (c k) -> p c k", c=32)
            sk = tabf[:, 4096:8192].rearrange("p (c k) -> p c k", c=32)
            tab = Buf("tab")
            off = 0
            btw, off = P.carve(arena, off, [128, 4096], BF16, "btw")
            ctb, off = P.carve(arena, off, [128, 4096], BF16, "ctb")
            pv, off = P.carve(arena, off, [128, 14, 32], F32, "pv")
            gir, off = P.carve(arena, off, [128, 32], F32, "gir")
            gii, off = P.carve(arena, off, [128, 32], F32, "gii")
            dcol, off = P.carve(arena, off, [128, 8], F32, "dcol")
            bglu, off = P.carve(arena, off, [128, 16], F32, "bglu")
            ring_off = off
            uT = TB(pstate.t[:].rearrange("p (a b) -> p a b", a=8), pstate.b)
            uTb, off = P.carve(arena, off, [128, 8, 128], BF16, "uTb")
            szT, off = P.carve(arena, off, [128, 8, 128], BF16, "szT")
            gyT, off = P.carve(arena, off, [128, 8, 128], BF16, "gyT")
            hlr, off = P.carve(arena, off, [128, 32, 8], F32, "hlr")
            hli, off = P.carve(arena, off, [128, 32, 8], F32, "hli")
            adr, off = P.carve(arena, off, [128, 32, 8], F32, "adr")
            adi, off = P.carve(arena, off, [128, 32, 8], F32, "adi")
            r0t, off = P.carve(arena, off, [128, 128], F32, "r0t")
            sstg, off = carve_ring(off, 2, [128, 512], BF16, name="sstg")
            w512, off = carve_ring(off, 9, [128, 512], name="s512_")
            assert off <= ARENA, off
            so = ring_off
            tmr, so = carve_ring(so, 5, [128, 1024], name="stmp")
            ctw, so = P.carve(arena, so, [128, 2048], F32, "ctw")
            assert so <= ARENA, so
            BT = btw.t[:].rearrange("p (k r q c) -> p k r q c", k=8, r=2, q=2)
            CT = ctw.t[:].rearrange("p (c r m) -> p c r m", c=32, r=2)
            CTB = ctb.t[:].rearrange("p (a q r m) -> p a q r m", a=16, q=2, r=2)
            R_, COS_, SIN_, ARE_, AIM_, CRE_, CIM_, ICR_, ICI_, TH_, C128_, S128_ = range(12)

            load_w(win, 0, s5_w_in[j], 2048)
            load_w(wglu, 0, s5_w_glu[j], 2048)
            load_w(wout, 0, s5_w_out[j], D)
            load_ln(layer)
            P.dma("pool", btw.t[:], s5_bt, writes=[btw])
            P.op("pool", "memset", ctb.t[:], 0.0, writes=[ctb])
            P.dma("sp", ctw.t[:], s5_ct, writes=[ctw])
            P.dma("sp", dcol.t[:], s5_d[j].rearrange("g c -> (g c)").rearrange("(kb r) -> r kb", r=128), writes=[dcol], allow_slow_non_contiguous=True)
            P.dma("sp", bglu.t[:], s5_b_glu[j].rearrange("(blk r) -> r blk", r=128), writes=[bglu], allow_slow_non_contiguous=True)

            def sincos(u_ap, out_ap, tmpA, tmpI, rd, wr, shape_p):
                P.op("dve", "tensor_copy", out=tmpI, in_=u_ap, reads=rd, writes=[tmpI_tb])
                P.op("dve", "tensor_copy", out=tmpA, in_=tmpI, reads=[tmpI_tb], writes=[tmpA_tb])
                P.op("dve", "tensor_tensor", out=tmpA, in0=u_ap, in1=tmpA, op=ALU.subtract, reads=rd + [tmpA_tb], writes=[tmpA_tb])
                P.op("dve", "tensor_scalar", out=tmpI.bitcast(F32), in0=tmpA, scalar1=0.0, scalar2=None, op0=ALU.is_lt, reads=[tmpA_tb], writes=[tmpI_tb])
                P.op("dve", "tensor_tensor", out=tmpA, in0=tmpA, in1=tmpI.bitcast(F32), op=ALU.add, reads=[tmpA_tb, tmpI_tb], writes=[tmpA_tb])
                P.op("act", "activation", out=out_ap, in_=tmpA, func=AF.Sin, scale=TWO_PI, bias=epsb.t[0:shape_p, 3:4], reads=[tmpA_tb, epsb], writes=wr)

            T = [tmr.next() for _ in range(5)]
            tA, tB, tC, tD, tE = T
            tmpA_tb, tmpI_tb = tD, tE
            def tv(tb, i):
                return tb.t[0:32, i * 128:(i + 1) * 128]
            lamre, lamim, dtb, lr, th, rr, cs, sn = (tv(tA, i) for i in range(8))
            are, aim, num, den, cre, cim, icr, ici = (tv(tB, i) for i in range(8))
            u_s, u_c, t0_, t1_, thp = (tv(tC, i) for i in range(5))
            P.dma("sp", lamre, s5_lam_re[j].rearrange("(cb gl) p -> cb (gl p)", gl=2), writes=[tA])
            P.dma("sp", lamim, s5_lam_im[j].rearrange("(cb gl) p -> cb (gl p)", gl=2), writes=[tA])
            P.dma("sp", tC.t[0:32, 896:898], s5_log_dt[j].rearrange("(cb gl) -> cb gl", gl=2), writes=[tC])
            P.op("act", "activation", out=tC.t[0:32, 896:898], in_=tC.t[0:32, 896:898], func=AF.Exp, reads=[tC], writes=[tC])
            for gl in range(2):
                P.op("dve", "tensor_scalar", out=tA.t[0:32, 256 + gl * 64:256 + (gl + 1) * 64], in0=ones[0:32, 0:64],
                     scalar1=tC.t[0:32, 896 + gl:897 + gl], scalar2=None, op0=ALU.mult, reads=[cst, tC], writes=[tA])
            def tt(o, a, b, op, R, W):
                P.op("dve", "tensor_tensor", out=o, in0=a, in1=b, op=op, reads=R, writes=W)
            tt(lr, lamre, dtb, ALU.mult, [tA], [tA])
            tt(th, lamim, dtb, ALU.mult, [tA], [tA])
            P.op("act", "activation", out=rr, in_=lr, func=AF.Exp, reads=[tA], writes=[tA])
            P.op("dve", "tensor_scalar", out=thp, in0=th, scalar1=1.0 / TWO_PI, scalar2=None, op0=ALU.mult, reads=[tA], writes=[tC])
            P.op("dve", "tensor_scalar", out=u_s, in0=thp, scalar1=8.5, scalar2=None, op0=ALU.add, reads=[tC], writes=[tC])
            P.op("dve", "tensor_scalar", out=u_c, in0=thp, scalar1=8.75, scalar2=None, op0=ALU.add, reads=[tC], writes=[tC])
            sincos(u_s, sn, tD.t[0:32, 0:128], tE.t[0:32, 0:128].bitcast(I32), [tC], [tA], 32)
            sincos(u_c, cs, tD.t[0:32, 0:128], tE.t[0:32, 0:128].bitcast(I32), [tC], [tA], 32)
            tt(are, rr, cs, ALU.mult, [tA], [tB])
            tt(aim, rr, sn, ALU.mult, [tA], [tB])
            P.op("dve", "tensor_scalar", out=num, in0=are, scalar1=-1.0, scalar2=None, op0=ALU.add, reads=[tB], writes=[tB])
            tt(den, lamre, lamre, ALU.mult, [tA], [tB])
            tt(t0_, lamim, lamim, ALU.mult, [tA], [tC])
            tt(den, den, t0_, ALU.add, [tB, tC], [tB])
            P.op("dve", "reciprocal", out=den, in_=den, reads=[tB], writes=[tB])
            tt(cre, num, lamre, ALU.mult, [tA, tB], [tB])
            tt(t0_, aim, lamim, ALU.mult, [tA, tB], [tC])
            tt(cre, cre, t0_, ALU.add, [tB, tC], [tB])
            tt(cre, cre, den, ALU.mult, [tB], [tB])
            tt(cim, aim, lamre, ALU.mult, [tA, tB], [tB])
            tt(t0_, num, lamim, ALU.mult, [tA, tB], [tC])
            tt(cim, cim, t0_, ALU.subtract, [tB, tC], [tB])
            tt(cim, cim, den, ALU.mult, [tB], [tB])
            tt(t0_, cre, cre, ALU.mult, [tB], [tC])
            tt(t1_, cim, cim, ALU.mult, [tB], [tC])
            tt(t0_, t0_, t1_, ALU.add, [tC], [tC])
            P.op("dve", "reciprocal", out=t0_, in_=t0_, reads=[tC], writes=[tC])
            tt(icr, cre, t0_, ALU.mult, [tB, tC], [tB])
            P.op("dve", "scalar_tensor_tensor", out=ici, in0=cim, scalar=-1.0, in1=t0_, op0=ALU.mult, op1=ALU.mult, reads=[tB, tC], writes=[tB])
            srcs = [(rr, tA), (cs, tA), (sn, tA), (are, tB), (aim, tB), (cre, tB), (cim, tB), (icr, tB), (ici, tB), (thp, tC)]
            for i, (ap_, tb_) in enumerate(srcs):
                bk = P.bank()
                P.op("pe", "matmul", bk.t[:, 0:32], lhsT=ap_, rhs=ident[0:32, 0:32], start=True, stop=True, reads=[tb_, cst], writes=[bk])
                P.op("act", "activation", out=pv.t[:, i, :], in_=bk.t[:, 0:32], func=AF.Copy, reads=[bk], writes=[pv])
            for ch in range(4):
                c0 = ch * 8
                uu = tA.t[:].rearrange("p (c k) -> p c k", c=8)
                P.op("dve", "tensor_tensor", out=uu, in0=kkrow[:, None, :].broadcast_to([128, 8, 128]),
                     in1=pv.t[:, TH_, c0:c0 + 8][:, :, None].broadcast_to([128, 8, 128]), op=ALU.mult, reads=[cst, pv], writes=[tA])
                P.op("dve", "tensor_scalar", out=tB.t[:], in0=tA.t[:], scalar1=8.5, scalar2=None, op0=ALU.add, reads=[tA], writes=[tB])
                P.op("dve", "tensor_scalar", out=tC.t[:], in0=tA.t[:], scalar1=8.75, scalar2=None, op0=ALU.add, reads=[tA], writes=[tC])
                sincos(tB.t[:], sk[:, c0:c0 + 8, :].rearrange("p c k -> p (c k)"), tD.t[:], tE.t[:].bitcast(I32), [tB], [tab], 128)
                sincos(tC.t[:], ck[:, c0:c0 + 8, :].rearrange("p c k -> p (c k)"), tD.t[:], tE.t[:].bitcast(I32), [tC], [tab], 128)
            pvv = pv.t
            x1 = tA.t[:, 0:32]; x2 = tA.t[:, 32:64]
            tt(x1, ck[:, :, 127], pvv[:, COS_, :], ALU.mult, [tab, pv], [tA])
            tt(x2, sk[:, :, 127], pvv[:, SIN_, :], ALU.mult, [tab, pv], [tA])
            tt(pvv[:, C128_, :], x1, x2, ALU.subtract, [tA], [pv])
            tt(x1, sk[:, :, 127], pvv[:, COS_, :], ALU.mult, [tab, pv], [tA])
            tt(x2, ck[:, :, 127], pvv[:, SIN_, :], ALU.mult, [tab, pv], [tA])
            tt(pvv[:, S128_, :], x1, x2, ALU.add, [tA], [pv])
            cf = lambda i: pvv[:, i, :][:, :, None].broadcast_to([128, 32, 32])
            c_re = CT[:, :, 0, :]
            c_im = CT[:, :, 1, :]
            y1 = tB.t[:].rearrange("p (c m) -> p c m", c=32)
            y2 = tC.t[:].rearrange("p (c m) -> p c m", c=32)
            y3 = tD.t[:].rearrange("p (c m) -> p c m", c=32)
            tt(y1, c_re, cf(CRE_), ALU.mult, [ctw, pv], [tB])
            tt(y2, c_im, cf(CIM_), ALU.mult, [ctw, pv], [tC])
            tt(y1, y1, y2, ALU.subtract, [tB, tC], [tB])
            tt(y2, c_re, cf(CIM_), ALU.mult, [ctw, pv], [tC])
            tt(y3, c_im, cf(CRE_), ALU.mult, [ctw, pv], [tD])
            tt(y2, y2, y3, ALU.add, [tC, tD], [tC])
            y1p = tB.t[:].rearrange("p (a q m) -> p a q m", a=16, q=2)
            y2p = tC.t[:].rearrange("p (a q m) -> p a q m", a=16, q=2)
            for qq in range(2):
                P.op("dve", "tensor_copy", out=CTB[:, :, qq, 0, qq * 32:(qq + 1) * 32], in_=y1p[:, :, qq, :], reads=[tB], writes=[ctb])
                P.op("dve", "tensor_scalar", out=CTB[:, :, qq, 1, qq * 32:(qq + 1) * 32], in0=y2p[:, :, qq, :], scalar1=-1.0, scalar2=None, op0=ALU.mult,
                     reads=[tC], writes=[ctb])
            P.op("pool", "memset", gir.t[:], 0.0, writes=[gir])
            P.op("pool", "memset", gii.t[:], 0.0, writes=[gii])
            P.barrier()

            def cmul(o_re, o_im, a_re, a_im, b_re, b_im, R, t1, t2, eng="dve"):
                pass

            order = ([n_pt, n_pt + 1] if has_sample else []) + list(range(n_pt))
            for t in order:
                samp = (t >= n_pt)
                nseq = 16 if samp else 1
                nreal = 8 if samp else 1
                L = 128 // nseq
                g0 = (t - n_pt) * 8
                xs = load_x(src, t)
                xT = make_xT(xs)

                def tabv(tb_ap, kb):
                    if samp:
                        return tb_ap[:, 4 * kb:4 * kb + 4, 0:8][:, :, None, :].broadcast_to([128, 4, 16, 8])
                    return tb_ap[:, 4 * kb:4 * kb + 4, :]

                def gv(tb):
                    if samp:
                        return tb.t[:].rearrange("p (c s l) -> p c s l", c=4, s=16)
                    return tb.t[:].rearrange("p (c k) -> p c k", c=4)

                if samp:
                    for (srcst, dstt) in ((st_re, hlr), (st_im, hli)):
                        for c4 in range(8):
                            stg = w512.next()
                            P.dma("sp", stg.t[0:8, :], srcst[j, g0:g0 + 8].rearrange("s g p -> s (g p)")[:, c4 * 512:(c4 + 1) * 512], writes=[stg])
                            bk = P.bank()
                            for q in range(4):
                                P.op("pe", "matmul", bk.t[:, q * 8:(q + 1) * 8], lhsT=stg.t[0:8, q * 128:(q + 1) * 128], rhs=ident[0:8, 0:8],
                                     start=True, stop=True, reads=[stg, cst], writes=[bk])
                            P.op("act", "activation", out=dstt.t[:, c4 * 4:(c4 + 1) * 4, :],
                                 in_=bk.t[:, 0:32].rearrange("p (q s) -> p q s", q=4), func=AF.Copy, reads=[bk], writes=[dstt])
                    bc = lambda i: pvv[:, i, :][:, :, None].broadcast_to([128, 32, 8])
                    w1 = w512.next(); w2 = w512.next(); w3 = w512.next(); w4 = w512.next()
                    v1 = w1.t[:, 0:256].rearrange("p (c s) -> p c s", c=32)
                    v2 = w2.t[:, 0:256].rearrange("p (c s) -> p c s", c=32)
                    v3 = w3.t[:, 0:256].rearrange("p (c s) -> p c s", c=32)
                    v4 = w4.t[:, 0:256].rearrange("p (c s) -> p c s", c=32)
                    tt(v1, hlr.t[:], bc(ICR_), ALU.mult, [hlr, pv], [w1])
                    tt(v2, hli.t[:], bc(ICI_), ALU.mult, [hli, pv], [w2])
                    tt(v3, v1, v2, ALU.subtract, [w1, w2], [w3])
                    tt(v1, hlr.t[:], bc(ICI_), ALU.mult, [hlr, pv], [w1])
                    tt(v2, hli.t[:], bc(ICR_), ALU.mult, [hli, pv], [w2])
                    tt(v4, v1, v2, ALU.add, [w1, w2], [w4])
                    tt(v1, v3, bc(ARE_), ALU.mult, [w3, pv], [w1])
                    tt(v2, v4, bc(AIM_), ALU.mult, [w4, pv], [w2])
                    tt(adr.t[:], v1, v2, ALU.subtract, [w1, w2], [adr])
                    tt(v1, v3, bc(AIM_), ALU.mult, [w3, pv], [w1])
                    tt(v2, v4, bc(ARE_), ALU.mult, [w4, pv], [w2])
                    tt(adi.t[:], v1, v2, ALU.add, [w1, w2], [adi])

                for part in range(2):
                    bkT = P.bank()
                    bkTb = bkT.t[:].bitcast(BF16)
                    for half in range(2):
                        bk = P.bank()
                        proj_tm(xT, win, part * 1024 + half * 512, 512, bk.t[:, :], bk)
                        st_ = sstg.next()
                        if half == 0:
                            P.op("act", "activation", out=st_.t[:], in_=bk.t[:, :], func=AF.Copy, reads=[bk], writes=[st_])
                        else:
                            P.op("dve", "tensor_copy", out=st_.t[:], in_=bk.t[:, :], reads=[bk], writes=[st_])
                        for q in range(4):
                            blk = half * 4 + q
                            P.op("pe", "transpose", bkTb[:, blk * 128:(blk + 1) * 128], st_.t[:, q * 128:(q + 1) * 128], identb.t[:],
                                 reads=[st_, identb], writes=[bkT])
                    src3 = bkTb.rearrange("p (a k) -> p a k", a=8)
                    if part == 0:
                        P.op("act", "activation", out=uT.t[:], in_=src3, func=AF.Copy, reads=[bkT], writes=[uT])
                        P.op("dve", "tensor_copy", out=uTb.t[:], in_=src3, reads=[bkT], writes=[uTb])
                    else:
                        P.op("act", "activation", out=szT.t[:], in_=src3, func=AF.Silu, reads=[bkT], writes=[szT])

                for kb in range(8):
                    bkr = P.bank()
                    bki = P.bank()
                    for ri, bkx in ((0, bkr), (1, bki)):
                        for q in range(4):
                            hh, qq = q // 2, q % 2
                            P.op("pe", "matmul", bkx.t[:, q * 128:(q + 1) * 128], lhsT=BT[hh * 64:(hh + 1) * 64, kb, ri, qq, :],
                                 rhs=uTb.t[hh * 64:(hh + 1) * 64, kb, :], start=True, stop=True, reads=[btw, uTb], writes=[bkx])
                    CK = tabv(ck, kb)
                    SK = tabv(sk, kb)
                    a1 = w512.next(); a2 = w512.next(); rre = w512.next(); rim = w512.next()
                    pr = gv(bkr) if False else (bkr.t[:, :].rearrange("p (c s l) -> p c s l", c=4, s=16) if samp else bkr.t[:, :].rearrange("p (c k) -> p c k", c=4))
                    pi_ = (bki.t[:, :].rearrange("p (c s l) -> p c s l", c=4, s=16) if samp else bki.t[:, :].rearrange("p (c k) -> p c k", c=4))
                    tt(gv(a1), pr, CK, ALU.mult, [bkr, tab], [a1])
                    tt(gv(a2), pi_, SK, ALU.mult, [bki, tab], [a2])
                    P.op("pool", "tensor_tensor", out=rre.t[:], in0=a1.t[:], in1=a2.t[:], op=ALU.add, reads=[a1, a2], writes=[rre])
                    a3 = w512.next(); a4 = w512.next()
                    tt(gv(a3), pi_, CK, ALU.mult, [bki, tab], [a3])
                    tt(gv(a4), pr, SK, ALU.mult, [bkr, tab], [a4])
                    P.op("pool", "tensor_tensor", out=rim.t[:], in0=a3.t[:], in1=a4.t[:], op=ALU.subtract, reads=[a3, a4], writes=[rim])
                    if samp:
                        rv = rre.t[:].rearrange("p (c s l) -> p c s l", c=4, s=16)
                        iv = rim.t[:].rearrange("p (c s l) -> p c s l", c=4, s=16)
                        tt(rv[:, :, 0:8, 0], rv[:, :, 0:8, 0], adr.t[:, 4 * kb:4 * kb + 4, :], ALU.add, [rre, adr], [rre])
                        tt(iv[:, :, 0:8, 0], iv[:, :, 0:8, 0], adi.t[:, 4 * kb:4 * kb + 4, :], ALU.add, [rim, adi], [rim])
                    gre = w512.next(); gim = w512.next()
                    for q in range(4):
                        cb_ = 4 * kb + q
                        if samp:
                            P.op("dve", "tensor_scalar", out=r0t.t[:], in0=notfirst, scalar1=pvv[:, R_, cb_:cb_ + 1], scalar2=None, op0=ALU.mult,
                                 reads=[cst, pv], writes=[r0t])
                            d0 = r0t.t[:]
                            rd0 = [r0t]
                            ini_r = 0.0
                            ini_i = 0.0
                            rdi = []
                        else:
                            d0 = pvv[:, R_, cb_:cb_ + 1].broadcast_to([128, 128])
                            rd0 = [pv]
                            ini_r = gir.t[:, cb_:cb_ + 1]
                            ini_i = gii.t[:, cb_:cb_ + 1]
                            rdi = [gir, gii]
                        P.op("dve", "tensor_tensor_scan", out=gre.t[:, q * 128:(q + 1) * 128], data0=d0, data1=rre.t[:, q * 128:(q + 1) * 128],
                             initial=ini_r, op0=ALU.mult, op1=ALU.add, reads=rd0 + [rre] + rdi, writes=[gre])
                        P.op("dve", "tensor_tensor_scan", out=gim.t[:, q * 128:(q + 1) * 128], data0=d0, data1=rim.t[:, q * 128:(q + 1) * 128],
                             initial=ini_i, op0=ALU.mult, op1=ALU.add, reads=rd0 + [rim] + rdi, writes=[gim])
                    if not samp:
                        gl_r = gre.t[:].rearrange("p (c k) -> p c k", c=4)[:, :, 127]
                        gl_i = gim.t[:].rearrange("p (c k) -> p c k", c=4)[:, :, 127]
                        c128 = pvv[:, C128_, 4 * kb:4 * kb + 4]
                        s128 = pvv[:, S128_, 4 * kb:4 * kb + 4]
                        x1 = a1.t[:, 0:4]; x2 = a2.t[:, 0:4]
                        tt(x1, gl_r, c128, ALU.mult, [gre, pv], [a1])
                        tt(x2, gl_i, s128, ALU.mult, [gim, pv], [a2])
                        tt(gir.t[:, 4 * kb:4 * kb + 4], x1, x2, ALU.subtract, [a1, a2], [gir])
                        tt(x1, gl_r, s128, ALU.mult, [gre, pv], [a1])
                        tt(x2, gl_i, c128, ALU.mult, [gim, pv], [a2])
                        tt(gii.t[:, 4 * kb:4 * kb + 4], x1, x2, ALU.add, [a1, a2], [gii])
                    hre = w512.next(); him = w512.next()
                    b1 = w512.next(); b2 = w512.next()
                    hreb = hre.t[:].bitcast(BF16)[:, 0:512]
                    himb = him.t[:].bitcast(BF16)[:, 0:512]
                    tt(gv(b1), gv(gre), CK, ALU.mult, [gre, tab], [b1])
                    P.op("pool", "tensor_tensor", out=gv(b2), in0=gv(gim), in1=SK, op=ALU.mult, reads=[gim, tab], writes=[b2])
                    tt(hreb, b1.t[:], b2.t[:], ALU.subtract, [b1, b2], [hre])
                    b3 = w512.next(); b4 = w512.next()
                    tt(gv(b3), gv(gre), SK, ALU.mult, [gre, tab], [b3])
                    P.op("pool", "tensor_tensor", out=gv(b4), in0=gv(gim), in1=CK, op=ALU.mult, reads=[gim, tab], writes=[b4])
                    tt(himb, b3.t[:], b4.t[:], ALU.add, [b3, b4], [him])
                    if samp or t == n_pt - 1:
                        ns = 8 if samp else 1
                        if samp:
                            glr = gre.t[:].rearrange("p (c s l) -> p c s l", c=4, s=16)[:, :, 0:8, 7]
                            gli = gim.t[:].rearrange("p (c s l) -> p c s l", c=4, s=16)[:, :, 0:8, 7]
                            cl = ck[:, 4 * kb:4 * kb + 4, 7:8].broadcast_to([128, 4, 8])
                            sl_ = sk[:, 4 * kb:4 * kb + 4, 7:8].broadcast_to([128, 4, 8])
                        else:
                            glr = gre.t[:].rearrange("p (c k) -> p c k", c=4)[:, :, 127:128]
                            gli = gim.t[:].rearrange("p (c k) -> p c k", c=4)[:, :, 127:128]
                            cl = ck[:, 4 * kb:4 * kb + 4, 127:128]
                            sl_ = sk[:, 4 * kb:4 * kb + 4, 127:128]
                        x1 = b1.t[:, 0:4 * ns].rearrange("p (c s) -> p c s", c=4)
                        x2 = b2.t[:, 0:4 * ns].rearrange("p (c s) -> p c s", c=4)
                        tt(x1, glr, cl, ALU.mult, [gre, tab], [b1])
                        tt(x2, gli, sl_, ALU.mult, [gim, tab], [b2])
                        tt(hlr.t[:, 4 * kb:4 * kb + 4, 0:ns], x1, x2, ALU.subtract, [b1, b2], [hlr])
                        tt(x1, glr, sl_, ALU.mult, [gre, tab], [b1])
                        tt(x2, gli, cl, ALU.mult, [gim, tab], [b2])
                        tt(hli.t[:, 4 * kb:4 * kb + 4, 0:ns], x1, x2, ALU.add, [b1, b2], [hli])
                    bky = P.bank()
                    for hh in range(2):
                        n_ = 0
                        for qq in range(2):
                            q = hh * 2 + qq
                            for ri, hb, htb in ((0, hreb, hre), (1, himb, him)):
                                P.op("pe", "matmul", bky.t[hh * 64:(hh + 1) * 64, 0:128], lhsT=CTB[:, 2 * kb + hh, qq, ri, :],
                                     rhs=hb[:, q * 128:(q + 1) * 128], start=(n_ == 0), stop=(n_ == 3), reads=[ctb, htb], writes=[bky])
                                n_ += 1
                    yt = w512.next()
                    P.op("dve", "scalar_tensor_tensor", out=yt.t[:, 0:128], in0=uT.t[:, kb, :], scalar=dcol.t[:, kb:kb + 1], in1=bky.t[:, 0:128],
                         op0=ALU.mult, op1=ALU.add, reads=[uT, dcol, bky], writes=[yt])
                    P.op("act", "activation", out=gyT.t[:, kb, :], in_=yt.t[:, 0:128], func=AF.Gelu, reads=[yt], writes=[gyT])
                ogT = ogT_r.next()
                for blk in range(8):
                    bk1 = P.bank()
                    for kc in range(8):
                        P.op("pe", "matmul", bk1.t[:, 0:128], lhsT=wglu.t[:, kc, blk * 128:(blk + 1) * 128], rhs=gyT.t[:, kc, :],
                             start=(kc == 0), stop=(kc == 7), reads=[wglu, gyT], writes=[bk1])
                    for kc in range(8):
                        P.op("pe", "matmul", bk1.t[:, 128:256], lhsT=wglu.t[:, kc, 1024 + blk * 128:1024 + (blk + 1) * 128], rhs=gyT.t[:, kc, :],
                             start=(kc == 0), stop=(kc == 7), reads=[wglu, gyT], writes=[bk1])
                    s2 = w512.next()
                    P.op("act", "activation", out=s2.t[:, 0:128], in_=bk1.t[:, 128:256], func=AF.Sigmoid, bias=bglu.t[:, 8 + blk:9 + blk],
                         reads=[bk1, bglu], writes=[s2])
                    P.op("dve", "scalar_tensor_tensor", out=s2.t[:, 128:256], in0=bk1.t[:, 0:128], scalar=bglu.t[:, blk:blk + 1], in1=s2.t[:, 0:128],
                         op0=ALU.add, op1=ALU.mult, reads=[bk1, bglu, s2], writes=[s2])
                    P.op("pool", "tensor_tensor", out=ogT.t[:, blk, :], in0=s2.t[:, 128:256], in1=szT.t[:, blk, :], op=ALU.mult,
                         reads=[s2, szT], writes=[ogT])
                out_proj_ln(ogT, xs, layer, dst, t)
                if samp or t == n_pt - 1:
                    ns = 8 if samp else 1
                    bc = lambda i: pvv[:, i, :][:, :, None].broadcast_to([128, 32, ns])
                    w1 = w512.next(); w2 = w512.next(); w3 = w512.next(); w4 = w512.next()
                    v1 = w1.t[:, 0:32 * ns].rearrange("p (c s) -> p c s", c=32)
                    v2 = w2.t[:, 0:32 * ns].rearrange("p (c s) -> p c s", c=32)
                    v3 = w3.t[:, 0:32 * ns].rearrange("p (c s) -> p c s", c=32)
                    v4 = w4.t[:, 0:32 * ns].rearrange("p (c s) -> p c s", c=32)
                    hr_ = hlr.t[:, :, 0:ns]
                    hi_ = hli.t[:, :, 0:ns]
                    tt(v1, hr_, bc(CRE_), ALU.mult, [hlr, pv], [w1])
                    tt(v2, hi_, bc(CIM_), ALU.mult, [hli, pv], [w2])
                    v3 = adr.t[:, :, 0:ns]
                    v4 = adi.t[:, :, 0:ns]
                    tt(v3, v1, v2, ALU.subtract, [w1, w2], [adr])
                    tt(v1, hr_, bc(CIM_), ALU.mult, [hlr, pv], [w1])
                    tt(v2, hi_, bc(CRE_), ALU.mult, [hli, pv], [w2])
                    tt(v4, v1, v2, ALU.add, [w1, w2], [adi])
                    for (vv, wv, dsts, dstp) in ((v3, adr, s_re, p_re), (v4, adi, s_im, p_im)):
                        for c4 in range(8):
                            bk = P.bank()
                            for q in range(4):
                                P.op("pe", "matmul", bk.t[0:ns, q * 128:(q + 1) * 128], lhsT=vv[:, c4 * 4 + q, :], rhs=ident, start=True, stop=True, reads=[wv, cst], writes=[bk])
                            stg = w512.next()
                            P.op("act", "activation", out=stg.t[0:ns, :], in_=bk.t[0:ns, :], func=AF.Copy, reads=[bk], writes=[stg])
                            if samp:
                                P.dma("sp", dsts[j, g0:g0 + 8].rearrange("s g p -> s (g p)")[:, c4 * 512:(c4 + 1) * 512], stg.t[0:8, :], reads=[stg])
                            else:
                                P.dma("sp", dstp[j].rearrange("g p -> (g p)").rearrange("(o f) -> o f", o=1)[:, c4 * 512:(c4 + 1) * 512], stg.t[0:1, :], reads=[stg])

        cur = x_in
        li = 0
        for layer in layers:
            last = (layer == layers[-1])
            dst = y_out if last else xscr[li % 2]
            kind = layer % 3
            j = layer // 3
            if kind == 0:
                gla_layer(j, layer, cur, dst)
            elif kind == 1:
                gdn_layer(j, layer, cur, dst)
            else:
                s5_layer(j, layer, cur, dst)
            cur = dst
            li += 1
        P.finish()
        P.emit()
    return nc


def make_consts():
    c = np.zeros((128, CW), np.float32)
    p = np.arange(128)[:, None]
    f = np.arange(128)[None, :]
    c[:, 0:128] = (p == f)
    c[:, 128:256] = 1.0
    c[:, 256:384] = (p <= f)
    c[:, 384:512] = (f < p)
    same = (p // 8) == (f // 8)
    c[:, 512:640] = (p <= f) & same
    c[:, 640:768] = (f < p) & same
    c[:, 768:784] = (np.arange(128)[:, None] // 8) == np.arange(16)[None, :]
    c[:, 784:912] = np.arange(128)[None, :]
    c[:, 912:1040] = (np.arange(128)[None, :] % 8) != 0
    return c


def core_inputs(inp, core, n_pt=16, has_sample=True):
    sl = slice(16 * core, 16 * core + 16)
    xp = np.asarray(inp["x_prompt"][core, :n_pt * 128], np.float32)
    parts = [xp]
    if has_sample:
        xsm = np.asarray(inp["x_sample"][sl], np.float32).reshape(2, 64, D)
        zpad = np.zeros((64, D), np.float32)
        parts += [xsm[0], zpad, xsm[1], zpad]
    m = {
        "x_in": np.ascontiguousarray(np.concatenate(parts, 0)),
        "consts": make_consts(),
        "ln_g": inp["ln_g"], "ln_b": inp["ln_b"],
        "gla_w_in": inp["gla_w_in"], "gla_w_a2": inp["gla_w_a2"], "gla_b_a": inp["gla_b_a"],
        "gla_norm_g": inp["gla_norm_g"], "gla_w_out": inp["gla_w_out"],
        "st_gla": np.ascontiguousarray(inp["state_gla"][:, sl]),
        "gdn_w_in": inp["gdn_w_in"], "gdn_w_conv": inp["gdn_w_conv"], "gdn_a_log": inp["gdn_a_log"],
        "gdn_dt_bias": inp["gdn_dt_bias"], "gdn_norm_g": inp["gdn_norm_g"], "gdn_w_out": inp["gdn_w_out"],
        "s5_w_in": inp["s5_w_in"], "s5_lam_re": inp["s5_lam_re"], "s5_lam_im": inp["s5_lam_im"], "s5_log_dt": inp["s5_log_dt"],
        "s5_bt": s5_bt_layout(inp["s5_b_re"][0], inp["s5_b_im"][0]), "s5_ct": s5_ct_layout(inp["s5_c_re"][0], inp["s5_c_im"][0]),
        "s5_d": inp["s5_d"], "s5_w_glu": inp["s5_w_glu"], "s5_b_glu": inp["s5_b_glu"], "s5_w_out": inp["s5_w_out"],
        "st_re": np.ascontiguousarray(inp["state_s5_re"][:, sl]), "st_im": np.ascontiguousarray(inp["state_s5_im"][:, sl]),
        "st_gdn": np.ascontiguousarray(inp["state_gdn"][:, sl]), "st_conv": np.ascontiguousarray(inp["state_gdn_conv"][:, sl]),
    }
    return {k: np.ascontiguousarray(np.asarray(v, np.float32)) for k, v in m.items()}


def s5_bt_layout(b_re, b_im):
    out = np.zeros((128, 8, 2, 2, 128), np.float32)
    for ri, b in enumerate((np.asarray(b_re, np.float32), np.asarray(b_im, np.float32))):
        bb = b.reshape(8, 4, 2, 64, 16)
        for q in range(4):
            for gl in range(2):
                r0 = q * 32 + gl * 16
                out[r0:r0 + 16, :, ri, q % 2, gl * 64:(gl + 1) * 64] = bb[:, q, gl].transpose(2, 0, 1)
    return out.reshape(128, 8 * 2 * 2 * 128)


def s5_ct_layout(c_re, c_im):
    out = np.zeros((128, 32, 2, 32), np.float32)
    for ri, c in enumerate((np.asarray(c_re, np.float32), np.asarray(c_im, np.float32))):
        cc = c.reshape(32, 2, 16, 64)
        for gl in range(2):
            out[gl * 64:(gl + 1) * 64, :, ri, gl * 16:(gl + 1) * 16] = cc[:, gl].transpose(2, 0, 1)
    return out.reshape(128, 32 * 2 * 32)


_NC_CACHE = {}


def kernel(**inputs):
    inp = {k: np.asarray(v) for k, v in inputs.items()}
    if "full" not in _NC_CACHE:
        _NC_CACHE["full"] = build(n_pt=16, layers=(0, 1, 2, 3), has_sample=True)
    nc = _NC_CACHE["full"]
    in_maps = [core_inputs(inp, c, 16) for c in range(NCORES)]
    res = run_bass_kernel_spmd(nc, in_maps, core_ids=list(range(NCORES))).results
    f32 = lambda a: np.ascontiguousarray(np.asarray(a, dtype=np.float32))
    LP = 2048
    y_prompt = np.stack([f32(r["y_out"])[:LP] for r in res], 0)
    y_sample = np.concatenate([np.concatenate([f32(r["y_out"])[LP:LP + 64], f32(r["y_out"])[LP + 128:LP + 192]], 0).reshape(16, 8, D)
                               for r in res], 0)
    catb = lambda k: np.stack([f32(r[k]) for r in res], 1)
    cats = lambda k: np.concatenate([f32(r[k]) for r in res], 1)
    return (y_prompt, y_sample,
            catb("p_gla"), catb("p_gdn"), catb("p_conv"), catb("p_re"), catb("p_im"),
            cats("s_gla"), cats("s_gdn"), cats("s_conv"), cats("s_re"), cats("s_im"))
```
